# Optimizing a Trainium2 kernel written in Bass

```python
import math
import jax, jax.numpy as jnp
from jax import lax
import numpy as np

D_MODEL = 1024
BATCH = 2
SEQ = 8192
DEPTH = 2

HEAD_DIM = 64
RET_HEADS = 4
FOX_HEADS = 4
RWKV_HEADS = 4
RET_W = RET_HEADS * HEAD_DIM
FOX_W = FOX_HEADS * HEAD_DIM
RWKV_W = RWKV_HEADS * HEAD_DIM
RET_CHUNK = 128
FOX_BLOCK = 128
ROPE_BASE = 10000.0
DECAY_LORA = 64
AAA_LORA = 64
GATE_LORA = 128
N_BRANCH = 3
D_FF = -(-(8 * D_MODEL) // (3 * 256)) * 256
NORM_EPS = 1e-6
RET_GN_EPS = 1e-5
RWKV_GN_EPS = 64e-5

RET_COLS = 4 * RET_W
FOX_COLS = 3 * FOX_W + FOX_HEADS
RWKV_COLS = 3 * RWKV_W + DECAY_LORA + AAA_LORA + GATE_LORA
GATE_COLS = N_BRANCH * D_MODEL
IN_COLS = RET_COLS + FOX_COLS + RWKV_COLS + GATE_COLS

kernel_name = "hybrid_retention_fox_rwkv7_gated_block"


def _rmsnorm(x, g, eps=NORM_EPS):
    xf = x.astype(jnp.float32)
    y = xf * lax.rsqrt(jnp.mean(xf * xf, axis=-1, keepdims=True) + eps)
    return (y * g.astype(jnp.float32)).astype(x.dtype)


def _head_layernorm(y, eps):
    mu = jnp.mean(y, axis=-1, keepdims=True)
    var = jnp.mean(jnp.square(y - mu), axis=-1, keepdims=True)
    return (y - mu) * lax.rsqrt(var + eps)


def _heads(t, n_heads):
    B, S, _ = t.shape
    return t.reshape(B, S, n_heads, HEAD_DIM).transpose(0, 2, 1, 3)


def _rotary(t, pos):
    half = HEAD_DIM // 2
    inv_freq = ROPE_BASE ** (-jnp.arange(half, dtype=jnp.float32) / half)
    ang = pos[:, None] * inv_freq[None, :]
    cos, sin = jnp.cos(ang), jnp.sin(ang)
    t1, t2 = t[..., :half], t[..., half:]
    return jnp.concatenate([t1 * cos - t2 * sin, t1 * sin + t2 * cos], axis=-1)


def _retention(q, k, v):
    B, H, S, Dh = q.shape
    C = RET_CHUNK
    N = S // C
    log_gamma = jnp.log1p(-jnp.exp2(-5.0 - jnp.arange(H, dtype=jnp.float32)))
    q = q.reshape(B, H, N, C, Dh)
    k = (k * Dh ** -0.5).reshape(B, H, N, C, Dh)
    v = v.reshape(B, H, N, C, Dh)
    idx = jnp.arange(C, dtype=jnp.float32)
    dist = idx[:, None] - idx[None, :]
    lg = log_gamma[:, None, None]
    dmat = jnp.where(dist >= 0, jnp.exp(lg * jnp.maximum(dist, 0.0)), 0.0)
    scores = jnp.einsum('bhnid,bhnjd->bhnij', q, k) * dmat[None, :, None]
    inner = jnp.einsum('bhnij,bhnjd->bhnid', scores, v)
    k_dec = jnp.exp(log_gamma[:, None] * (C - 1 - idx)[None, :])
    upd = jnp.einsum('bhnjd,bhnje->bhnde', k * k_dec[None, :, None, :, None], v)
    chunk_dec = jnp.exp(log_gamma * C)[None, :, None, None]

    def step(R, u):
        return R * chunk_dec + u, R

    _, r_prev = lax.scan(step, jnp.zeros((B, H, Dh, Dh), jnp.float32), jnp.moveaxis(upd, 2, 0))
    r_prev = jnp.moveaxis(r_prev, 0, 2)
    q_dec = jnp.exp(log_gamma[:, None] * (idx + 1.0)[None, :])
    cross = jnp.einsum('bhnid,bhnde->bhnie', q * q_dec[None, :, None, :, None], r_prev)
    return (inner + cross).reshape(B, H, S, Dh)


def _retention_branch(cols, gn_gain):
    B, S, _ = cols.shape
    q, k, v, g = jnp.split(cols, 4, axis=-1)
    pos = jnp.arange(S, dtype=jnp.float32)
    q = _rotary(_heads(q, RET_HEADS), pos)
    k = _rotary(_heads(k, RET_HEADS), pos)
    y = _retention(q, k, _heads(v, RET_HEADS))
    y = _head_layernorm(y, RET_GN_EPS).transpose(0, 2, 1, 3).reshape(B, S, RET_W)
    return jax.nn.silu(g) * (y * gn_gain)


def _forgetting_attention(q, k, v, log_f):
    B, H, S, Dh = q.shape
    c = jnp.cumsum(log_f, axis=-1)
    scale = Dh ** -0.5
    kpos = jnp.arange(S)
    n_blocks = S // FOX_BLOCK

    def block(i):
        t0 = i * FOX_BLOCK
        qb = lax.dynamic_slice_in_dim(q, t0, FOX_BLOCK, axis=2)
        cb = lax.dynamic_slice_in_dim(c, t0, FOX_BLOCK, axis=2)
        logits = jnp.einsum('bhqd,bhkd->bhqk', qb, k) * scale + cb[..., None] - c[:, :, None, :]
        qpos = t0 + jnp.arange(FOX_BLOCK)
        logits = jnp.where(kpos[None, :] <= qpos[:, None], logits, -jnp.inf)
        p = jax.nn.softmax(logits, axis=-1)
        return jnp.einsum('bhqk,bhkd->bhqd', p, v)

    out = lax.map(block, jnp.arange(n_blocks))
    return jnp.moveaxis(out, 0, 2).reshape(B, H, S, Dh)


def _fox_branch(cols, f_bias, q_gain, k_gain):
    B, S, _ = cols.shape
    q, k, v, f = jnp.split(cols, [FOX_W, 2 * FOX_W, 3 * FOX_W], axis=-1)
    q = _rmsnorm(_heads(q, FOX_HEADS), q_gain)
    k = _rmsnorm(_heads(k, FOX_HEADS), k_gain)
    log_f = jax.nn.log_sigmoid(f + f_bias).transpose(0, 2, 1)
    y = _forgetting_attention(q, k, _heads(v, FOX_HEADS), log_f)
    return y.transpose(0, 2, 1, 3).reshape(B, S, FOX_W)


def _rwkv7_scan(r, w, k, v, a, b):
    B, S, H, Dh = r.shape

    def step(state, inp):
        r_t, w_t, k_t, v_t, a_t, b_t = inp
        sa = jnp.einsum('bhvk,bhk->bhv', state, a_t)
        state = (state * w_t[:, :, None, :] + sa[..., None] * b_t[:, :, None, :]
                 + v_t[..., None] * k_t[:, :, None, :])
        return state, jnp.einsum('bhvk,bhk->bhv', state, r_t)

    xs = tuple(jnp.moveaxis(t, 1, 0) for t in (r, w, k, v, a, b))
    _, y = lax.scan(step, jnp.zeros((B, H, Dh, Dh), jnp.float32), xs)
    return jnp.moveaxis(y, 0, 1)


def _rwkv7_branch(cols, mu, w0, w2, a0, a2, g2, k_k, k_a, r_k, gn_gain):
    B, S, _ = cols.shape
    prev = jnp.pad(cols[:, :-1], ((0, 0), (1, 0), (0, 0)))
    cols = cols + (prev - cols) * mu
    r, k, v, xw, xa, xg = jnp.split(
        cols, [RWKV_W, 2 * RWKV_W, 3 * RWKV_W, 3 * RWKV_W + DECAY_LORA,
               3 * RWKV_W + DECAY_LORA + AAA_LORA], axis=-1)
    w_pre = -jax.nn.softplus(-(w0 + jnp.tanh(xw) @ w2)) - 0.5
    decay = jnp.exp(-jnp.exp(w_pre))
    a = jax.nn.sigmoid(a0 + xa @ a2)
    g = jax.nn.sigmoid(xg) @ g2
    hd = lambda t: t.reshape(B, S, RWKV_HEADS, HEAD_DIM)
    kk = hd(k * k_k)
    kk = kk / jnp.maximum(jnp.sqrt(jnp.sum(kk * kk, axis=-1, keepdims=True)), 1e-12)
    k = k * (1.0 + (a - 1.0) * k_a)
    r_h, k_h, v_h, a_h = hd(r), hd(k), hd(v), hd(a)
    y = _rwkv7_scan(r_h, hd(decay), k_h, v_h, -kk, kk * a_h)
    y = _head_layernorm(y, RWKV_GN_EPS).reshape(B, S, RWKV_W) * gn_gain
    r_k_h = r_k.reshape(RWKV_HEADS, HEAD_DIM)
    bonus = jnp.sum(r_h * k_h * r_k_h, axis=-1, keepdims=True) * v_h
    return (y + bonus.reshape(B, S, RWKV_W)) * g


def setup_inputs(seed: int = 0) -> dict:
    key = jax.random.key(seed)
    ks = jax.random.split(key, 32)
    f32 = jnp.float32
    nrm = lambda k, shape, s: (jax.random.normal(k, shape, f32) * s)
    L, D = DEPTH, D_MODEL
    return {
        "x": nrm(ks[0], (BATCH, SEQ, D), 1.0),
        "mix_norm": 1.0 + nrm(ks[1], (L, D), 0.02),
        "w_in": nrm(ks[2], (L, D, IN_COLS), D ** -0.5),
        "ret_gn": 1.0 + nrm(ks[3], (L, RET_W), 0.02),
        "fox_q_norm": 1.0 + nrm(ks[4], (L, HEAD_DIM), 0.02),
        "fox_k_norm": 1.0 + nrm(ks[5], (L, HEAD_DIM), 0.02),
        "fox_f_bias": 2.0 + nrm(ks[6], (L, FOX_HEADS), 0.5),
        "rwkv_mu": jax.random.uniform(ks[7], (L, RWKV_COLS), f32),
        "rwkv_w0": nrm(ks[8], (L, RWKV_W), 0.5),
        "rwkv_w2": nrm(ks[9], (L, DECAY_LORA, RWKV_W), 0.1),
        "rwkv_a0": nrm(ks[10], (L, RWKV_W), 0.1),
        "rwkv_a2": nrm(ks[11], (L, AAA_LORA, RWKV_W), 0.1),
        "rwkv_g2": nrm(ks[12], (L, GATE_LORA, RWKV_W), GATE_LORA ** -0.5),
        "rwkv_k_k": 0.85 + nrm(ks[13], (L, RWKV_W), 0.02),
        "rwkv_k_a": 1.0 + nrm(ks[14], (L, RWKV_W), 0.02),
        "rwkv_r_k": nrm(ks[15], (L, RWKV_W), 0.1),
        "rwkv_gn": 1.0 + nrm(ks[16], (L, RWKV_W), 0.02),
        "p_ret": nrm(ks[17], (L, RET_W, D), RET_W ** -0.5),
        "p_fox": nrm(ks[18], (L, FOX_W, D), FOX_W ** -0.5),
        "p_rwkv": nrm(ks[19], (L, RWKV_W, D), RWKV_W ** -0.5),
        "w_out": nrm(ks[20], (L, D, D), D ** -0.5),
        "ffn_norm": 1.0 + nrm(ks[21], (L, D), 0.02),
        "w_gate_up": nrm(ks[22], (L, D, 2 * D_FF), D ** -0.5),
        "w_down": nrm(ks[23], (L, D_FF, D), D_FF ** -0.5),
    }


def reference(x, mix_norm, w_in, ret_gn, fox_q_norm, fox_k_norm, fox_f_bias,
              rwkv_mu, rwkv_w0, rwkv_w2, rwkv_a0, rwkv_a2, rwkv_g2, rwkv_k_k, rwkv_k_a,
              rwkv_r_k, rwkv_gn, p_ret, p_fox, p_rwkv, w_out, ffn_norm, w_gate_up, w_down):
    B, S, D = x.shape
    splits = [RET_COLS, RET_COLS + FOX_COLS, RET_COLS + FOX_COLS + RWKV_COLS]
    for l in range(DEPTH):
        h = _rmsnorm(x, mix_norm[l])
        proj = (h @ w_in[l]).astype(jnp.float32)
        ret_c, fox_c, rwkv_c, gate_c = jnp.split(proj, splits, axis=-1)
        y_ret = _retention_branch(ret_c, ret_gn[l])
        y_fox = _fox_branch(fox_c, fox_f_bias[l], fox_q_norm[l], fox_k_norm[l])
        y_rwkv = _rwkv7_branch(rwkv_c, rwkv_mu[l], rwkv_w0[l], rwkv_w2[l], rwkv_a0[l],
                               rwkv_a2[l], rwkv_g2[l], rwkv_k_k[l], rwkv_k_a[l],
                               rwkv_r_k[l], rwkv_gn[l])
        gates = jax.nn.sigmoid(gate_c).astype(x.dtype).reshape(B, S, N_BRANCH, D)
        merged = (gates[:, :, 0] * (y_ret.astype(x.dtype) @ p_ret[l])
                  + gates[:, :, 1] * (y_fox.astype(x.dtype) @ p_fox[l])
                  + gates[:, :, 2] * (y_rwkv.astype(x.dtype) @ p_rwkv[l]))
        x = x + merged @ w_out[l]
        h = _rmsnorm(x, ffn_norm[l])
        gu = h @ w_gate_up[l]
        g, u = jnp.split(gu, 2, axis=-1)
        x = x + (jax.nn.silu(g) * u) @ w_down[l]
    return x
```

```python
import numpy as np
import concourse.bass as bass
import concourse.mybir as mybir
from concourse.bass_utils import run_bass_kernel_spmd

F32 = mybir.dt.float32
BF16 = mybir.dt.bfloat16
AF = mybir.ActivationFunctionType
ALU = mybir.AluOpType
AX = mybir.AxisListType


class Prog:
    ENGS = ("pe", "act", "dve", "pool", "sp")
    NDMA = 8

    def __init__(self, nc):
        self.nc = nc
        self.ops = {e: [] for e in self.ENGS}
        self.state = {}
        self.dma_slot_uses = {}
        self.dma_slot_last = {}
        self.dma_n = {e: 0 for e in self.ENGS}
        self.pending = {e: [] for e in self.ENGS}
        self.nseq = 0

    def barrier(self):
        deps = []
        for e in self.ENGS:
            comp = [o for o in self.ops[e] if not o["dma"]]
            if comp:
                deps.append(comp[-1])
        deps.extend(self.dma_slot_last.values())
        for e in self.ENGS:
            self.pending[e] = list(deps)
        self.state = {}

    PSUM_PREFIX = ("rps", "misc", "facc", "fsc", "pbank", "wps")

    @classmethod
    def bank_of(cls, key):
        if key.startswith("rps"):
            return "B_rps"
        if key.startswith("wps"):
            return "B_wps"
        if key.startswith("misc"):
            return "B_misc"
        if key.startswith("facc"):
            return "B_facc"
        if key.startswith("fsc") or key.startswith("pbank") or key.startswith("ssb"):
            return "B_" + key
        return None

    def _add(self, eng, fn, reads, writes, dma):
        banks = []
        for k in tuple(reads) + tuple(writes):
            bk = self.bank_of(k)
            if bk is not None and bk not in banks:
                banks.append(bk)
        writes = tuple(writes) + tuple(banks)
        deps = []
        for b in reads:
            st = self.state.setdefault(b, [None, []])
            if st[0] is not None:
                deps.append(st[0])
        for b in writes:
            st = self.state.setdefault(b, [None, []])
            if st[0] is not None:
                deps.append(st[0])
            deps.extend(st[1])
        if eng == "pe":
            deps = [d for d in deps if d["eng"] != "pe"]
        if self.pending[eng]:
            deps = deps + self.pending[eng]
            self.pending[eng] = []
        op = {"eng": eng, "fn": fn, "deps": deps, "dma": dma, "idx": len(self.ops[eng]), "seq": self.nseq}
        self.nseq += 1
        if dma:
            slot = self.dma_n[eng] % self.NDMA
            self.dma_n[eng] += 1
            key = (eng, slot)
            prev = self.dma_slot_last.get(key)
            if prev is not None:
                deps.append(prev)
            self.dma_slot_uses[key] = self.dma_slot_uses.get(key, 0) + 1
            op["sig"] = (("dma", eng, slot), 16 * self.dma_slot_uses[key])
            self.dma_slot_last[key] = op
        else:
            op["sig"] = None
        self.ops[eng].append(op)
        for b in reads:
            self.state[b][1].append(op)
        for b in writes:
            self.state[b] = [op, []]
        return op

    def op(self, eng, fn, reads=(), writes=()):
        return self._add(eng, fn, tuple(reads), tuple(writes), False)

    def dma(self, eng, out, in_, reads=(), writes=()):
        return self._add(eng, lambda e: e.dma_start(out=out, in_=in_), tuple(reads), tuple(writes), True)

    def emit(self):
        nc = self.nc
        for e in self.ENGS:
            n = 0
            for op in self.ops[e]:
                if not op["dma"]:
                    n += 1
                    op["sig"] = (("eng", e), n)
        semkeys = [("eng", e) for e in self.ENGS]
        for e in self.ENGS:
            for s in range(self.NDMA):
                if (e, s) in self.dma_slot_uses:
                    semkeys.append(("dma", e, s))
        from contextlib import ExitStack
        with ExitStack() as es:
            sems = {}
            for k in semkeys:
                sems[k] = es.enter_context(nc.semaphore("s_" + "_".join(str(x) for x in k)))
            block = es.enter_context(nc.Block())
            def clock(op):
                c = op.get("clock")
                if c is not None:
                    return c
                c = {}
                for d in op["deps"]:
                    for k, v in clock(d).items():
                        if c.get(k, 0) < v:
                            c[k] = v
                k, v = op["sig"]
                c[k] = max(c.get(k, 0), v)
                op["clock"] = c
                return c
            import sys
            sys.setrecursionlimit(1000000)
            allops = []
            for e in self.ENGS:
                allops.extend(self.ops[e])
            allops.sort(key=lambda o: o["seq"])
            for op in allops:
                clock(op)

            def run_engine(e, engobj):
                know = {}
                for op in self.ops[e]:
                    need = {}
                    for d in op["deps"]:
                        k, v = d["sig"]
                        if know.get(k, 0) >= v:
                            continue
                        if need.get(k, 0) < v:
                            need[k] = v
                    for k, v in need.items():
                        engobj.wait_ge(sems[k], v)
                    for d in op["deps"]:
                        for k, v in d["clock"].items():
                            if know.get(k, 0) < v:
                                know[k] = v
                    ins = op["fn"](engobj)
                    k, v = op["sig"]
                    ins.then_inc(sems[k], 16 if op["dma"] else 1)
                for s in range(self.NDMA):
                    key = (e, s)
                    if key in self.dma_slot_uses:
                        engobj.wait_ge(sems[("dma", e, s)], 16 * self.dma_slot_uses[key])

            if self.ops["sp"]:
                @block.sync
                def _(eng):
                    run_engine("sp", eng)
            if self.ops["pe"]:
                @block.tensor
                def _(eng):
                    run_engine("pe", eng)
            if self.ops["act"]:
                @block.scalar
                def _(eng):
                    run_engine("act", eng)
            if self.ops["dve"]:
                @block.vector
                def _(eng):
                    run_engine("dve", eng)
            if self.ops["pool"]:
                @block.gpsimd
                def _(eng):
                    run_engine("pool", eng)

from contextlib import ExitStack


D = 1024
HD = 64
TT = 512
RW_RATE = 4
FG = [("rq", 64), ("rk", 64), ("rqs", 64), ("rks", 64), ("fq", 64), ("fk", 64), ("ff", 1),
      ("wr", 64), ("wk", 64), ("wv", 64), ("wxw", 64), ("wxa", 64), ("wxg", 128)]
FOFF = {}
_o = 0
for _n, _w in FG:
    FOFF[_n] = (_o, _w)
    _o += _w
NF = _o
NTM = 192
PPN = ["fqg", "fkg", "fb", "mu_r", "mu_k", "mu_v", "mu_xw", "mu_xa", "mu_xg", "w0", "a0", "k_k", "k_a",
       "r_k", "cdec", "kdec"]
PP = {n: i for i, n in enumerate(PPN)}
NPP = len(PPN)
NBC = 256


def build_B(S, dbg=False):
    NT = S // TT
    nc = bass.Bass("TRN2", target_bir_lowering=False)
    din = lambda n, shp, dt=F32: nc.dram_tensor(n, shp, dt, kind="ExternalInput").ap()
    dout = lambda n, shp, dt=F32: nc.dram_tensor(n, shp, dt, kind="ExternalOutput").ap()
    xT = din("xT", [D, S])
    wf = din("wf", [D, NF])
    wt = din("wt", [D, NTM])
    gmix = din("gmix", [128, 8])
    pp_d = din("pp", [128, NPP])
    bc_d = din("bc", [128, NBC])
    rope_d = din("rope", [64, 2, S])
    dmat_d = din("dmat", [128, 128])
    w2_d = din("w2h", [64, 64])
    a2_d = din("a2h", [64, 64])
    g2_d = din("g2h", [128, 64])
    msk_d = din("masks", [128, 3, 128])
    iden_d = din("iden", [128, 128])
    y_ret = dout("y_ret", [S, 64])
    y_fox = dout("y_fox", [S, 64])
    y_rw = dout("y_rw", [S, 64])

    A = dict(xT=xT, wf=wf, wt=wt, gmix=gmix, pp=pp_d, bc=bc_d, rope=rope_d, dmat=dmat_d, w2h=w2_d, a2h=a2_d, g2h=g2_d,
             masks=msk_d, iden=iden_d, y_ret=y_ret, y_fox=y_fox, y_rw=y_rw)
    P = Prog(nc)
    with ExitStack() as es:
        emit_B(P, nc, es, S, A, "")
    P.emit()
    return nc


def emit_B(P, nc, es, S, A, pfx):
    NT = S // TT
    xT, wf, wt, gmix, pp_d, bc_d, rope_d, dmat_d = (A[k] for k in ("xT", "wf", "wt", "gmix", "pp", "bc", "rope", "dmat"))
    w2_d, a2_d, g2_d, msk_d, iden_d, y_ret, y_fox, y_rw = (A[k] for k in ("w2h", "a2h", "g2h", "masks", "iden", "y_ret", "y_fox", "y_rw"))
    if True:
        def sb(name, shape, dt=F32):
            return es.enter_context(nc.sbuf_tensor("sb_" + pfx + name, shape, dt))

        def ps(name, shape, dt=F32):
            return es.enter_context(nc.psum_tensor("ps_" + pfx + name, shape, dt))

        pp = sb("pp", [128, NPP]); bc = sb("bc", [128, NBC]); gm = sb("gm", [128, 8])
        dmat = sb("dmat", [128, 128]); w2h = sb("w2h", [64, 64]); a2h = sb("a2h", [64, 64])
        g2h = sb("g2h", [128, 64]); msk = sb("msk", [128, 3, 128]); iden = sb("iden", [128, 128])
        for t, d, k in ((pp, pp_d, "pp"), (bc, bc_d, "bc"), (gm, gmix, "gm"), (dmat, dmat_d, "dmat"),
                        (w2h, w2_d, "w2h"), (a2h, a2_d, "a2h"), (g2h, g2_d, "g2h"), (msk, msk_d, "msk"),
                        (iden, iden_d, "iden")):
            P.dma("sp", t[:], d, writes=[k])
        ppc = lambda n, np_=64: pp[0:np_, PP[n]:PP[n] + 1]
        iden_bf = sb("iden_bf", [128, 128], BF16)
        P.op("dve", lambda e: e.tensor_copy(out=iden_bf[:], in_=iden[:]), reads=["iden"], writes=["iden_bf"])
        mski_bf = sb("mski_bf", [128, 128], BF16)
        P.op("dve", lambda e: e.tensor_copy(out=mski_bf[:], in_=msk[:, 2, :]), reads=["msk"], writes=["mski_bf"])
        ones_bf = sb("ones_bf", [128, 128], BF16)
        P.op("pool", lambda e: e.memset(ones_bf[:], 1.0), writes=["ones_bf"])
        mask_one = sb("mask_one", [1, TT])
        P.op("pool", lambda e: e.memset(mask_one[:], 1.0), writes=["mask_one"])
        ones_f = sb("ones_f", [128, 64])
        P.op("pool", lambda e: e.memset(ones_f[:], 1.0), writes=["ones_f"])
        mask64 = sb("mask64", [64, TT])
        P.op("pool", lambda e: e.memset(mask64[:], 1.0), writes=["mask64"])
        P.op("pool", lambda e: e.memset(mask64[:].rearrange("p (c j) -> p c j", j=128)[:, :, 0:1], 0.0),
             writes=["mask64"])
        der = sb("der", [128, 4])
        P.op("dve", lambda e: e.tensor_scalar(out=der[0:64, 0:1], in0=ppc("fqg"), scalar1=0.125, scalar2=None,
                                              op0=ALU.mult), reads=["pp"], writes=["der0"])
        P.op("dve", lambda e: e.tensor_scalar(out=der[0:64, 1:2], in0=ppc("k_a"), scalar1=-1.0, scalar2=1.0,
                                              op0=ALU.mult, op1=ALU.add), reads=["pp"], writes=["der1"])
        P.op("dve", lambda e: e.tensor_scalar(out=der[0:1, 2:3], in0=pp[0:1, PP["fb"]:PP["fb"] + 1], scalar1=-1.0,
                                              scalar2=None, op0=ALU.mult), reads=["pp"], writes=["der2"])

        wfb = sb("wfb", [128, 8, NF], BF16)
        wtb = sb("wtb", [128, 8, NTM], BF16)
        xs1 = sb("xs", [128, 8, TT])
        wst = [xs1[:].rearrange("p c t -> p (c t)")[:, i * 2048:i * 2048 + NF + NTM] for i in range(2)]
        for c in range(8):
            st = wst[c % 2]
            k = "wst%d" % (c % 2)
            P.dma("sp", st[:, 0:NF], wf[c * 128:(c + 1) * 128, :], reads=["xsall"], writes=[k + "a"])
            P.dma("sp", st[:, NF:NF + NTM], wt[c * 128:(c + 1) * 128, :], reads=["xsall"], writes=[k + "b"])
            P.op("dve", lambda e, st=st, c=c: e.tensor_scalar(out=wfb[:, c, :], in0=st[:, 0:NF], scalar1=gm[:, c:c + 1],
                                                              scalar2=None, op0=ALU.mult),
                 reads=[k + "a", "gm"], writes=["wfb"])
            P.op("act", lambda e, st=st, c=c: e.activation(out=wtb[:, c, :], in_=st[:, NF:NF + NTM], func=AF.Copy,
                                                           scale=gm[:, c:c + 1]),
                 reads=[k + "b", "gm"], writes=["wtb"])
        for g in ("rk", "rks"):
            o, w = FOFF[g]
            P.op("dve", lambda e, o=o, w=w: e.tensor_scalar(out=wfb[:, :, o:o + w], in0=wfb[:, :, o:o + w], scalar1=0.125,
                                                            scalar2=None, op0=ALU.mult), reads=["wfb"], writes=["wfb"])

        xs = [xs1, xs1]
        xb = sb("xb", [128, 8, TT], BF16)
        sqr = [sb("sq%d" % i, [128, TT], BF16) for i in range(2)]
        rstd = sb("rstd", [128, TT])
        rstd_tm = sb("rstd_tm", [128, 4])
        gbuf = {}
        for n, w in FG:
            gbuf[n] = sb("g_" + n, [w, TT + 1])
        tm = sb("tm", [128, 4, NTM])
        ropet = sb("ropet", [64, 2, TT])
        pbank = [ps("pbank%d" % i, [128, 512]) for i in range(2)]
        misc = ps("misc", [128, 512])
        fsc = [ps("fsc%d" % i, [128, 512]) for i in range(2)]
        facc = ps("facc", [128, 4, 128])
        rps = ps("rps", [128, 512])
        wps = ps("wps", [128, 4, 128])

        Rst = sb("Rst", [64, 64]); Rbf = sb("Rbf", [64, 64], BF16)
        P.op("pool", lambda e: e.memset(Rst[:], 0.0), writes=["Rst"])
        P.op("pool", lambda e: e.memset(Rbf[:], 0.0), writes=["Rbf"])
        KT = sb("KT", [70, S], BF16)
        QT = sb("QT", [70, TT], BF16)
        VA = sb("VA", [128, S // 128, 65], BF16)
        P.op("pool", lambda e: e.memset(KT[64:70, :], 1.0), writes=["KTaug"])
        P.op("pool", lambda e: e.memset(QT[64:70, :], -1.0), writes=["QTaug"])
        P.op("pool", lambda e: e.memset(VA[:, :, 64:65], 1.0), writes=["VAone"])
        ccar = sb("ccar", [1, 1])
        P.op("pool", lambda e: e.memset(ccar[:], 0.0), writes=["ccar"])
        Sst = sb("Sst", [64, 5, 64])
        P.op("pool", lambda e: e.memset(Sst[:, 0, :], 0.0), writes=["S0"])
        for n in ("wr", "wk", "wv", "wxw", "wxa", "wxg"):
            P.op("pool", lambda e, n=n: e.memset(gbuf[n][:, 0:1], 0.0), writes=["g_" + n])

        pcount = [0]

        def proj_bank():
            i = pcount[0] % 2
            pcount[0] += 1
            return pbank[i], "pbank%d" % i

        rw = {}
        for n in ("mr", "mk", "mv", "mxw", "mxa"):
            rw[n] = sb("rw_" + n, [64, TT])
        rw["mxg"] = sb("rw_mxg", [128, TT])
        for n in ("t1", "t2", "t3", "t4", "cl", "asb", "kk", "rkb"):
            rw[n] = sb("rw_" + n, [64, TT])
        for n in ("At", "Bt", "Kt", "Rt", "Bh", "Kh", "mvb"):
            rw[n] = sb("rw_" + n, [64, TT], BF16)
        sgx = sb("rw_sgx", [128, TT], BF16)
        d128 = sb("rw_d128", [128, TT])
        g2h_bf = sb("g2h_bf", [128, 64], BF16)
        P.op("dve", lambda e: e.tensor_copy(out=g2h_bf[:], in_=g2h[:]), reads=["g2h"], writes=["g2h_bf"])

        def proj_gen(n):
            t0 = n * TT
            X = xs[n % 2]
            xk = "xs"
            for c in range(8):
                P.dma("sp" if c % 2 == 0 else "pool", X[:, c, :], xT[c * 128:(c + 1) * 128, t0:t0 + TT],
                      writes=[xk + "_%d" % c] + (["wst0a", "wst0b", "wst1a", "wst1b"] if n == 0 else []))
            P.dma("sp", ropet[:], rope_d[:, :, t0:t0 + TT], writes=["ropet"])
            xr = [xk + "_%d" % c for c in range(8)]
            P.op("act", lambda e, X=X: e.activation(out=xb[:], in_=X[:], func=AF.Copy), reads=xr, writes=["xb"])
            for c in range(8):
                P.op("pool", lambda e, X=X, c=c: e.tensor_tensor(out=sqr[c % 2][:], in0=X[:, c, :], in1=X[:, c, :],
                                                                 op=ALU.mult), reads=[xr[c]], writes=["sq%d" % (c % 2)])
                P.op("pe", lambda e, c=c: e.matmul(misc[:], lhsT=ones_bf[:], rhs=sqr[c % 2][:], start=(c == 0),
                                                   stop=(c == 7)), reads=["ones_bf", "sq%d" % (c % 2)], writes=["misc"])
            P.op("act", lambda e: e.activation(out=rstd[:], in_=misc[:], func=AF.Ln, scale=1.0 / D, bias=1e-6),
                 reads=["misc"], writes=["rstd"])
            P.op("act", lambda e: e.activation(out=rstd[:], in_=rstd[:], func=AF.Exp, scale=-0.5), reads=["rstd"],
                 writes=["rstd"])
            for s in range(4):
                P.op("pe", lambda e, s=s: e.matmul(misc[:, s:s + 1], lhsT=rstd[0:1, s * 128:(s + 1) * 128],
                                                   rhs=ones_f[0:1, 0:1], start=True, stop=True),
                     reads=["rstd", "ones_f"], writes=["misc"])
            P.op("dve", lambda e: e.tensor_copy(out=rstd_tm[:], in_=misc[:, 0:4]), reads=["misc"], writes=["rstd_tm"])
            yield
            for gname, gw in FG:
                o, w = FOFF[gname]
                bank, bk = proj_bank()
                for c in range(8):
                    P.op("pe", lambda e, c=c, o=o, w=w, bank=bank: e.matmul(bank[0:w, :], lhsT=wfb[:, c, o:o + w],
                                                                            rhs=xb[:, c, :], start=(c == 0), stop=(c == 7)),
                         reads=["wfb", "xb"], writes=[bk])
                P.op("dve", lambda e, w=w, bank=bank, gname=gname: e.tensor_tensor(
                    out=gbuf[gname][:, 1:TT + 1], in0=bank[0:w, :], in1=rstd[0:w, :], op=ALU.mult),
                    reads=[bk, "rstd"], writes=["g_" + gname])
                yield
            for s in range(4):
                bank, bk = proj_bank()
                for c in range(8):
                    P.op("pe", lambda e, c=c, s=s, bank=bank: e.matmul(bank[:, 0:NTM], lhsT=xb[:, c, s * 128:(s + 1) * 128],
                                                                       rhs=wtb[:, c, :], start=(c == 0), stop=(c == 7)),
                         reads=["wtb", "xb"], writes=[bk])
                P.op("act", lambda e, s=s, bank=bank: e.activation(out=tm[:, s, :], in_=bank[:, 0:NTM], func=AF.Copy,
                                                                   scale=rstd_tm[:, s:s + 1]),
                     reads=[bk, "rstd_tm"], writes=["tm%d" % s])
                yield


        def run_all(items):
            active = [list(it) for it in items]
            rnd = 0
            while active:
                for it in list(active):
                    g, m, k = it
                    if rnd % k:
                        continue
                    for _ in range(m):
                        try:
                            next(g)
                        except StopIteration:
                            active.remove(it)
                            break
                rnd += 1

        L = dict(locals())
        run_all([(proj_gen(0), 1, 1)])
        for n in range(NT):
            ret_head(P, nc, n, L)
            fox_head(P, nc, n, L)
            rwkv_head(P, nc, n, L)
            items = [(rwkv_body(P, nc, n, L), RW_RATE, 1), (fox_body(P, nc, n, L), 1, 1), (ret_body(P, nc, n, L), 1, 4)]
            if n + 1 < NT:
                items.append((proj_gen(n + 1), 1, 3))
            run_all(items)


def ret_head(P, nc, n, L):
    gbuf, ropet, tm, rps, dmat, bc, pp = L["gbuf"], L["ropet"], L["tm"], L["rps"], L["dmat"], L["bc"], L["pp"]
    Rst, Rbf, iden_bf, y_ret, sb = L["Rst"], L["Rbf"], L["iden_bf"], L["y_ret"], L["sb"]
    ppc = L["ppc"]
    if n == 0:
        R = {}
        R["t1"] = sb("r_t1", [64, TT]); R["t2"] = sb("r_t2", [64, TT])
        R["qT"] = sb("r_qT", [64, TT], BF16); R["kT"] = sb("r_kT", [64, TT], BF16); R["qd"] = sb("r_qd", [64, TT], BF16)
        R["kd"] = sb("r_kd", [128, 64], BF16); R["sc"] = sb("r_sc", [128, 128], BF16)
        R["y"] = sb("r_y", [128, 64]); R["st"] = sb("r_st", [128, 4]); R["yc"] = sb("r_yc", [128, 64])
        R["junk"] = sb("r_junk", [128, 64]); R["yo"] = sb("r_yo", [128, 64])
        R["vall"] = sb("r_vall", [128, 4, 64], BF16); R["sgall"] = sb("r_sgall", [128, 4, 64])
        ret_head.R = R
    R = ret_head.R
    t0 = n * TT
    cosv, sinv = ropet[:, 0, :], ropet[:, 1, :]
    for nm, a, b_ in (("qT", "rq", "rqs"), ("kT", "rk", "rks")):
        P.op("pool", lambda e, a=a: e.tensor_tensor(out=R["t1"][:], in0=gbuf[a][:, 1:TT + 1], in1=cosv, op=ALU.mult),
             reads=["g_" + a, "ropet"], writes=["r_t1"])
        P.op("pool", lambda e, b_=b_: e.tensor_tensor(out=R["t2"][:], in0=gbuf[b_][:, 1:TT + 1], in1=sinv, op=ALU.mult),
             reads=["g_" + b_, "ropet"], writes=["r_t2"])
        P.op("dve", lambda e, nm=nm: e.tensor_tensor(out=R[nm][:], in0=R["t1"][:], in1=R["t2"][:], op=ALU.add),
             reads=["r_t1", "r_t2"], writes=["r_" + nm])
    P.op("pool", lambda e: e.tensor_tensor(out=R["qd"][:].rearrange("p (c j) -> p c j", j=128),
                                           in0=R["qT"][:].rearrange("p (c j) -> p c j", j=128),
                                           in1=bc[0:64, 128:256].unsqueeze(1).to_broadcast([64, 4, 128]), op=ALU.mult),
         reads=["r_qT", "bc"], writes=["r_qd"])
    tmk = ["tm%d" % s for s in range(4)]
    P.op("pool", lambda e: e.tensor_copy(out=R["vall"][:], in_=tm[:, :, 0:64]), reads=tmk, writes=["r_vall"])
    P.op("act", lambda e: e.activation(out=R["sgall"][:], in_=tm[:, :, 128:192], func=AF.Silu), reads=tmk, writes=["r_sgall"])
    P.op("pool", lambda e: e.tensor_tensor(out=R["sgall"][:], in0=R["sgall"][:],
                                           in1=bc[:, 0:64].unsqueeze(1).to_broadcast([128, 4, 64]), op=ALU.mult),
         reads=["r_sgall", "bc"], writes=["r_sgall"])


def ret_body(P, nc, n, L):
    gbuf, ropet, tm, rps, dmat, bc, pp = L["gbuf"], L["ropet"], L["tm"], L["rps"], L["dmat"], L["bc"], L["pp"]
    Rst, Rbf, iden_bf, y_ret, sb = L["Rst"], L["Rbf"], L["iden_bf"], L["y_ret"], L["sb"]
    ppc = L["ppc"]
    R = ret_head.R
    t0 = n * TT
    for s in range(4):
        cs = slice(s * 128, (s + 1) * 128)
        P.op("pe", lambda e, cs=cs: e.matmul(rps[:, 320:384], lhsT=R["kT"][:, cs], rhs=iden_bf[0:64, 0:64], start=True,
                                             stop=True), reads=["r_kT", "iden_bf"], writes=["rps_k"])
        P.op("dve", lambda e: e.tensor_scalar(out=R["kd"][:], in0=rps[:, 320:384], scalar1=ppc("kdec", 128), scalar2=None,
                                              op0=ALU.mult), reads=["rps_k", "pp"], writes=["r_kd"])
        P.op("pe", lambda e, cs=cs: e.matmul(rps[:, 0:128], lhsT=R["kT"][:, cs], rhs=R["qT"][:, cs], start=True, stop=True),
             reads=["r_kT", "r_qT"], writes=["rps_s"])
        P.op("dve", lambda e: e.tensor_tensor(out=R["sc"][:], in0=rps[:, 0:128], in1=dmat[:], op=ALU.mult),
             reads=["rps_s", "dmat"], writes=["r_sc"])
        P.op("pe", lambda e, s=s: e.matmul(rps[:, 128:192], lhsT=R["sc"][:], rhs=R["vall"][:, s, :], start=True, stop=False),
             reads=["r_sc", "r_vall"], writes=["rps_y"])
        P.op("pe", lambda e, cs=cs: e.matmul(rps[:, 128:192], lhsT=R["qd"][:, cs], rhs=Rbf[:], start=False, stop=True),
             reads=["r_qd", "Rbf"], writes=["rps_y"])
        P.op("pe", lambda e, s=s: e.matmul(rps[0:64, 192:256], lhsT=R["kd"][:], rhs=R["vall"][:, s, :], start=True, stop=True),
             reads=["r_kd", "r_vall"], writes=["rps_u"])
        yield
        P.op("dve", lambda e: e.scalar_tensor_tensor(out=Rst[:], in0=Rst[:], scalar=ppc("cdec"), in1=rps[0:64, 192:256],
                                                     op0=ALU.mult, op1=ALU.add), reads=["Rst", "rps_u", "pp"], writes=["Rst"])
        P.op("act", lambda e: e.activation(out=Rbf[:], in_=Rst[:], func=AF.Copy), reads=["Rst"], writes=["Rbf"])
        yield
        st = R["st"]
        P.op("act", lambda e: e.activation(out=R["y"][:], in_=rps[:, 128:192], func=AF.Copy, accum_out=st[:, 0:1]),
             reads=["rps_y"], writes=["r_y", "r_st0"])
        P.op("dve", lambda e: e.tensor_scalar(out=st[:, 1:2], in0=st[:, 0:1], scalar1=1.0 / 64, scalar2=None, op0=ALU.mult),
             reads=["r_st0"], writes=["r_st1"])
        P.op("dve", lambda e: e.tensor_scalar(out=R["yc"][:], in0=R["y"][:], scalar1=st[:, 1:2], scalar2=None,
                                              op0=ALU.subtract), reads=["r_y", "r_st1"], writes=["r_yc"])
        P.op("act", lambda e: e.activation(out=R["junk"][:], in_=R["yc"][:], func=AF.Square, accum_out=st[:, 2:3]),
             reads=["r_yc"], writes=["r_junk", "r_st2"])
        P.op("act", lambda e: e.activation(out=st[:, 3:4], in_=st[:, 2:3], func=AF.Ln, scale=1.0 / 64, bias=1e-5),
             reads=["r_st2"], writes=["r_st3"])
        P.op("act", lambda e: e.activation(out=st[:, 3:4], in_=st[:, 3:4], func=AF.Exp, scale=-0.5), reads=["r_st3"],
             writes=["r_st3"])
        P.op("dve", lambda e, s=s: e.scalar_tensor_tensor(out=R["yo"][:], in0=R["yc"][:], scalar=st[:, 3:4], in1=R["sgall"][:, s, :],
                                                          op0=ALU.mult, op1=ALU.mult), reads=["r_yc", "r_st3", "r_sgall"],
             writes=["r_yo"])
        P.dma("sp", y_ret[t0 + s * 128:t0 + (s + 1) * 128, :], R["yo"][:], reads=["r_yo"])
        yield


def fox_head(P, nc, n, L):
    gbuf, tm, misc, pp, der = L["gbuf"], L["tm"], L["misc"], L["pp"], L["der"]
    KT, QT, VA, fsc, facc, ccar = L["KT"], L["QT"], L["VA"], L["fsc"], L["facc"], L["ccar"]
    ones_bf, ones_f, mski_bf, y_fox, sb, ppc = L["ones_bf"], L["ones_f"], L["mski_bf"], L["y_fox"], L["sb"], L["ppc"]
    if n == 0:
        F = {}
        F["sq"] = sb("f_sq", [64, TT], BF16); F["rs"] = sb("f_rs", [64, TT])
        F["e"] = sb("f_e", [1, TT]); F["c"] = sb("f_c", [1, TT]); F["r1"] = sb("f_r1", [1, TT])
        F["pc"] = sb("f_pc", [1, 3, TT], BF16)
        F["pT"] = [sb("f_pT%d" % i, [128, TT], BF16) for i in range(2)]
        F["acc"] = sb("f_acc", [128, 4, 65]); F["rec"] = sb("f_rec", [128, 4, 1]); F["yo"] = sb("f_yo", [128, 4, 64])
        F["cnt"] = 0
        fox_head.F = F
    F = fox_head.F
    t0 = n * TT
    for src, gcol, dst, dk in (("fq", der[0:64, 0:1], QT[0:64, :], "QTm"), ("fk", ppc("fkg"), KT[0:64, t0:t0 + TT], "KTm")):
        P.op("act", lambda e, src=src: e.activation(out=F["sq"][:], in_=gbuf[src][:, 1:TT + 1], func=AF.Square),
             reads=["g_" + src], writes=["f_sq"])
        P.op("pe", lambda e: e.matmul(misc[0:64, :], lhsT=ones_bf[0:64, 0:64], rhs=F["sq"][:], start=True, stop=True),
             reads=["ones_bf", "f_sq"], writes=["misc"])
        P.op("act", lambda e: e.activation(out=F["rs"][:], in_=misc[0:64, :], func=AF.Ln, scale=1.0 / 64, bias=1e-6),
             reads=["misc"], writes=["f_rs"])
        P.op("act", lambda e: e.activation(out=F["rs"][:], in_=F["rs"][:], func=AF.Exp, scale=-0.5), reads=["f_rs"],
             writes=["f_rs"])
        P.op("dve", lambda e, src=src, gcol=gcol, dst=dst: e.scalar_tensor_tensor(
            out=dst, in0=gbuf[src][:, 1:TT + 1], scalar=gcol, in1=F["rs"][:], op0=ALU.mult, op1=ALU.mult),
            reads=["g_" + src, "f_rs", "pp", "der0"], writes=[dk])
    P.op("act", lambda e: e.activation(out=F["e"][:], in_=gbuf["ff"][:, 1:TT + 1], func=AF.Exp, scale=-1.0,
                                       bias=der[0:1, 2:3]), reads=["g_ff", "der2"], writes=["f_e"])
    P.op("act", lambda e: e.activation(out=F["e"][:], in_=F["e"][:], func=AF.Ln, bias=1.0), reads=["f_e"], writes=["f_e"])
    P.op("dve", lambda e: e.tensor_tensor_scan(out=F["c"][:], data0=L["mask_one"][:],
                                               data1=F["e"][:], initial=ccar[:], op0=ALU.mult, op1=ALU.subtract),
         reads=["f_e", "ccar", "mask_one"], writes=["f_c"])
    P.op("dve", lambda e: e.tensor_copy(out=ccar[:], in_=F["c"][:, TT - 1:TT]), reads=["f_c"], writes=["ccar"])
    pc = F["pc"]
    P.op("dve", lambda e: e.tensor_copy(out=pc[:, 0, :], in_=F["c"][:]), reads=["f_c"], writes=["f_pc0"])
    P.op("dve", lambda e: e.tensor_tensor(out=F["r1"][:], in0=F["c"][:], in1=pc[:, 0, :], op=ALU.subtract),
         reads=["f_c", "f_pc0"], writes=["f_r1"])
    P.op("dve", lambda e: e.tensor_copy(out=pc[:, 1, :], in_=F["r1"][:]), reads=["f_r1"], writes=["f_pc1"])
    P.op("dve", lambda e: e.tensor_tensor(out=F["r1"][:], in0=F["r1"][:], in1=pc[:, 1, :], op=ALU.subtract),
         reads=["f_r1", "f_pc1"], writes=["f_r1"])
    P.op("dve", lambda e: e.tensor_copy(out=pc[:, 2, :], in_=F["r1"][:]), reads=["f_r1"], writes=["f_pc2"])
    pcr = ["f_pc0", "f_pc1", "f_pc2"]
    P.dma("sp", QT[64:67, :], pc[0:1, :, :], reads=pcr + ["QTaug"], writes=["QTc"])
    P.dma("sp", KT[67:70, t0:t0 + TT], pc[0:1, :, :], reads=pcr + ["KTaug"], writes=["KTc"])
    for s in range(4):
        P.op("pool", lambda e, s=s: e.tensor_copy(out=VA[:, 4 * n + s, 0:64], in_=tm[:, s, 64:128]), reads=["tm%d" % s],
             writes=["VAv"])


def fox_body(P, nc, n, L):
    gbuf, tm, misc, pp, der = L["gbuf"], L["tm"], L["misc"], L["pp"], L["der"]
    KT, QT, VA, fsc, facc, ccar = L["KT"], L["QT"], L["VA"], L["fsc"], L["facc"], L["ccar"]
    ones_bf, ones_f, mski_bf, y_fox, sb, ppc = L["ones_bf"], L["ones_f"], L["mski_bf"], L["y_fox"], L["sb"], L["ppc"]
    F = fox_head.F
    t0 = n * TT
    nk = 4 * n + 4
    P.op("dve", lambda e: e.memset(facc[:, :, 0:65], 0.0), writes=["facc"])
    for kt in range(nk):
        jmin = max(0, kt - 4 * n)
        c0 = jmin * 128
        i = F["cnt"] % 2
        F["cnt"] += 1
        sc, sk, pT, pk = fsc[i], "fsc%d" % i, F["pT"][i], "f_pT%d" % i
        P.op("pe", lambda e, sc=sc, kt=kt, c0=c0: e.matmul(sc[:, c0:TT], lhsT=KT[0:70, kt * 128:(kt + 1) * 128],
                                                           rhs=QT[0:70, c0:TT], start=True, stop=True),
             reads=["KTm", "KTc", "KTaug", "QTm", "QTc", "QTaug"], writes=[sk])
        P.op("act", lambda e, sc=sc, pT=pT, c0=c0: e.activation(out=pT[:, c0:TT], in_=sc[:, c0:TT], func=AF.Exp),
             reads=[sk], writes=[pk])
        if kt >= 4 * n:
            P.op("pool", lambda e, pT=pT, c0=c0: e.tensor_tensor(out=pT[:, c0:c0 + 128], in0=pT[:, c0:c0 + 128],
                                                                 in1=mski_bf[:], op=ALU.mult),
                 reads=[pk, "mski_bf"], writes=[pk])
        for qs in range(jmin, 4):
            P.op("pe", lambda e, pT=pT, qs=qs, kt=kt: e.matmul(facc[:, qs, 0:65], lhsT=pT[:, qs * 128:(qs + 1) * 128],
                                                               rhs=VA[:, kt, :], start=False, stop=(kt == 4 * n + qs), skip_group_check=True),
                 reads=[pk, "VAv", "VAone"], writes=["facc"])
        yield
    P.op("act", lambda e: e.activation(out=F["acc"][:], in_=facc[:, :, 0:65], func=AF.Copy), reads=["facc"], writes=["f_acc"])
    P.op("dve", lambda e: e.reciprocal(out=F["rec"][:], in_=F["acc"][:, :, 64:65]), reads=["f_acc"], writes=["f_rec"])
    P.op("dve", lambda e: e.tensor_tensor(out=F["yo"][:], in0=F["acc"][:, :, 0:64],
                                          in1=F["rec"][:].to_broadcast([128, 4, 64]), op=ALU.mult),
         reads=["f_acc", "f_rec"], writes=["f_yo"])
    P.dma("sp", y_fox[t0:t0 + TT, :].rearrange("(s p) d -> p s d", p=128), F["yo"][:], reads=["f_yo"])


def rwkv_head(P, nc, n, L):
    gbuf, rw, sgx, misc, wps, pp, der, bc = L["gbuf"], L["rw"], L["sgx"], L["misc"], L["wps"], L["pp"], L["der"], L["bc"]
    w2h, a2h, g2h, msk, iden, ones_f, mask64, Sst = (L[k] for k in ("w2h", "a2h", "g2h", "msk", "iden", "ones_f", "mask64", "Sst"))
    y_rw, sb, ppc = L["y_rw"], L["sb"], L["ppc"]
    if n == 0:
        W = {}
        for nm in ("Na", "Nb", "La", "Lb", "Ta", "Tb", "AkT", "ArbT", "ArkT"):
            W[nm] = sb("w_" + nm, [128, 128], BF16)
        for nm in ("X2", "M1", "M2", "Atm", "Bhtm", "Khtm", "Vtm"):
            W[nm] = sb("w_" + nm, [128, 64], BF16)
        for nm in ("GT", "H"):
            W[nm] = sb("w_" + nm, [64, 64])
        W["QtT"] = sb("w_QtT", [64, 128])
        for nm in ("y", "yc", "junk", "o1", "o2", "yo"):
            W[nm] = sb("w_" + nm, [128, 64])
        W["PCs"] = sb("w_PCs", [64, 4]); W["st"] = sb("w_st", [128, 6])
        W["slot"] = 0
        rwkv_head.W = W
    W = rwkv_head.W
    t0 = n * TT
    i64 = iden[0:64, 0:64]
    for g, m, mu in (("wr", "mr", "mu_r"), ("wk", "mk", "mu_k"), ("wv", "mv", "mu_v"), ("wxw", "mxw", "mu_xw"),
                     ("wxa", "mxa", "mu_xa"), ("wxg", "mxg", "mu_xg")):
        w = 128 if g == "wxg" else 64
        tmp = L["d128"] if g == "wxg" else rw["t1"]
        tk = "d128" if g == "wxg" else "rw_t1"
        gb = gbuf[g]
        P.op("pool", lambda e, gb=gb, tmp=tmp: e.tensor_tensor(out=tmp[:], in0=gb[:, 0:TT], in1=gb[:, 1:TT + 1], op=ALU.subtract),
             reads=["g_" + g], writes=[tk])
        P.op("dve", lambda e, gb=gb, tmp=tmp, m=m, mu=mu, w=w: e.scalar_tensor_tensor(
            out=rw[m][:], in0=tmp[:], scalar=ppc(mu, w), in1=gb[:, 1:TT + 1], op0=ALU.mult, op1=ALU.add),
            reads=[tk, "g_" + g, "pp"], writes=["rw_" + m])
        P.op("pool", lambda e, gb=gb: e.tensor_copy(out=gb[:, 0:1], in_=gb[:, TT:TT + 1]), reads=[], writes=["g_" + g])


def rwkv_body(P, nc, n, L):
    gbuf, rw, sgx, misc, wps, pp, der, bc = L["gbuf"], L["rw"], L["sgx"], L["misc"], L["wps"], L["pp"], L["der"], L["bc"]
    w2h, a2h, g2h, msk, iden, ones_f, mask64, Sst = (L[k] for k in ("w2h", "a2h", "g2h", "msk", "iden", "ones_f", "mask64", "Sst"))
    y_rw, sb, ppc = L["y_rw"], L["sb"], L["ppc"]
    W = rwkv_head.W
    t0 = n * TT
    i64 = iden[0:64, 0:64]
    i64b = L["iden_bf"][0:64, 0:64]
    t1, t2, t3, cl = rw["t1"], rw["t2"], rw["t3"], rw["cl"]
    P.op("act", lambda e: e.activation(out=t1[:], in_=rw["mxw"][:], func=AF.Tanh), reads=["rw_mxw"], writes=["rw_t1"])
    P.op("pe", lambda e: e.matmul(misc[0:64, :], lhsT=w2h[:], rhs=t1[:], start=True, stop=True), reads=["w2h", "rw_t1"],
         writes=["misc"])
    P.op("act", lambda e: e.activation(out=t2[:], in_=misc[0:64, :], func=AF.Sigmoid, bias=ppc("w0")), reads=["misc", "pp"],
         writes=["rw_t2"])
    P.op("dve", lambda e: e.tensor_scalar(out=t2[:], in0=t2[:], scalar1=-0.6065306597126334, scalar2=None, op0=ALU.mult),
         reads=["rw_t2"], writes=["rw_t2"])
    P.op("dve", lambda e: e.tensor_tensor_scan(out=cl[:], data0=mask64[:], data1=t2[:], initial=0.0, op0=ALU.mult,
                                               op1=ALU.add), reads=["mask64", "rw_t2"], writes=["rw_cl"])
    yield
    P.op("act", lambda e: e.activation(out=t3[:], in_=cl[:], func=AF.Exp), reads=["rw_cl"], writes=["rw_t3"])
    P.op("dve", lambda e: e.tensor_tensor(out=rw["Rt"][:], in0=rw["mr"][:], in1=t3[:], op=ALU.mult), reads=["rw_mr", "rw_t3"],
         writes=["rw_Rt"])
    P.op("dve", lambda e: e.tensor_copy(out=W["PCs"][:], in_=t3[:].rearrange("p (c j) -> p c j", j=128)[:, :, 127]),
         reads=["rw_t3"], writes=["w_PCs"])
    P.op("dve", lambda e: e.tensor_tensor(out=t1[:], in0=cl[:], in1=t2[:], op=ALU.subtract), reads=["rw_cl", "rw_t2"],
         writes=["rw_t1"])
    P.op("act", lambda e: e.activation(out=t1[:], in_=t1[:], func=AF.Exp), reads=["rw_t1"], writes=["rw_t1"])
    yield
    P.op("pe", lambda e: e.matmul(misc[0:64, :], lhsT=a2h[:], rhs=rw["mxa"][:], start=True, stop=True),
         reads=["a2h", "rw_mxa"], writes=["misc"])
    P.op("act", lambda e: e.activation(out=rw["asb"][:], in_=misc[0:64, :], func=AF.Sigmoid, bias=ppc("a0")),
         reads=["misc", "pp"], writes=["rw_asb"])
    yield
    kk = rw["kk"]
    P.op("dve", lambda e: e.tensor_scalar(out=kk[:], in0=rw["mk"][:], scalar1=ppc("k_k"), scalar2=None, op0=ALU.mult),
         reads=["rw_mk", "pp"], writes=["rw_kk"])
    P.op("act", lambda e: e.activation(out=t2[:], in_=kk[:], func=AF.Square), reads=["rw_kk"], writes=["rw_t2"])
    P.op("pe", lambda e: e.matmul(misc[0:64, :], lhsT=ones_f[0:64, 0:64], rhs=t2[:], start=True, stop=True),
         reads=["ones_f", "rw_t2"], writes=["misc"])
    P.op("dve", lambda e: e.tensor_scalar(out=t2[:], in0=misc[0:64, :], scalar1=1e-24, scalar2=None, op0=ALU.max),
         reads=["misc"], writes=["rw_t2"])
    P.op("act", lambda e: e.activation(out=t2[:], in_=t2[:], func=AF.Ln), reads=["rw_t2"], writes=["rw_t2"])
    P.op("act", lambda e: e.activation(out=t2[:], in_=t2[:], func=AF.Exp, scale=-0.5), reads=["rw_t2"], writes=["rw_t2"])
    P.op("dve", lambda e: e.tensor_tensor(out=kk[:], in0=kk[:], in1=t2[:], op=ALU.mult), reads=["rw_kk", "rw_t2"],
         writes=["rw_kk"])
    yield
    P.op("dve", lambda e: e.scalar_tensor_tensor(out=rw["At"][:], in0=kk[:], scalar=-1.0, in1=t1[:], op0=ALU.mult,
                                                 op1=ALU.mult), reads=["rw_kk", "rw_t1"], writes=["rw_At"])
    P.op("act", lambda e: e.activation(out=t3[:], in_=cl[:], func=AF.Exp, scale=-1.0), reads=["rw_cl"], writes=["rw_t3"])
    P.op("dve", lambda e: e.tensor_scalar(out=t2[:], in0=rw["asb"][:], scalar1=ppc("k_a"), scalar2=der[0:64, 1:2],
                                          op0=ALU.mult, op1=ALU.add), reads=["rw_asb", "pp", "der1"], writes=["rw_t2"])
    t4 = rw["t4"]
    P.op("dve", lambda e: e.tensor_tensor(out=t4[:], in0=rw["mk"][:], in1=t2[:], op=ALU.mult), reads=["rw_mk", "rw_t2"],
         writes=["rw_t4"])
    P.op("dve", lambda e: e.scalar_tensor_tensor(out=rw["rkb"][:], in0=t4[:], scalar=ppc("r_k"), in1=rw["mr"][:],
                                                 op0=ALU.mult, op1=ALU.mult), reads=["rw_t4", "rw_mr", "pp"], writes=["rw_rkb"])
    yield
    P.op("pool", lambda e: e.tensor_tensor(out=t2[:], in0=kk[:], in1=rw["asb"][:], op=ALU.mult), reads=["rw_kk", "rw_asb"],
         writes=["rw_t2"])
    P.op("dve", lambda e: e.tensor_tensor(out=rw["Kt"][:], in0=t4[:], in1=t3[:], op=ALU.mult), reads=["rw_t4", "rw_t3"],
         writes=["rw_Kt"])
    P.op("dve", lambda e: e.tensor_tensor(out=rw["Bt"][:], in0=t2[:], in1=t3[:], op=ALU.mult), reads=["rw_t2", "rw_t3"],
         writes=["rw_Bt"])
    P.op("pool", lambda e: e.tensor_copy(out=rw["mvb"][:], in_=rw["mv"][:]), reads=["rw_mv"], writes=["rw_mvb"])
    for src, dst in (("Bt", "Bh"), ("Kt", "Kh")):
        P.op("pool", lambda e, src=src, dst=dst: e.tensor_tensor(
            out=rw[dst][:].rearrange("p (c j) -> p c j", j=128), in0=rw[src][:].rearrange("p (c j) -> p c j", j=128),
            in1=W["PCs"][:].unsqueeze(2).to_broadcast([64, 4, 128]), op=ALU.mult),
            reads=["rw_" + src, "w_PCs"], writes=["rw_" + dst])
    P.op("act", lambda e: e.activation(out=sgx[:], in_=rw["mxg"][:], func=AF.Sigmoid), reads=["rw_mxg"], writes=["sgx"])
    yield

    NSL = 4

    def mm(rows, cols, lhsT, rhs, reads, start=True, stop=True, slot=None):
        if slot is None:
            slot = W["slot"] % NSL
            W["slot"] += 1
        P.op("pe", lambda e: e.matmul(wps[0:rows, slot, 0:cols], lhsT=lhsT, rhs=rhs, start=start, stop=stop), reads=reads,
             writes=["wps_%d" % slot])
        return slot

    def ev_copy(dst, slot, rows, cols, eng="act"):
        if eng == "act":
            P.op("act", lambda e: e.activation(out=W[dst][:], in_=wps[0:rows, slot, 0:cols], func=AF.Copy),
                 reads=["wps_%d" % slot], writes=["w_" + dst])
        else:
            P.op("dve", lambda e: e.tensor_copy(out=W[dst][:], in_=wps[0:rows, slot, 0:cols]), reads=["wps_%d" % slot],
                 writes=["w_" + dst])

    def ev_mask(dst, slot, mi):
        P.op("dve", lambda e: e.tensor_tensor(out=W[dst][:], in0=wps[:, slot, :], in1=msk[:, mi, :], op=ALU.mult),
             reads=["wps_%d" % slot, "msk"], writes=["w_" + dst])

    CH = 128
    for c in range(TT // CH):
        cs = slice(c * CH, (c + 1) * CH)
        At, Bt, Kt, Rt, Bh, Kh = (rw[k][:, cs] for k in ("At", "Bt", "Kt", "Rt", "Bh", "Kh"))
        for src, srck, dst in ((rw["mvb"][:, cs], "rw_mvb", "Vtm"), (At, "rw_At", "Atm"), (Bh, "rw_Bh", "Bhtm"), (Kh, "rw_Kh", "Khtm")):
            sl = mm(CH, 64, src, i64b, [srck, "iden_bf"])
            ev_copy(dst, sl, CH, 64)
            yield
        sl = mm(CH, CH, Bt, At, ["rw_Bt", "rw_At"]); ev_mask("Na", sl, 0)
        yield
        sl = mm(CH, CH, At, Bt, ["rw_Bt", "rw_At"]); ev_mask("La", sl, 1)
        P.op("dve", lambda e: e.tensor_tensor(out=W["Ta"][:], in0=W["Na"][:], in1=L["iden_bf"][:], op=ALU.add), reads=["w_Na", "iden_bf"],
             writes=["w_Ta"])
        yield
        Nc, Lc, Tc = "Na", "La", "Ta"
        other = {"Na": "Nb", "Nb": "Na", "La": "Lb", "Lb": "La", "Ta": "Tb", "Tb": "Ta"}
        for k in range(6):
            Nn, Ln, Tn = other[Nc], other[Lc], other[Tc]
            if k < 5:
                sl = mm(CH, CH, W[Lc][:], W[Nc][:], ["w_" + Lc, "w_" + Nc]); ev_copy(Nn, sl, CH, CH, "act")
                yield
            sl = mm(CH, CH, W[Nc][:], W[Lc][:], ["w_" + Lc, "w_" + Nc]); ev_copy(Ln, sl, CH, CH, "act")
            yield
            sl = mm(CH, CH, W[Ln][:], W[Tc][:], ["w_" + Ln, "w_" + Tc])
            P.op("dve", lambda e, sl=sl, Tc=Tc, Tn=Tn: e.tensor_tensor(out=W[Tn][:], in0=wps[:, sl, :], in1=W[Tc][:], op=ALU.add),
                 reads=["wps_%d" % sl, "w_" + Tc], writes=["w_" + Tn])
            yield
            Nc, Lc, Tc = Nn, Ln, Tn
        TT_ = W[Tc]
        tk = "w_" + Tc
        sl = mm(CH, CH, Kt, At, ["rw_Kt", "rw_At"]); ev_mask("AkT", sl, 0)
        yield
        sl = mm(CH, 64, W["AkT"][:], W["Vtm"][:], ["w_AkT", "w_Vtm"]); ev_copy("X2", sl, CH, 64)
        yield
        sl = mm(CH, 64, TT_[:], W["X2"][:], [tk, "w_X2"]); ev_copy("M2", sl, CH, 64)
        yield
        sl = mm(CH, 64, TT_[:], W["Atm"][:], [tk, "w_Atm"]); ev_copy("M1", sl, CH, 64, "dve")
        yield
        sl = mm(64, 64, W["M1"][:], W["Bhtm"][:], ["w_M1", "w_Bhtm"])
        P.op("dve", lambda e, sl=sl, c=c: e.scalar_tensor_tensor(out=W["GT"][:], in0=i64, scalar=W["PCs"][:, c:c + 1],
                                                                 in1=wps[0:64, sl, 0:64], op0=ALU.mult, op1=ALU.add),
             reads=["wps_%d" % sl, "iden", "w_PCs"], writes=["w_GT"])
        yield
        sl = mm(64, 64, W["Bhtm"][:], W["M2"][:], ["w_Bhtm", "w_M2"], start=True, stop=False)
        mm(64, 64, W["Khtm"][:], W["Vtm"][:], ["w_Khtm", "w_Vtm"], start=False, stop=True, slot=sl)
        ev_copy("H", sl, 64, 64)
        yield
        sl = mm(64, 64, W["GT"][:], Sst[:, c, :], ["w_GT", "S%d" % c])
        P.op("dve", lambda e, sl=sl, c=c: e.tensor_tensor(out=Sst[:, c + 1, :], in0=wps[0:64, sl, 0:64], in1=W["H"][:], op=ALU.add),
             reads=["wps_%d" % sl, "w_H"], writes=["S%d" % (c + 1)])
        yield
        sl = mm(CH, CH, Bt, Rt, ["rw_Bt", "rw_Rt"]); ev_mask("ArbT", sl, 2)
        yield
        sl = mm(CH, CH, Kt, Rt, ["rw_Kt", "rw_Rt"]); ev_mask("ArkT", sl, 2)
        yield
        sl = mm(64, CH, W["M1"][:], W["ArbT"][:], ["w_M1", "w_ArbT"])
        P.op("dve", lambda e, sl=sl, Rt=Rt: e.tensor_tensor(out=W["QtT"][:], in0=wps[0:64, sl, :], in1=Rt, op=ALU.add),
             reads=["wps_%d" % sl, "rw_Rt"], writes=["w_QtT"])
        yield
        sl = mm(CH, 64, W["QtT"][:], Sst[:, c, :], ["w_QtT", "S%d" % c], start=True, stop=False)
        mm(CH, 64, W["ArbT"][:], W["M2"][:], ["w_ArbT", "w_M2"], start=False, stop=False, slot=sl)
        mm(CH, 64, W["ArkT"][:], W["Vtm"][:], ["w_ArkT", "w_Vtm"], start=False, stop=True, slot=sl)
        st = W["st"]
        P.op("act", lambda e, sl=sl: e.activation(out=W["y"][:], in_=wps[:, sl, 0:64], func=AF.Copy, accum_out=st[:, 0:1]),
             reads=["wps_%d" % sl], writes=["w_y", "w_st0"])
        P.op("dve", lambda e: e.tensor_scalar(out=st[:, 1:2], in0=st[:, 0:1], scalar1=1.0 / 64, scalar2=None, op0=ALU.mult),
             reads=["w_st0"], writes=["w_st1"])
        P.op("dve", lambda e: e.tensor_scalar(out=W["yc"][:], in0=W["y"][:], scalar1=st[:, 1:2], scalar2=None, op0=ALU.subtract),
             reads=["w_y", "w_st1"], writes=["w_yc"])
        yield
        P.op("act", lambda e: e.activation(out=W["junk"][:], in_=W["yc"][:], func=AF.Square, accum_out=st[:, 2:3]),
             reads=["w_yc"], writes=["w_junk", "w_st2"])
        P.op("act", lambda e: e.activation(out=st[:, 3:4], in_=st[:, 2:3], func=AF.Ln, scale=1.0 / 64, bias=64e-5),
             reads=["w_st2"], writes=["w_st3"])
        P.op("act", lambda e: e.activation(out=st[:, 3:4], in_=st[:, 3:4], func=AF.Exp, scale=-0.5), reads=["w_st3"],
             writes=["w_st3"])
        P.op("dve", lambda e: e.scalar_tensor_tensor(out=W["o1"][:], in0=W["yc"][:], scalar=st[:, 3:4], in1=bc[:, 64:128],
                                                     op0=ALU.mult, op1=ALU.mult), reads=["w_yc", "w_st3", "bc"], writes=["w_o1"])
        yield
        sl = mm(CH, 1, rw["rkb"][:, cs], ones_f[0:64, 0:1], ["rw_rkb", "ones_f"])
        P.op("act", lambda e, sl=sl: e.activation(out=st[:, 4:5], in_=wps[:, sl, 0:1], func=AF.Copy), reads=["wps_%d" % sl],
             writes=["w_st4"])
        P.op("dve", lambda e: e.scalar_tensor_tensor(out=W["o2"][:], in0=W["Vtm"][:], scalar=st[:, 4:5], in1=W["o1"][:],
                                                     op0=ALU.mult, op1=ALU.add), reads=["w_Vtm", "w_st4", "w_o1"], writes=["w_o2"])
        yield
        sl = mm(CH, 64, sgx[:, cs], L["g2h_bf"][:], ["sgx", "g2h_bf"])
        P.op("dve", lambda e, sl=sl: e.tensor_tensor(out=W["yo"][:], in0=wps[:, sl, 0:64], in1=W["o2"][:], op=ALU.mult),
             reads=["wps_%d" % sl, "w_o2"], writes=["w_yo"])
        P.dma("sp", y_rw[t0 + c * CH:t0 + (c + 1) * CH, :], W["yo"][:], reads=["w_yo"])
        yield
    P.op("dve", lambda e: e.tensor_copy(out=Sst[:, 0, :], in_=Sst[:, 4, :]), reads=["S4"], writes=["S0"])


DFF = 2816
NFC = 22


def build_C(TC):
    nc = bass.Bass("TRN2", target_bir_lowering=False)
    din = lambda n, shp, dt=F32: nc.dram_tensor(n, shp, dt, kind="ExternalInput").ap()
    A = dict(xsrc=din("xT", [D, TC]), ytm=din("ytm", [TC, 768]), wg=din("wg", [D, 3072]), pw=din("pw", [768, D]),
             wo=din("wo", [D, D]), wgu=din("wgu", [D, 2 * DFF]), wd=din("wd", [DFF, D]), g1=din("g1", [128, 8]),
             g2=din("g2", [128, 8]), iden=din("iden", [128, 128]))
    A["xdst"] = nc.dram_tensor("outT", [D, TC], F32, kind="ExternalOutput").ap()
    P = Prog(nc)
    with ExitStack() as es:
        emit_C(P, nc, es, TC, A, "")
    P.emit()
    return nc


def emit_C(P, nc, es, TC, A, pfx):
    NS = TC // 512
    xT, ytm, wg, pw, wo, wgu, wd, g1d, g2d, outT = (A[k] for k in ("xsrc", "ytm", "wg", "pw", "wo", "wgu", "wd", "g1", "g2", "xdst"))
    if True:
        def sb(name, shape, dt=F32):
            return es.enter_context(nc.sbuf_tensor("sb_" + pfx + name, shape, dt))

        def ps(name, shape, dt=F32):
            return es.enter_context(nc.psum_tensor("ps_" + pfx + name, shape, dt))
        g1 = sb("g1", [128, 8]); g2 = sb("g2", [128, 8])
        P.dma("sp", g1[:], g1d, writes=["g1"]); P.dma("sp", g2[:], g2d, writes=["g2"])
        ones_bf = sb("ones_bf", [128, 128], BF16)
        P.op("pool", lambda e: e.memset(ones_bf[:], 1.0), writes=["ones_bf"])
        xb = sb("xb", [128, 8, TC], BF16)
        big = sb("big", [128, NFC, TC], BF16)
        mb = big[:, 0:8, :]; ybf = big[:, 8:14, :]; hb = big
        rstd = [sb("rstd%d" % i, [128, TC]) for i in range(2)]
        macc = sb("macc", [128, NS, 512])
        stg = [sb("stg%d" % i, [128, 3072]) for i in range(2)]
        wcb = [sb("wcb%d" % i, [128, 3072], BF16) for i in range(2)]
        iden_f = sb("iden_f", [128, 128]); iden_bf = sb("iden_bf", [128, 128], BF16)
        P.dma("sp", iden_f[:], A["iden"], writes=["iden_f"])
        P.op("dve", lambda e: e.tensor_copy(out=iden_bf[:], in_=iden_f[:]), reads=["iden_f"], writes=["iden_bf"])
        xt = [sb("xt%d" % i, [128, 512]) for i in range(2)]
        sqb = [sb("sqb%d" % i, [128, 512], BF16) for i in range(2)]
        tA = [sb("tA%d" % i, [128, 512]) for i in range(2)]
        tB = [sb("tB%d" % i, [128, 512]) for i in range(2)]
        tC_ = [sb("tC%d" % i, [128, 512]) for i in range(2)]
        pbank = [ps("pbank%d" % i, [128, 512]) for i in range(4)]
        ssb = [ps("ssb%d" % i, [128, 512]) for i in range(NS)]
        cnt = {"pb": 0, "st": 0, "xt": 0, "t": 0}

        def nb():
            i = cnt["pb"] % 4
            cnt["pb"] += 1
            return pbank[i], "pbank%d" % i

        def wload(src_ap, shape3, gain=None, eng="dve"):
            i = cnt["st"] % 2
            cnt["st"] += 1
            a, b = shape3
            sv = stg[i][:, 0:a * b].rearrange("p (a b) -> p a b", b=b)
            wv = wcb[i][:, 0:a * b].rearrange("p (a b) -> p a b", b=b)
            P.dma("sp" if i == 0 else "pool", sv, src_ap, writes=["stg%d" % i])
            if gain is not None:
                P.op("dve", lambda e: e.tensor_tensor(out=wv, in0=sv, in1=gain[:].unsqueeze(2).to_broadcast([128, a, b]),
                                                      op=ALU.mult), reads=["stg%d" % i, "g1", "g2"], writes=["wcb%d" % i])
            elif eng == "act":
                P.op("act", lambda e: e.activation(out=wv, in_=sv, func=AF.Copy), reads=["stg%d" % i], writes=["wcb%d" % i])
            else:
                P.op("pool", lambda e: e.tensor_copy(out=wv, in_=sv), reads=["stg%d" % i], writes=["wcb%d" % i])
            return wv, "wcb%d" % i

        def rstd_from(ssk, s, dst):
            sl = slice(s * 512, (s + 1) * 512)
            P.op("act", lambda e: e.activation(out=dst[:, sl], in_=ssb[s][:], func=AF.Ln, scale=1.0 / D, bias=1e-6),
                 reads=["ssb%d" % s], writes=[ssk + "_%d" % s])
            P.op("act", lambda e: e.activation(out=dst[:, sl], in_=dst[:, sl], func=AF.Exp, scale=-0.5),
                 reads=[ssk + "_%d" % s], writes=[ssk + "_%d" % s])

        for s in range(NS):
            sl = slice(s * 512, (s + 1) * 512)
            for c in range(8):
                i = cnt["xt"] % 2
                cnt["xt"] += 1
                P.dma("sp", xt[i][:], xT[c * 128:(c + 1) * 128, sl], writes=["xt%d" % i])
                P.op("act", lambda e, i=i, c=c, sl=sl: e.activation(out=xb[:, c, sl], in_=xt[i][:], func=AF.Copy),
                     reads=["xt%d" % i], writes=["xb_%d_%d" % (c, s)])
                P.op("pool", lambda e, i=i: e.tensor_tensor(out=sqb[i][:], in0=xt[i][:], in1=xt[i][:], op=ALU.mult),
                     reads=["xt%d" % i], writes=["sqb%d" % i])
                P.op("pe", lambda e, i=i, c=c, s=s: e.matmul(ssb[s][:], lhsT=ones_bf[:], rhs=sqb[i][:], start=(c == 0),
                                                             stop=(c == 7)), reads=["ones_bf", "sqb%d" % i], writes=["ssb%d" % s])
            rstd_from("rstd0", s, rstd[0])
        for s in range(NS):
            i = cnt["st"] % 2
            cnt["st"] += 1
            yv = stg[i][:, 0:3072].rearrange("p (j f) -> p j f", f=768)
            yb = wcb[i][:, 0:3072].rearrange("p (j f) -> p j f", f=768)
            P.dma("sp", yv, ytm[s * 512:(s + 1) * 512, :].rearrange("(j p) f -> p j f", p=128), writes=["stg%d" % i])
            P.op("pool", lambda e, yv=yv, yb=yb: e.tensor_copy(out=yb, in_=yv), reads=["stg%d" % i], writes=["wcb%d" % i])
            for k in range(6):
                bank, bk = nb()
                for j in range(4):
                    P.op("pe", lambda e, bank=bank, j=j, k=k, yb=yb: e.matmul(bank[:, j * 128:(j + 1) * 128],
                                                                              lhsT=yb[:, j, k * 128:(k + 1) * 128], rhs=iden_bf[:],
                                                                              start=True, stop=True),
                         reads=["wcb%d" % i, "iden_bf"], writes=[bk])
                P.op("act", lambda e, bank=bank, k=k, s=s: e.activation(out=ybf[:, k, s * 512:(s + 1) * 512], in_=bank[:], func=AF.Copy),
                     reads=[bk], writes=["ybf%d_%d" % (k, s)])
        for dc in range(8):
            for br in range(3):
                col0 = br * 1024 + dc * 128
                wgb, wgk = wload(wg[:, col0:col0 + 128].rearrange("(c p) n -> p c n", p=128), (8, 128), gain=g1)
                pb, pk = wload(pw[br * 256:(br + 1) * 256, dc * 128:(dc + 1) * 128].rearrange("(c p) n -> p c n", p=128),
                               (2, 128), eng="act")
                for s in range(NS):
                    sl = slice(s * 512, (s + 1) * 512)
                    A, Ak = nb()
                    for c in range(8):
                        P.op("pe", lambda e, A=A, c=c, sl=sl, wgb=wgb: e.matmul(A[:], lhsT=wgb[:, c, :], rhs=xb[:, c, sl],
                                                                                 start=(c == 0), stop=(c == 7)),
                             reads=[wgk, "xb_%d_%d" % (c, s)], writes=[Ak])
                    B, Bk = nb()
                    for k in range(2):
                        P.op("pe", lambda e, B=B, k=k, sl=sl, pb=pb, br=br: e.matmul(B[:], lhsT=pb[:, k, :], rhs=ybf[:, br * 2 + k, sl],
                                                                                      start=(k == 0), stop=(k == 1)),
                             reads=[pk, "ybf%d_%d" % (br * 2 + k, s)], writes=[Bk])
                    i = cnt["t"] % 2
                    cnt["t"] += 1
                    P.op("dve", lambda e, A=A, i=i, sl=sl: e.tensor_tensor(out=tA[i][:], in0=A[:], in1=rstd[0][:, sl], op=ALU.mult),
                         reads=[Ak, "rstd0_%d" % s], writes=["tA%d" % i])
                    P.op("act", lambda e, i=i: e.activation(out=tA[i][:], in_=tA[i][:], func=AF.Sigmoid), reads=["tA%d" % i],
                         writes=["tA%d" % i])
                    if br == 0:
                        P.op("dve", lambda e, B=B, i=i, s=s: e.tensor_tensor(out=macc[:, s, :], in0=B[:], in1=tA[i][:], op=ALU.mult),
                             reads=[Bk, "tA%d" % i], writes=["macc%d" % s])
                    else:
                        P.op("dve", lambda e, B=B, i=i: e.tensor_tensor(out=tB[i][:], in0=B[:], in1=tA[i][:], op=ALU.mult),
                             reads=[Bk, "tA%d" % i], writes=["tB%d" % i])
                        if br == 1:
                            P.op("pool", lambda e, i=i, s=s: e.tensor_tensor(out=macc[:, s, :], in0=macc[:, s, :], in1=tB[i][:], op=ALU.add),
                                 reads=["tB%d" % i, "macc%d" % s], writes=["macc%d" % s])
                        else:
                            P.op("pool", lambda e, i=i, s=s, dc=dc, sl=sl: e.tensor_tensor(out=mb[:, dc, sl], in0=macc[:, s, :], in1=tB[i][:],
                                                                                           op=ALU.add),
                                 reads=["tB%d" % i, "macc%d" % s], writes=["mb_%d_%d" % (dc, s)])
        for dc in range(8):
            wob, wok = wload(wo[:, dc * 128:(dc + 1) * 128].rearrange("(c p) n -> p c n", p=128), (8, 128), eng="act")
            for s in range(NS):
                sl = slice(s * 512, (s + 1) * 512)
                A, Ak = nb()
                for c in range(8):
                    P.op("pe", lambda e, A=A, c=c, sl=sl, wob=wob: e.matmul(A[:], lhsT=wob[:, c, :], rhs=mb[:, c, sl], start=(c == 0),
                                                                             stop=(c == 7)), reads=[wok, "mb_%d_%d" % (c, s)], writes=[Ak])
                i = cnt["xt"] % 2
                cnt["xt"] += 1
                P.dma("sp", xt[i][:], xT[dc * 128:(dc + 1) * 128, sl], writes=["xt%d" % i])
                j = cnt["t"] % 2
                cnt["t"] += 1
                P.op("dve", lambda e, A=A, i=i, j=j: e.tensor_tensor(out=tC_[j][:], in0=A[:], in1=xt[i][:], op=ALU.add),
                     reads=[Ak, "xt%d" % i], writes=["tC%d" % j])
                P.dma("pool", outT[dc * 128:(dc + 1) * 128, sl], tC_[j][:], reads=["tC%d" % j], writes=["o_%d_%d" % (dc, s)])
                P.op("act", lambda e, j=j, dc=dc, sl=sl: e.activation(out=xb[:, dc, sl], in_=tC_[j][:], func=AF.Copy),
                     reads=["tC%d" % j], writes=["xb_%d_%d" % (dc, s)])
                P.op("pool", lambda e, j=j: e.tensor_tensor(out=sqb[j][:], in0=tC_[j][:], in1=tC_[j][:], op=ALU.mult),
                     reads=["tC%d" % j], writes=["sqb%d" % j])
                P.op("pe", lambda e, j=j, s=s, dc=dc: e.matmul(ssb[s][:], lhsT=ones_bf[:], rhs=sqb[j][:], start=(dc == 0),
                                                               stop=(dc == 7)), reads=["ones_bf", "sqb%d" % j], writes=["ssb%d" % s])
        for s in range(NS):
            rstd_from("rstd1", s, rstd[1])
        allmb = ["mb_%d_%d" % (dc, s) for dc in range(8) for s in range(NS)] + ["ybf%d_%d" % (k, s) for k in range(6) for s in range(NS)]
        for fc in range(NFC):
            i0 = cnt["st"] % 2
            cnt["st"] += 1
            sv = stg[i0][:, 0:2048].rearrange("p (a b) -> p a b", b=256)
            wv = wcb[i0][:, 0:2048].rearrange("p (a b) -> p a b", b=256)
            P.dma("sp", sv[:, :, 0:128], wgu[:, fc * 128:(fc + 1) * 128].rearrange("(c p) n -> p c n", p=128), writes=["stg%d" % i0])
            P.dma("pool", sv[:, :, 128:256], wgu[:, DFF + fc * 128:DFF + (fc + 1) * 128].rearrange("(c p) n -> p c n", p=128),
                  reads=["stg%d" % i0], writes=["stgx%d_1" % i0])
            P.op("dve", lambda e, sv=sv, wv=wv: e.tensor_tensor(out=wv, in0=sv, in1=g2[:].unsqueeze(2).to_broadcast([128, 8, 256]),
                                                                op=ALU.mult),
                 reads=["stg%d" % i0, "stgx%d_1" % i0, "g2"], writes=["wcb%d" % i0, "stg%d" % i0])
            wk = "wcb%d" % i0
            for s in range(NS):
                sl = slice(s * 512, (s + 1) * 512)
                A, Ak = nb()
                for c in range(8):
                    P.op("pe", lambda e, A=A, c=c, sl=sl, wv=wv: e.matmul(A[:], lhsT=wv[:, c, 0:128], rhs=xb[:, c, sl], start=(c == 0),
                                                                           stop=(c == 7)), reads=[wk, "xb_%d_%d" % (c, s)], writes=[Ak])
                B, Bk = nb()
                for c in range(8):
                    P.op("pe", lambda e, B=B, c=c, sl=sl, wv=wv: e.matmul(B[:], lhsT=wv[:, c, 128:256], rhs=xb[:, c, sl], start=(c == 0),
                                                                           stop=(c == 7)), reads=[wk, "xb_%d_%d" % (c, s)], writes=[Bk])
                i = cnt["t"] % 2
                cnt["t"] += 1
                P.op("dve", lambda e, A=A, i=i, sl=sl: e.tensor_tensor(out=tA[i][:], in0=A[:], in1=rstd[1][:, sl], op=ALU.mult),
                     reads=[Ak, "rstd1_%d" % s], writes=["tA%d" % i])
                P.op("act", lambda e, i=i: e.activation(out=tA[i][:], in_=tA[i][:], func=AF.Silu), reads=["tA%d" % i], writes=["tA%d" % i])
                P.op("dve", lambda e, B=B, i=i, sl=sl: e.tensor_tensor(out=tB[i][:], in0=B[:], in1=rstd[1][:, sl], op=ALU.mult),
                     reads=[Bk, "rstd1_%d" % s], writes=["tB%d" % i])
                P.op("pool", lambda e, i=i, fc=fc, sl=sl: e.tensor_tensor(out=hb[:, fc, sl], in0=tA[i][:], in1=tB[i][:], op=ALU.mult),
                     reads=["tA%d" % i, "tB%d" % i], writes=["hb_%d_%d" % (fc, s)] + (allmb if fc < 14 else []))
        for dc in range(8):
            wdb, wdk = wload(wd[:, dc * 128:(dc + 1) * 128].rearrange("(c p) n -> p c n", p=128), (NFC, 128), eng="act")
            for s in range(NS):
                sl = slice(s * 512, (s + 1) * 512)
                A, Ak = nb()
                for fc in range(NFC):
                    P.op("pe", lambda e, A=A, fc=fc, sl=sl, wdb=wdb: e.matmul(A[:], lhsT=wdb[:, fc, :], rhs=hb[:, fc, sl], start=(fc == 0),
                                                                               stop=(fc == NFC - 1)),
                         reads=[wdk, "hb_%d_%d" % (fc, s)], writes=[Ak])
                i = cnt["xt"] % 2
                cnt["xt"] += 1
                P.dma("sp", xt[i][:], outT[dc * 128:(dc + 1) * 128, sl], reads=["o_%d_%d" % (dc, s)], writes=["xt%d" % i])
                j = cnt["t"] % 2
                cnt["t"] += 1
                P.op("dve", lambda e, A=A, i=i, j=j: e.tensor_tensor(out=tC_[j][:], in0=A[:], in1=xt[i][:], op=ALU.add),
                     reads=[Ak, "xt%d" % i], writes=["tC%d" % j])
                P.dma("pool", outT[dc * 128:(dc + 1) * 128, sl], tC_[j][:], reads=["tC%d" % j], writes=["o_%d_%d" % (dc, s)])


def build_fused(S):
    TCQ = S // 4
    nc = bass.Bass("TRN2", target_bir_lowering=False)
    din = lambda n, shp, dt=F32: nc.dram_tensor(n, shp, dt, kind="ExternalInput").ap()
    xT = din("xT", [D, S])
    wf = din("wf", [2, 4, D, NF]); wt = din("wt", [2, 4, D, NTM]); gmix = din("gmix", [2, 128, 8])
    pp = din("pp", [2, 4, 128, NPP]); bc = din("bc", [2, 4, 128, NBC]); rope = din("rope", [64, 2, S])
    dmat = din("dmat", [4, 128, 128]); w2h = din("w2h", [2, 4, 64, 64]); a2h = din("a2h", [2, 4, 64, 64])
    g2h = din("g2h", [2, 4, 128, 64]); masks = din("masks", [128, 3, 128]); iden = din("iden", [128, 128])
    wg = din("wg", [2, D, 3072]); pw = din("pw", [2, 768, D]); wo = din("wo", [2, D, D])
    wgu = din("wgu", [2, D, 2 * DFF]); wd = din("wd", [2, DFF, D]); g2n = din("g2n", [2, 128, 8])
    outT = nc.dram_tensor("outT", [D, S], F32, kind="ExternalOutput").ap()
    xs1 = nc.dram_tensor("xs1", [D, S], F32).ap()
    ytm = nc.dram_tensor("ytm", [S, 768], F32).ap()
    P = Prog(nc)
    for l in range(2):
        xsrc = xT if l == 0 else xs1
        xdst = xs1 if l == 0 else outT
        for h in range(4):
            A = dict(xT=xsrc, wf=wf[l, h], wt=wt[l, h], gmix=gmix[l], pp=pp[l, h], bc=bc[l, h], rope=rope, dmat=dmat[h],
                     w2h=w2h[l, h], a2h=a2h[l, h], g2h=g2h[l, h], masks=masks, iden=iden,
                     y_ret=ytm[:, h * 64:(h + 1) * 64], y_fox=ytm[:, 256 + h * 64:256 + (h + 1) * 64],
                     y_rw=ytm[:, 512 + h * 64:512 + (h + 1) * 64])
            with ExitStack() as es:
                emit_B(P, nc, es, S, A, "L%dH%d_" % (l, h))
            P.barrier()
        for q in range(4):
            sl = slice(q * TCQ, (q + 1) * TCQ)
            A = dict(xsrc=xsrc[:, sl], ytm=ytm[sl, :], wg=wg[l], pw=pw[l], wo=wo[l], wgu=wgu[l], wd=wd[l], g1=gmix[l],
                     g2=g2n[l], iden=iden, xdst=xdst[:, sl])
            with ExitStack() as es:
                emit_C(P, nc, es, TCQ, A, "L%dQ%d_" % (l, q))
            P.barrier()
    P.emit()
    return nc


RET0, FOX0, RW0, GATE0 = 0, 1024, 1796, 2820

def consts_B(h, S):
    half = 32
    inv_freq = (np.float32(10000.0) ** (-np.arange(half, dtype=np.float32) / np.float32(half))).astype(np.float32)
    pos = np.arange(S, dtype=np.float32)
    ang = (pos[None, :] * inv_freq[:, None]).astype(np.float32)
    cos = np.cos(ang).astype(np.float32); sin = np.sin(ang).astype(np.float32)
    rope = np.zeros((64, 2, S), np.float32)
    rope[0:32, 0] = cos; rope[32:64, 0] = cos
    rope[0:32, 1] = -sin; rope[32:64, 1] = sin
    lg = np.log1p(-np.exp2(np.float32(-5.0 - h))).astype(np.float32)
    idx = np.arange(128, dtype=np.float32)
    dist = idx[None, :] - idx[:, None]
    dmatT = np.where(dist >= 0, np.exp(lg * np.maximum(dist, 0)), 0).astype(np.float32)
    kdec = np.exp(lg * (127 - idx)).astype(np.float32)
    qdec = np.exp(lg * (idx + 1)).astype(np.float32)
    cdec = np.exp(lg * 128).astype(np.float32)
    ii = np.arange(128)
    masks = np.zeros((128, 3, 128), np.float32)
    masks[:, 0, :] = (ii[None, :] > ii[:, None])
    masks[:, 1, :] = (ii[None, :] < ii[:, None])
    masks[:, 2, :] = (ii[None, :] >= ii[:, None])
    iden = np.eye(128, dtype=np.float32)
    return dict(rope=rope, dmat=dmatT, masks=masks, iden=iden), kdec, qdec, cdec

def prep_B(inp, l, b, h, S, x=None):
    cst, kdec, qdec, cdec = consts_B(h, S)
    w_in = inp["w_in"][l]
    hs = slice(h * 64, (h + 1) * 64)
    def col(base, blk):
        return w_in[:, base + blk * 256 + h * 64: base + blk * 256 + (h + 1) * 64]
    def swp(a):
        return np.concatenate([a[:, 32:64], a[:, 0:32]], axis=1)
    rq, rk, rv, rg = (col(RET0, i) for i in range(4))
    fq, fk, fv = (col(FOX0, i) for i in range(3))
    ff = w_in[:, FOX0 + 768 + h: FOX0 + 768 + h + 1]
    wr, wk, wv = (col(RW0, i) for i in range(3))
    wxw = w_in[:, RW0 + 768: RW0 + 832]; wxa = w_in[:, RW0 + 832: RW0 + 896]; wxg = w_in[:, RW0 + 896: RW0 + 1024]
    wf = np.concatenate([rq, rk, swp(rq), swp(rk), fq, fk, ff, wr, wk, wv, wxw, wxa, wxg], axis=1)
    wt = np.concatenate([rv, fv, rg], axis=1)
    assert wf.shape[1] == NF
    mu = inp["rwkv_mu"][l]
    pp = np.zeros((128, NPP), np.float32)
    def setc(n, v):
        v = np.asarray(v, np.float32).reshape(-1)
        if v.size == 1:
            pp[:, PP[n]] = v[0]
        else:
            pp[:v.size, PP[n]] = v
    setc("fqg", inp["fox_q_norm"][l]); setc("fkg", inp["fox_k_norm"][l]); setc("fb", inp["fox_f_bias"][l][h])
    setc("mu_r", mu[0 * 256 + h * 64: 0 * 256 + (h + 1) * 64]); setc("mu_k", mu[256 + h * 64: 256 + (h + 1) * 64])
    setc("mu_v", mu[512 + h * 64: 512 + (h + 1) * 64]); setc("mu_xw", mu[768:832]); setc("mu_xa", mu[832:896])
    setc("mu_xg", mu[896:1024])
    for n, k in (("w0", "rwkv_w0"), ("a0", "rwkv_a0"), ("k_k", "rwkv_k_k"), ("k_a", "rwkv_k_a"), ("r_k", "rwkv_r_k")):
        setc(n, inp[k][l][hs])
    setc("cdec", cdec); setc("kdec", kdec)
    bc = np.zeros((128, NBC), np.float32)
    bc[:, 0:64] = inp["ret_gn"][l][hs][None, :]
    bc[:, 64:128] = inp["rwkv_gn"][l][hs][None, :]
    bc[:, 128:256] = qdec[None, :]
    xx = inp["x"][b] if x is None else x
    m = dict(xT=np.ascontiguousarray(xx[:S].T), wf=np.ascontiguousarray(wf), wt=np.ascontiguousarray(wt),
             gmix=np.ascontiguousarray(inp["mix_norm"][l].reshape(8, 128).T), pp=pp, bc=bc,
             w2h=np.ascontiguousarray(inp["rwkv_w2"][l][:, hs]), a2h=np.ascontiguousarray(inp["rwkv_a2"][l][:, hs]),
             g2h=np.ascontiguousarray(inp["rwkv_g2"][l][:, hs]))
    m.update(cst)
    return m


def prep_fused(inp, b, S):
    per = [[prep_B(inp, l, b, h, S) for h in range(4)] for l in range(2)]
    st = lambda k: np.ascontiguousarray(np.stack([np.stack([per[l][h][k] for h in range(4)]) for l in range(2)]))
    m = dict(xT=per[0][0]["xT"], wf=st("wf"), wt=st("wt"), pp=st("pp"), bc=st("bc"), w2h=st("w2h"), a2h=st("a2h"), g2h=st("g2h"),
             gmix=np.ascontiguousarray(np.stack([per[l][0]["gmix"] for l in range(2)])),
             rope=per[0][0]["rope"], masks=per[0][0]["masks"], iden=per[0][0]["iden"],
             dmat=np.ascontiguousarray(np.stack([per[0][h]["dmat"] for h in range(4)])),
             wg=np.ascontiguousarray(inp["w_in"][:, :, 2820:]),
             pw=np.ascontiguousarray(np.concatenate([inp["p_ret"], inp["p_fox"], inp["p_rwkv"]], axis=1)),
             wo=np.ascontiguousarray(inp["w_out"]), wgu=np.ascontiguousarray(inp["w_gate_up"]), wd=np.ascontiguousarray(inp["w_down"]),
             g2n=np.ascontiguousarray(np.stack([inp["ffn_norm"][l].reshape(8, 128).T for l in range(2)])))
    return m


_NC = {}


def kernel(**inputs):
    inp = {k: np.asarray(v, dtype=np.float32) for k, v in inputs.items()}
    x = inp["x"]
    Bn, S, Dm = x.shape
    if "F" not in _NC:
        _NC["F"] = build_fused(S)
    maps = [prep_fused(inp, b, S) for b in range(Bn)]
    res = run_bass_kernel_spmd(_NC["F"], maps, core_ids=list(range(Bn))).results
    out = np.stack([np.ascontiguousarray(res[b]["outT"].T) for b in range(Bn)], axis=0)
    return np.ascontiguousarray(out, dtype=np.float32)
```

```python
import heapq
import numpy as np
import concourse.bass as bass
import concourse.mybir as mybir
from concourse.bass_utils import run_bass_kernel_spmd

F32 = mybir.dt.float32
BF16 = mybir.dt.bfloat16
AF = mybir.ActivationFunctionType
ALU = mybir.AluOpType
AX = mybir.AxisListType


class Prog:
    ENGS = ("pe", "act", "dve", "pool", "sp")
    NDMA = 8
    DUR = {"pe": 0.30, "act": 0.35, "dve": 0.30, "pool": 0.40}
    DMA_ISSUE = 0.10
    DMA_LAT = 2.5
    HOP = 0.15
    REORDER = True

    def __init__(self, nc):
        self.nc = nc
        self.allops = []
        self.state = {}
        self.markers = {e: None for e in self.ENGS}
        self.bar_start = 0

    @classmethod
    def bank_of(cls, key):
        if key.startswith("rps"):
            return "B_rps"
        if key.startswith("wps"):
            return "B_wps"
        if key.startswith("misc"):
            return "B_misc"
        if key.startswith("facc"):
            return "B_facc"
        if key.startswith("fsc") or key.startswith("pbank") or key.startswith("ssb"):
            return "B_" + key
        return None

    def _add(self, eng, fn, reads, writes, dma, dur):
        banks = []
        for k in tuple(reads) + tuple(writes):
            bk = self.bank_of(k)
            if bk is not None and bk not in banks:
                banks.append(bk)
        writes = tuple(writes) + tuple(banks)
        deps = []
        for b in reads:
            st = self.state.setdefault(b, [None, []])
            if st[0] is not None:
                deps.append(st[0])
        for b in writes:
            st = self.state.setdefault(b, [None, []])
            if st[0] is not None:
                deps.append(st[0])
            deps.extend(st[1])
        hard, soft, seen = [], [], set()
        for d in deps:
            if id(d) in seen:
                continue
            seen.add(id(d))
            if eng == "pe" and d["eng"] == "pe" and not d["dma"]:
                soft.append(d)
            else:
                hard.append(d)
        m = self.markers[eng]
        if m is not None:
            soft.append(m)
        if dur is None:
            dur = self.DMA_LAT if dma else self.DUR[eng]
        op = {"eng": eng, "fn": fn, "hard": hard, "soft": soft, "dma": dma, "dur": dur, "seq": len(self.allops),
              "marker": False}
        self.allops.append(op)
        for b in reads:
            self.state[b][1].append(op)
        for b in writes:
            self.state[b] = [op, []]
        return op

    def op(self, eng, fn, reads=(), writes=(), dur=None):
        return self._add(eng, fn, tuple(reads), tuple(writes), False, dur)

    def dma(self, eng, out, in_, reads=(), writes=(), dur=None):
        return self._add(eng, lambda e: e.dma_start(out=out, in_=in_), tuple(reads), tuple(writes), True, dur)

    def barrier(self):
        prev = self.allops[self.bar_start:]
        start = len(self.allops)
        for e in self.ENGS:
            mk = {"eng": e, "fn": None, "hard": list(prev), "soft": [], "dma": False, "dur": 0.0,
                  "seq": len(self.allops), "marker": True}
            if self.markers[e] is not None:
                mk["soft"].append(self.markers[e])
            self.allops.append(mk)
            self.markers[e] = mk
        self.bar_start = start
        self.state = {}

    def _schedule(self):
        ops = self.allops
        n = len(ops)
        if not self.REORDER:
            return list(ops)
        succ = [[] for _ in range(n)]
        indeg = [0] * n
        for op in ops:
            for d in op["hard"]:
                succ[d["seq"]].append(op["seq"])
            for d in op["soft"]:
                succ[d["seq"]].append(op["seq"])
            indeg[op["seq"]] = len(op["hard"]) + len(op["soft"])
        prio = [0.0] * n
        for i in range(n - 1, -1, -1):
            best = 0.0
            for s in succ[i]:
                if prio[s] > best:
                    best = prio[s]
            prio[i] = best + ops[i]["dur"] + self.HOP
        ready = [0.0] * n
        finish = [0.0] * n
        free_at = {e: 0.0 for e in self.ENGS}
        future = {e: [] for e in self.ENGS}
        avail = {e: [] for e in self.ENGS}
        for op in ops:
            if indeg[op["seq"]] == 0:
                heapq.heappush(future[op["eng"]], (0.0, op["seq"]))
        order = []
        remaining = n
        while remaining:
            best = None
            for e in self.ENGS:
                fu, av = future[e], avail[e]
                t = free_at[e]
                while fu and fu[0][0] <= t:
                    r, i = heapq.heappop(fu)
                    heapq.heappush(av, (-prio[i], i))
                if av:
                    cand = (t, av[0][0], e, True)
                elif fu:
                    cand = (fu[0][0], -prio[fu[0][1]], e, False)
                else:
                    continue
                if best is None or cand[:2] < best[:2]:
                    best = cand
            start, _, e, from_av = best
            if from_av:
                _, i = heapq.heappop(avail[e])
            else:
                _, i = heapq.heappop(future[e])
            op = ops[i]
            if op["dma"]:
                free_at[e] = start + self.DMA_ISSUE
                finish[i] = start + op["dur"]
            else:
                finish[i] = start + op["dur"]
                free_at[e] = finish[i]
            order.append(op)
            remaining -= 1
            for s in succ[i]:
                so = ops[s]
                hop = self.HOP if (so["eng"] != e or op["dma"]) else 0.05
                r = finish[i] + hop
                if r > ready[s]:
                    ready[s] = r
                indeg[s] -= 1
                if indeg[s] == 0:
                    heapq.heappush(future[so["eng"]], (ready[s], s))
        self.est_time = max(finish) if finish else 0.0
        return order

    def emit(self):
        nc = self.nc
        order = self._schedule()
        per_eng = {e: [] for e in self.ENGS}
        for op in order:
            per_eng[op["eng"]].append(op)
        dma_uses = {}
        for e in self.ENGS:
            ncomp = 0
            ndma = 0
            last = {}
            for op in per_eng[e]:
                if op["marker"]:
                    op["sig"] = None
                elif op["dma"]:
                    slot = ndma % self.NDMA
                    ndma += 1
                    if slot in last:
                        op["hard"].append(last[slot])
                    last[slot] = op
                    dma_uses[(e, slot)] = dma_uses.get((e, slot), 0) + 1
                    op["sig"] = (("dma", e, slot), 16 * dma_uses[(e, slot)])
                else:
                    ncomp += 1
                    op["sig"] = (("eng", e), ncomp)
        semkeys = [("eng", e) for e in self.ENGS] + [("dma", e, s) for (e, s) in sorted(dma_uses)]
        for op in order:
            c = {}
            if op["marker"]:
                for d in op["hard"]:
                    if d["marker"]:
                        for k, v in d["clock"].items():
                            if c.get(k, 0) < v:
                                c[k] = v
                    else:
                        k, v = d["sig"]
                        if c.get(k, 0) < v:
                            c[k] = v
                op["clock"] = c
                continue
            for d in op["hard"]:
                for k, v in d["clock"].items():
                    if c.get(k, 0) < v:
                        c[k] = v
            for d in op["soft"]:
                if d["marker"]:
                    for k, v in d["clock"].items():
                        if c.get(k, 0) < v:
                            c[k] = v
            if op["sig"] is not None:
                k, v = op["sig"]
                c[k] = max(c.get(k, 0), v)
            op["clock"] = c
        from contextlib import ExitStack
        with ExitStack() as es:
            sems = {}
            for k in semkeys:
                sems[k] = es.enter_context(nc.semaphore("s_" + "_".join(str(x) for x in k)))
            block = es.enter_context(nc.Block())

            def run_engine(e, engobj):
                know = {}
                for op in per_eng[e]:
                    need = {}
                    for d in op["hard"]:
                        if d["sig"] is None:
                            continue
                        k, v = d["sig"]
                        if know.get(k, 0) >= v:
                            continue
                        if need.get(k, 0) < v:
                            need[k] = v
                    for k, v in need.items():
                        engobj.wait_ge(sems[k], v)
                    if op["marker"]:
                        for k, v in op["clock"].items():
                            if know.get(k, 0) < v:
                                know[k] = v
                        continue
                    for d in op["hard"]:
                        for k, v in d["clock"].items():
                            if know.get(k, 0) < v:
                                know[k] = v
                    for d in op["soft"]:
                        if d["marker"]:
                            for k, v in d["clock"].items():
                                if know.get(k, 0) < v:
                                    know[k] = v
                    if op["marker"]:
                        continue
                    ins = op["fn"](engobj)
                    k, v = op["sig"]
                    ins.then_inc(sems[k], 16 if op["dma"] else 1)
                for (ee, s), cnt in dma_uses.items():
                    if ee == e:
                        engobj.wait_ge(sems[("dma", e, s)], 16 * cnt)

            if per_eng["sp"]:
                @block.sync
                def _(eng):
                    run_engine("sp", eng)
            if per_eng["pe"]:
                @block.tensor
                def _(eng):
                    run_engine("pe", eng)
            if per_eng["act"]:
                @block.scalar
                def _(eng):
                    run_engine("act", eng)
            if per_eng["dve"]:
                @block.vector
                def _(eng):
                    run_engine("dve", eng)
            if per_eng["pool"]:
                @block.gpsimd
                def _(eng):
                    run_engine("pool", eng)

from contextlib import ExitStack


D = 1024
HD = 64
TT = 512
RW_RATE = 4
FG = [("rq", 64), ("rk", 64), ("rqs", 64), ("rks", 64), ("fq", 64), ("fk", 64), ("ff", 1),
      ("wr", 64), ("wk", 64), ("wv", 64), ("wxw", 64), ("wxa", 64), ("wxg", 128)]
FOFF = {}
_o = 0
for _n, _w in FG:
    FOFF[_n] = (_o, _w)
    _o += _w
NF = _o
NTM = 192
PPN = ["fqg", "fkg", "fb", "mu_r", "mu_k", "mu_v", "mu_xw", "mu_xa", "mu_xg", "w0", "a0", "k_k", "k_a",
       "r_k", "cdec", "kdec"]
PP = {n: i for i, n in enumerate(PPN)}
NPP = len(PPN)
NBC = 256


def build_B(S, dbg=False):
    NT = S // TT
    nc = bass.Bass("TRN2", target_bir_lowering=False)
    din = lambda n, shp, dt=F32: nc.dram_tensor(n, shp, dt, kind="ExternalInput").ap()
    dout = lambda n, shp, dt=F32: nc.dram_tensor(n, shp, dt, kind="ExternalOutput").ap()
    xT = din("xT", [D, S])
    wf = din("wf", [D, NF])
    wt = din("wt", [D, NTM])
    gmix = din("gmix", [128, 8])
    pp_d = din("pp", [128, NPP])
    bc_d = din("bc", [128, NBC])
    rope_d = din("rope", [64, 2, S])
    dmat_d = din("dmat", [128, 128])
    w2_d = din("w2h", [64, 64])
    a2_d = din("a2h", [64, 64])
    g2_d = din("g2h", [128, 64])
    msk_d = din("masks", [128, 3, 128])
    iden_d = din("iden", [128, 128])
    y_ret = dout("y_ret", [S, 64])
    y_fox = dout("y_fox", [S, 64])
    y_rw = dout("y_rw", [S, 64])

    A = dict(xT=xT, wf=wf, wt=wt, gmix=gmix, pp=pp_d, bc=bc_d, rope=rope_d, dmat=dmat_d, w2h=w2_d, a2h=a2_d, g2h=g2_d,
             masks=msk_d, iden=iden_d, y_ret=y_ret, y_fox=y_fox, y_rw=y_rw)
    P = Prog(nc)
    with ExitStack() as es:
        emit_B(P, nc, es, S, A, "")
    P.emit()
    return nc


def emit_B(P, nc, es, S, A, pfx):
    NT = S // TT
    xT, wf, wt, gmix, pp_d, bc_d, rope_d, dmat_d = (A[k] for k in ("xT", "wf", "wt", "gmix", "pp", "bc", "rope", "dmat"))
    w2_d, a2_d, g2_d, msk_d, iden_d, y_ret, y_fox, y_rw = (A[k] for k in ("w2h", "a2h", "g2h", "masks", "iden", "y_ret", "y_fox", "y_rw"))
    if True:
        def sb(name, shape, dt=F32):
            return es.enter_context(nc.sbuf_tensor("sb_" + pfx + name, shape, dt))

        def ps(name, shape, dt=F32):
            return es.enter_context(nc.psum_tensor("ps_" + pfx + name, shape, dt))

        pp = sb("pp", [128, NPP]); bc = sb("bc", [128, NBC]); gm = sb("gm", [128, 8])
        dmat = sb("dmat", [128, 128]); w2h = sb("w2h", [64, 64]); a2h = sb("a2h", [64, 64])
        g2h = sb("g2h", [128, 64]); msk = sb("msk", [128, 3, 128]); iden = sb("iden", [128, 128])
        for t, d, k in ((pp, pp_d, "pp"), (bc, bc_d, "bc"), (gm, gmix, "gm"), (dmat, dmat_d, "dmat"),
                        (w2h, w2_d, "w2h"), (a2h, a2_d, "a2h"), (g2h, g2_d, "g2h"), (msk, msk_d, "msk"),
                        (iden, iden_d, "iden")):
            P.dma("sp", t[:], d, writes=[k])
        ppc = lambda n, np_=64: pp[0:np_, PP[n]:PP[n] + 1]
        iden_bf = sb("iden_bf", [128, 128], BF16)
        P.op("dve", lambda e: e.tensor_copy(out=iden_bf[:], in_=iden[:]), reads=["iden"], writes=["iden_bf"])
        mski_bf = sb("mski_bf", [128, 128], BF16)
        P.op("dve", lambda e: e.tensor_copy(out=mski_bf[:], in_=msk[:, 2, :]), reads=["msk"], writes=["mski_bf"])
        ones_bf = sb("ones_bf", [128, 128], BF16)
        P.op("pool", lambda e: e.memset(ones_bf[:], 1.0), writes=["ones_bf"])
        mask_one = sb("mask_one", [1, TT])
        P.op("pool", lambda e: e.memset(mask_one[:], 1.0), writes=["mask_one"])
        ones_f = sb("ones_f", [128, 64])
        P.op("pool", lambda e: e.memset(ones_f[:], 1.0), writes=["ones_f"])
        mask64 = sb("mask64", [64, TT])
        P.op("pool", lambda e: e.memset(mask64[:], 1.0), writes=["mask64"])
        P.op("pool", lambda e: e.memset(mask64[:].rearrange("p (c j) -> p c j", j=128)[:, :, 0:1], 0.0),
             writes=["mask64"])
        der = sb("der", [128, 4])
        P.op("dve", lambda e: e.tensor_scalar(out=der[0:64, 0:1], in0=ppc("fqg"), scalar1=0.125, scalar2=None,
                                              op0=ALU.mult), reads=["pp"], writes=["der0"])
        P.op("dve", lambda e: e.tensor_scalar(out=der[0:64, 1:2], in0=ppc("k_a"), scalar1=-1.0, scalar2=1.0,
                                              op0=ALU.mult, op1=ALU.add), reads=["pp"], writes=["der1"])
        P.op("dve", lambda e: e.tensor_scalar(out=der[0:1, 2:3], in0=pp[0:1, PP["fb"]:PP["fb"] + 1], scalar1=-1.0,
                                              scalar2=None, op0=ALU.mult), reads=["pp"], writes=["der2"])

        wfb = sb("wfb", [128, 8, NF], BF16)
        wtb = sb("wtb", [128, 8, NTM], BF16)
        xs1 = sb("xs", [128, 8, TT])
        wst = [xs1[:].rearrange("p c t -> p (c t)")[:, i * 2048:i * 2048 + NF + NTM] for i in range(2)]
        for c in range(8):
            st = wst[c % 2]
            k = "wst%d" % (c % 2)
            P.dma("sp", st[:, 0:NF], wf[c * 128:(c + 1) * 128, :], reads=["xsall"], writes=[k + "a"])
            P.dma("sp", st[:, NF:NF + NTM], wt[c * 128:(c + 1) * 128, :], reads=["xsall"], writes=[k + "b"])
            P.op("dve", lambda e, st=st, c=c: e.tensor_scalar(out=wfb[:, c, :], in0=st[:, 0:NF], scalar1=gm[:, c:c + 1],
                                                              scalar2=None, op0=ALU.mult),
                 reads=[k + "a", "gm"], writes=["wfb"])
            P.op("act", lambda e, st=st, c=c: e.activation(out=wtb[:, c, :], in_=st[:, NF:NF + NTM], func=AF.Copy,
                                                           scale=gm[:, c:c + 1]),
                 reads=[k + "b", "gm"], writes=["wtb"])
        for g in ("rk", "rks"):
            o, w = FOFF[g]
            P.op("dve", lambda e, o=o, w=w: e.tensor_scalar(out=wfb[:, :, o:o + w], in0=wfb[:, :, o:o + w], scalar1=0.125,
                                                            scalar2=None, op0=ALU.mult), reads=["wfb"], writes=["wfb"])

        xs = [xs1, xs1]
        xb = sb("xb", [128, 8, TT], BF16)
        sqr = [sb("sq%d" % i, [128, TT], BF16) for i in range(2)]
        rstd = sb("rstd", [128, TT])
        rstd_tm = sb("rstd_tm", [128, 4])
        gbuf = {}
        for n, w in FG:
            gbuf[n] = sb("g_" + n, [w, TT + 1])
        tm = sb("tm", [128, 4, NTM])
        ropet = sb("ropet", [64, 2, TT])
        pbank = [ps("pbank%d" % i, [128, 512]) for i in range(2)]
        misc = ps("misc", [128, 512])
        fsc = [ps("fsc%d" % i, [128, 512]) for i in range(2)]
        facc = ps("facc", [128, 4, 128])
        rps = ps("rps", [128, 512])
        wps = ps("wps", [128, 4, 128])

        Rst = sb("Rst", [64, 64]); Rbf = sb("Rbf", [64, 64], BF16)
        P.op("pool", lambda e: e.memset(Rst[:], 0.0), writes=["Rst"])
        P.op("pool", lambda e: e.memset(Rbf[:], 0.0), writes=["Rbf"])
        KT = sb("KT", [70, S], BF16)
        QT = sb("QT", [70, TT], BF16)
        VA = sb("VA", [128, S // 128, 65], BF16)
        P.op("pool", lambda e: e.memset(KT[64:70, :], 1.0), writes=["KTaug"])
        P.op("pool", lambda e: e.memset(QT[64:70, :], -1.0), writes=["QTaug"])
        P.op("pool", lambda e: e.memset(VA[:, :, 64:65], 1.0), writes=["VAone"])
        ccar = sb("ccar", [1, 1])
        P.op("pool", lambda e: e.memset(ccar[:], 0.0), writes=["ccar"])
        Sst = sb("Sst", [64, 5, 64])
        P.op("pool", lambda e: e.memset(Sst[:, 0, :], 0.0), writes=["S0"])
        for n in ("wr", "wk", "wv", "wxw", "wxa", "wxg"):
            P.op("pool", lambda e, n=n: e.memset(gbuf[n][:, 0:1], 0.0), writes=["g_" + n])

        pcount = [0]

        def proj_bank():
            i = pcount[0] % 2
            pcount[0] += 1
            return pbank[i], "pbank%d" % i

        rw = {}
        for n in ("mr", "mk", "mv", "mxw", "mxa"):
            rw[n] = sb("rw_" + n, [64, TT])
        rw["mxg"] = sb("rw_mxg", [128, TT])
        for n in ("t1", "t2", "t3", "t4", "cl", "asb", "kk", "rkb"):
            rw[n] = sb("rw_" + n, [64, TT])
        for n in ("At", "Bt", "Kt", "Rt", "Bh", "Kh", "mvb"):
            rw[n] = sb("rw_" + n, [64, TT], BF16)
        sgx = sb("rw_sgx", [128, TT], BF16)
        d128 = sb("rw_d128", [128, TT])
        g2h_bf = sb("g2h_bf", [128, 64], BF16)
        P.op("dve", lambda e: e.tensor_copy(out=g2h_bf[:], in_=g2h[:]), reads=["g2h"], writes=["g2h_bf"])

        def proj_gen(n):
            t0 = n * TT
            X = xs[n % 2]
            xk = "xs"
            for c in range(8):
                P.dma("sp" if c % 2 == 0 else "pool", X[:, c, :], xT[c * 128:(c + 1) * 128, t0:t0 + TT],
                      writes=[xk + "_%d" % c] + (["wst0a", "wst0b", "wst1a", "wst1b"] if n == 0 else []))
            P.dma("sp", ropet[:], rope_d[:, :, t0:t0 + TT], writes=["ropet"])
            xr = [xk + "_%d" % c for c in range(8)]
            P.op("act", lambda e, X=X: e.activation(out=xb[:], in_=X[:], func=AF.Copy), reads=xr, writes=["xb"])
            for c in range(8):
                P.op("pool", lambda e, X=X, c=c: e.tensor_tensor(out=sqr[c % 2][:], in0=X[:, c, :], in1=X[:, c, :],
                                                                 op=ALU.mult), reads=[xr[c]], writes=["sq%d" % (c % 2)])
                P.op("pe", lambda e, c=c: e.matmul(misc[:], lhsT=ones_bf[:], rhs=sqr[c % 2][:], start=(c == 0),
                                                   stop=(c == 7)), reads=["ones_bf", "sq%d" % (c % 2)], writes=["misc"])
            P.op("act", lambda e: e.activation(out=rstd[:], in_=misc[:], func=AF.Ln, scale=1.0 / D, bias=1e-6),
                 reads=["misc"], writes=["rstd"])
            P.op("act", lambda e: e.activation(out=rstd[:], in_=rstd[:], func=AF.Exp, scale=-0.5), reads=["rstd"],
                 writes=["rstd"])
            for s in range(4):
                P.op("pe", lambda e, s=s: e.matmul(misc[:, s:s + 1], lhsT=rstd[0:1, s * 128:(s + 1) * 128],
                                                   rhs=ones_f[0:1, 0:1], start=True, stop=True),
                     reads=["rstd", "ones_f"], writes=["misc"])
            P.op("dve", lambda e: e.tensor_copy(out=rstd_tm[:], in_=misc[:, 0:4]), reads=["misc"], writes=["rstd_tm"])
            yield
            for gname, gw in FG:
                o, w = FOFF[gname]
                bank, bk = proj_bank()
                for c in range(8):
                    P.op("pe", lambda e, c=c, o=o, w=w, bank=bank: e.matmul(bank[0:w, :], lhsT=wfb[:, c, o:o + w],
                                                                            rhs=xb[:, c, :], start=(c == 0), stop=(c == 7)),
                         reads=["wfb", "xb"], writes=[bk])
                P.op("dve", lambda e, w=w, bank=bank, gname=gname: e.tensor_tensor(
                    out=gbuf[gname][:, 1:TT + 1], in0=bank[0:w, :], in1=rstd[0:w, :], op=ALU.mult),
                    reads=[bk, "rstd"], writes=["g_" + gname])
                yield
            for s in range(4):
                bank, bk = proj_bank()
                for c in range(8):
                    P.op("pe", lambda e, c=c, s=s, bank=bank: e.matmul(bank[:, 0:NTM], lhsT=xb[:, c, s * 128:(s + 1) * 128],
                                                                       rhs=wtb[:, c, :], start=(c == 0), stop=(c == 7)),
                         reads=["wtb", "xb"], writes=[bk])
                P.op("act", lambda e, s=s, bank=bank: e.activation(out=tm[:, s, :], in_=bank[:, 0:NTM], func=AF.Copy,
                                                                   scale=rstd_tm[:, s:s + 1]),
                     reads=[bk, "rstd_tm"], writes=["tm%d" % s])
                yield


        def run_all(items):
            active = [list(it) for it in items]
            rnd = 0
            while active:
                for it in list(active):
                    g, m, k = it
                    if rnd % k:
                        continue
                    for _ in range(m):
                        try:
                            next(g)
                        except StopIteration:
                            active.remove(it)
                            break
                rnd += 1

        L = dict(locals())
        run_all([(proj_gen(0), 1, 1)])
        for n in range(NT):
            ret_head(P, nc, n, L)
            fox_head(P, nc, n, L)
            rwkv_head(P, nc, n, L)
            items = [(rwkv_body(P, nc, n, L), RW_RATE, 1), (fox_body(P, nc, n, L), 1, 1), (ret_body(P, nc, n, L), 1, 4)]
            if n + 1 < NT:
                items.append((proj_gen(n + 1), 1, 3))
            run_all(items)


def ret_head(P, nc, n, L):
    gbuf, ropet, tm, rps, dmat, bc, pp = L["gbuf"], L["ropet"], L["tm"], L["rps"], L["dmat"], L["bc"], L["pp"]
    Rst, Rbf, iden_bf, y_ret, sb = L["Rst"], L["Rbf"], L["iden_bf"], L["y_ret"], L["sb"]
    ppc = L["ppc"]
    if n == 0:
        R = {}
        R["t1"] = sb("r_t1", [64, TT]); R["t2"] = sb("r_t2", [64, TT])
        R["qT"] = sb("r_qT", [64, TT], BF16); R["kT"] = sb("r_kT", [64, TT], BF16); R["qd"] = sb("r_qd", [64, TT], BF16)
        R["kd"] = sb("r_kd", [128, 64], BF16); R["sc"] = sb("r_sc", [128, 128], BF16)
        R["y"] = sb("r_y", [128, 64]); R["st"] = sb("r_st", [128, 4]); R["yc"] = sb("r_yc", [128, 64])
        R["junk"] = sb("r_junk", [128, 64]); R["yo"] = sb("r_yo", [128, 64])
        R["vall"] = sb("r_vall", [128, 4, 64], BF16); R["sgall"] = sb("r_sgall", [128, 4, 64])
        ret_head.R = R
    R = ret_head.R
    t0 = n * TT
    cosv, sinv = ropet[:, 0, :], ropet[:, 1, :]
    for nm, a, b_ in (("qT", "rq", "rqs"), ("kT", "rk", "rks")):
        P.op("pool", lambda e, a=a: e.tensor_tensor(out=R["t1"][:], in0=gbuf[a][:, 1:TT + 1], in1=cosv, op=ALU.mult),
             reads=["g_" + a, "ropet"], writes=["r_t1"])
        P.op("pool", lambda e, b_=b_: e.tensor_tensor(out=R["t2"][:], in0=gbuf[b_][:, 1:TT + 1], in1=sinv, op=ALU.mult),
             reads=["g_" + b_, "ropet"], writes=["r_t2"])
        P.op("dve", lambda e, nm=nm: e.tensor_tensor(out=R[nm][:], in0=R["t1"][:], in1=R["t2"][:], op=ALU.add),
             reads=["r_t1", "r_t2"], writes=["r_" + nm])
    P.op("pool", lambda e: e.tensor_tensor(out=R["qd"][:].rearrange("p (c j) -> p c j", j=128),
                                           in0=R["qT"][:].rearrange("p (c j) -> p c j", j=128),
                                           in1=bc[0:64, 128:256].unsqueeze(1).to_broadcast([64, 4, 128]), op=ALU.mult),
         reads=["r_qT", "bc"], writes=["r_qd"])
    tmk = ["tm%d" % s for s in range(4)]
    P.op("pool", lambda e: e.tensor_copy(out=R["vall"][:], in_=tm[:, :, 0:64]), reads=tmk, writes=["r_vall"])
    P.op("act", lambda e: e.activation(out=R["sgall"][:], in_=tm[:, :, 128:192], func=AF.Silu), reads=tmk, writes=["r_sgall"])
    P.op("pool", lambda e: e.tensor_tensor(out=R["sgall"][:], in0=R["sgall"][:],
                                           in1=bc[:, 0:64].unsqueeze(1).to_broadcast([128, 4, 64]), op=ALU.mult),
         reads=["r_sgall", "bc"], writes=["r_sgall"])


def ret_body(P, nc, n, L):
    gbuf, ropet, tm, rps, dmat, bc, pp = L["gbuf"], L["ropet"], L["tm"], L["rps"], L["dmat"], L["bc"], L["pp"]
    Rst, Rbf, iden_bf, y_ret, sb = L["Rst"], L["Rbf"], L["iden_bf"], L["y_ret"], L["sb"]
    ppc = L["ppc"]
    R = ret_head.R
    t0 = n * TT
    for s in range(4):
        cs = slice(s * 128, (s + 1) * 128)
        P.op("pe", lambda e, cs=cs: e.matmul(rps[:, 320:384], lhsT=R["kT"][:, cs], rhs=iden_bf[0:64, 0:64], start=True,
                                             stop=True), reads=["r_kT", "iden_bf"], writes=["rps_k"])
        P.op("dve", lambda e: e.tensor_scalar(out=R["kd"][:], in0=rps[:, 320:384], scalar1=ppc("kdec", 128), scalar2=None,
                                              op0=ALU.mult), reads=["rps_k", "pp"], writes=["r_kd"])
        P.op("pe", lambda e, cs=cs: e.matmul(rps[:, 0:128], lhsT=R["kT"][:, cs], rhs=R["qT"][:, cs], start=True, stop=True),
             reads=["r_kT", "r_qT"], writes=["rps_s"])
        P.op("dve", lambda e: e.tensor_tensor(out=R["sc"][:], in0=rps[:, 0:128], in1=dmat[:], op=ALU.mult),
             reads=["rps_s", "dmat"], writes=["r_sc"])
        P.op("pe", lambda e, s=s: e.matmul(rps[:, 128:192], lhsT=R["sc"][:], rhs=R["vall"][:, s, :], start=True, stop=False),
             reads=["r_sc", "r_vall"], writes=["rps_y"])
        P.op("pe", lambda e, cs=cs: e.matmul(rps[:, 128:192], lhsT=R["qd"][:, cs], rhs=Rbf[:], start=False, stop=True),
             reads=["r_qd", "Rbf"], writes=["rps_y"])
        P.op("pe", lambda e, s=s: e.matmul(rps[0:64, 192:256], lhsT=R["kd"][:], rhs=R["vall"][:, s, :], start=True, stop=True),
             reads=["r_kd", "r_vall"], writes=["rps_u"])
        yield
        P.op("dve", lambda e: e.scalar_tensor_tensor(out=Rst[:], in0=Rst[:], scalar=ppc("cdec"), in1=rps[0:64, 192:256],
                                                     op0=ALU.mult, op1=ALU.add), reads=["Rst", "rps_u", "pp"], writes=["Rst"])
        P.op("act", lambda e: e.activation(out=Rbf[:], in_=Rst[:], func=AF.Copy), reads=["Rst"], writes=["Rbf"])
        yield
        st = R["st"]
        P.op("act", lambda e: e.activation(out=R["y"][:], in_=rps[:, 128:192], func=AF.Copy, accum_out=st[:, 0:1]),
             reads=["rps_y"], writes=["r_y", "r_st0"])
        P.op("dve", lambda e: e.tensor_scalar(out=st[:, 1:2], in0=st[:, 0:1], scalar1=1.0 / 64, scalar2=None, op0=ALU.mult),
             reads=["r_st0"], writes=["r_st1"])
        P.op("dve", lambda e: e.tensor_scalar(out=R["yc"][:], in0=R["y"][:], scalar1=st[:, 1:2], scalar2=None,
                                              op0=ALU.subtract), reads=["r_y", "r_st1"], writes=["r_yc"])
        P.op("act", lambda e: e.activation(out=R["junk"][:], in_=R["yc"][:], func=AF.Square, accum_out=st[:, 2:3]),
             reads=["r_yc"], writes=["r_junk", "r_st2"])
        P.op("act", lambda e: e.activation(out=st[:, 3:4], in_=st[:, 2:3], func=AF.Ln, scale=1.0 / 64, bias=1e-5),
             reads=["r_st2"], writes=["r_st3"])
        P.op("act", lambda e: e.activation(out=st[:, 3:4], in_=st[:, 3:4], func=AF.Exp, scale=-0.5), reads=["r_st3"],
             writes=["r_st3"])
        P.op("dve", lambda e, s=s: e.scalar_tensor_tensor(out=R["yo"][:], in0=R["yc"][:], scalar=st[:, 3:4], in1=R["sgall"][:, s, :],
                                                          op0=ALU.mult, op1=ALU.mult), reads=["r_yc", "r_st3", "r_sgall"],
             writes=["r_yo"])
        P.dma("sp", y_ret[t0 + s * 128:t0 + (s + 1) * 128, :], R["yo"][:], reads=["r_yo"])
        yield


def fox_head(P, nc, n, L):
    gbuf, tm, misc, pp, der = L["gbuf"], L["tm"], L["misc"], L["pp"], L["der"]
    KT, QT, VA, fsc, facc, ccar = L["KT"], L["QT"], L["VA"], L["fsc"], L["facc"], L["ccar"]
    ones_bf, ones_f, mski_bf, y_fox, sb, ppc = L["ones_bf"], L["ones_f"], L["mski_bf"], L["y_fox"], L["sb"], L["ppc"]
    if n == 0:
        F = {}
        F["sq"] = sb("f_sq", [64, TT], BF16); F["rs"] = sb("f_rs", [64, TT])
        F["e"] = sb("f_e", [1, TT]); F["c"] = sb("f_c", [1, TT]); F["r1"] = sb("f_r1", [1, TT])
        F["pc"] = sb("f_pc", [1, 3, TT], BF16)
        F["pT"] = [sb("f_pT%d" % i, [128, TT], BF16) for i in range(2)]
        F["acc"] = sb("f_acc", [128, 4, 65]); F["rec"] = sb("f_rec", [128, 4, 1]); F["yo"] = sb("f_yo", [128, 4, 64])
        F["cnt"] = 0
        fox_head.F = F
    F = fox_head.F
    t0 = n * TT
    for src, gcol, dst, dk in (("fq", der[0:64, 0:1], QT[0:64, :], "QTm"), ("fk", ppc("fkg"), KT[0:64, t0:t0 + TT], "KTm")):
        P.op("act", lambda e, src=src: e.activation(out=F["sq"][:], in_=gbuf[src][:, 1:TT + 1], func=AF.Square),
             reads=["g_" + src], writes=["f_sq"])
        P.op("pe", lambda e: e.matmul(misc[0:64, :], lhsT=ones_bf[0:64, 0:64], rhs=F["sq"][:], start=True, stop=True),
             reads=["ones_bf", "f_sq"], writes=["misc"])
        P.op("act", lambda e: e.activation(out=F["rs"][:], in_=misc[0:64, :], func=AF.Ln, scale=1.0 / 64, bias=1e-6),
             reads=["misc"], writes=["f_rs"])
        P.op("act", lambda e: e.activation(out=F["rs"][:], in_=F["rs"][:], func=AF.Exp, scale=-0.5), reads=["f_rs"],
             writes=["f_rs"])
        P.op("dve", lambda e, src=src, gcol=gcol, dst=dst: e.scalar_tensor_tensor(
            out=dst, in0=gbuf[src][:, 1:TT + 1], scalar=gcol, in1=F["rs"][:], op0=ALU.mult, op1=ALU.mult),
            reads=["g_" + src, "f_rs", "pp", "der0"], writes=[dk])
    P.op("act", lambda e: e.activation(out=F["e"][:], in_=gbuf["ff"][:, 1:TT + 1], func=AF.Exp, scale=-1.0,
                                       bias=der[0:1, 2:3]), reads=["g_ff", "der2"], writes=["f_e"])
    P.op("act", lambda e: e.activation(out=F["e"][:], in_=F["e"][:], func=AF.Ln, bias=1.0), reads=["f_e"], writes=["f_e"])
    P.op("dve", lambda e: e.tensor_tensor_scan(out=F["c"][:], data0=L["mask_one"][:],
                                               data1=F["e"][:], initial=ccar[:], op0=ALU.mult, op1=ALU.subtract),
         reads=["f_e", "ccar", "mask_one"], writes=["f_c"])
    P.op("dve", lambda e: e.tensor_copy(out=ccar[:], in_=F["c"][:, TT - 1:TT]), reads=["f_c"], writes=["ccar"])
    pc = F["pc"]
    P.op("dve", lambda e: e.tensor_copy(out=pc[:, 0, :], in_=F["c"][:]), reads=["f_c"], writes=["f_pc0"])
    P.op("dve", lambda e: e.tensor_tensor(out=F["r1"][:], in0=F["c"][:], in1=pc[:, 0, :], op=ALU.subtract),
         reads=["f_c", "f_pc0"], writes=["f_r1"])
    P.op("dve", lambda e: e.tensor_copy(out=pc[:, 1, :], in_=F["r1"][:]), reads=["f_r1"], writes=["f_pc1"])
    P.op("dve", lambda e: e.tensor_tensor(out=F["r1"][:], in0=F["r1"][:], in1=pc[:, 1, :], op=ALU.subtract),
         reads=["f_r1", "f_pc1"], writes=["f_r1"])
    P.op("dve", lambda e: e.tensor_copy(out=pc[:, 2, :], in_=F["r1"][:]), reads=["f_r1"], writes=["f_pc2"])
    pcr = ["f_pc0", "f_pc1", "f_pc2"]
    P.dma("sp", QT[64:67, :], pc[0:1, :, :], reads=pcr + ["QTaug"], writes=["QTc"])
    P.dma("sp", KT[67:70, t0:t0 + TT], pc[0:1, :, :], reads=pcr + ["KTaug"], writes=["KTc"])
    for s in range(4):
        P.op("pool", lambda e, s=s: e.tensor_copy(out=VA[:, 4 * n + s, 0:64], in_=tm[:, s, 64:128]), reads=["tm%d" % s],
             writes=["VAv"])


def fox_body(P, nc, n, L):
    gbuf, tm, misc, pp, der = L["gbuf"], L["tm"], L["misc"], L["pp"], L["der"]
    KT, QT, VA, fsc, facc, ccar = L["KT"], L["QT"], L["VA"], L["fsc"], L["facc"], L["ccar"]
    ones_bf, ones_f, mski_bf, y_fox, sb, ppc = L["ones_bf"], L["ones_f"], L["mski_bf"], L["y_fox"], L["sb"], L["ppc"]
    F = fox_head.F
    t0 = n * TT
    nk = 4 * n + 4
    P.op("dve", lambda e: e.memset(facc[:, :, 0:65], 0.0), writes=["facc"])
    for kt in range(nk):
        jmin = max(0, kt - 4 * n)
        c0 = jmin * 128
        i = F["cnt"] % 2
        F["cnt"] += 1
        sc, sk, pT, pk = fsc[i], "fsc%d" % i, F["pT"][i], "f_pT%d" % i
        P.op("pe", lambda e, sc=sc, kt=kt, c0=c0: e.matmul(sc[:, c0:TT], lhsT=KT[0:70, kt * 128:(kt + 1) * 128],
                                                           rhs=QT[0:70, c0:TT], start=True, stop=True),
             reads=["KTm", "KTc", "KTaug", "QTm", "QTc", "QTaug"], writes=[sk])
        P.op("act", lambda e, sc=sc, pT=pT, c0=c0: e.activation(out=pT[:, c0:TT], in_=sc[:, c0:TT], func=AF.Exp),
             reads=[sk], writes=[pk])
        if kt >= 4 * n:
            P.op("pool", lambda e, pT=pT, c0=c0: e.tensor_tensor(out=pT[:, c0:c0 + 128], in0=pT[:, c0:c0 + 128],
                                                                 in1=mski_bf[:], op=ALU.mult),
                 reads=[pk, "mski_bf"], writes=[pk])
        for qs in range(jmin, 4):
            P.op("pe", lambda e, pT=pT, qs=qs, kt=kt: e.matmul(facc[:, qs, 0:65], lhsT=pT[:, qs * 128:(qs + 1) * 128],
                                                               rhs=VA[:, kt, :], start=False, stop=(kt == 4 * n + qs), skip_group_check=True),
                 reads=[pk, "VAv", "VAone"], writes=["facc"])
        yield
    P.op("act", lambda e: e.activation(out=F["acc"][:], in_=facc[:, :, 0:65], func=AF.Copy), reads=["facc"], writes=["f_acc"])
    P.op("dve", lambda e: e.reciprocal(out=F["rec"][:], in_=F["acc"][:, :, 64:65]), reads=["f_acc"], writes=["f_rec"])
    P.op("dve", lambda e: e.tensor_tensor(out=F["yo"][:], in0=F["acc"][:, :, 0:64],
                                          in1=F["rec"][:].to_broadcast([128, 4, 64]), op=ALU.mult),
         reads=["f_acc", "f_rec"], writes=["f_yo"])
    P.dma("sp", y_fox[t0:t0 + TT, :].rearrange("(s p) d -> p s d", p=128), F["yo"][:], reads=["f_yo"])


def rwkv_head(P, nc, n, L):
    gbuf, rw, sgx, misc, wps, pp, der, bc = L["gbuf"], L["rw"], L["sgx"], L["misc"], L["wps"], L["pp"], L["der"], L["bc"]
    w2h, a2h, g2h, msk, iden, ones_f, mask64, Sst = (L[k] for k in ("w2h", "a2h", "g2h", "msk", "iden", "ones_f", "mask64", "Sst"))
    y_rw, sb, ppc = L["y_rw"], L["sb"], L["ppc"]
    if n == 0:
        W = {}
        for nm in ("Na", "Nb", "La", "Lb", "Ta", "Tb", "AkT", "ArbT", "ArkT"):
            W[nm] = sb("w_" + nm, [128, 128], BF16)
        for nm in ("X2", "M1", "M2", "Atm", "Bhtm", "Khtm", "Vtm"):
            W[nm] = sb("w_" + nm, [128, 64], BF16)
        for nm in ("GT", "H"):
            W[nm] = sb("w_" + nm, [64, 64])
        W["QtT"] = sb("w_QtT", [64, 128])
        for nm in ("y", "yc", "junk", "o1", "o2", "yo"):
            W[nm] = sb("w_" + nm, [128, 64])
        W["PCs"] = sb("w_PCs", [64, 4]); W["st"] = sb("w_st", [128, 6])
        W["slot"] = 0
        rwkv_head.W = W
    W = rwkv_head.W
    t0 = n * TT
    i64 = iden[0:64, 0:64]
    for g, m, mu in (("wr", "mr", "mu_r"), ("wk", "mk", "mu_k"), ("wv", "mv", "mu_v"), ("wxw", "mxw", "mu_xw"),
                     ("wxa", "mxa", "mu_xa"), ("wxg", "mxg", "mu_xg")):
        w = 128 if g == "wxg" else 64
        tmp = L["d128"] if g == "wxg" else rw["t1"]
        tk = "d128" if g == "wxg" else "rw_t1"
        gb = gbuf[g]
        P.op("pool", lambda e, gb=gb, tmp=tmp: e.tensor_tensor(out=tmp[:], in0=gb[:, 0:TT], in1=gb[:, 1:TT + 1], op=ALU.subtract),
             reads=["g_" + g], writes=[tk])
        P.op("dve", lambda e, gb=gb, tmp=tmp, m=m, mu=mu, w=w: e.scalar_tensor_tensor(
            out=rw[m][:], in0=tmp[:], scalar=ppc(mu, w), in1=gb[:, 1:TT + 1], op0=ALU.mult, op1=ALU.add),
            reads=[tk, "g_" + g, "pp"], writes=["rw_" + m])
        P.op("pool", lambda e, gb=gb: e.tensor_copy(out=gb[:, 0:1], in_=gb[:, TT:TT + 1]), reads=[], writes=["g_" + g])


def rwkv_body(P, nc, n, L):
    gbuf, rw, sgx, misc, wps, pp, der, bc = L["gbuf"], L["rw"], L["sgx"], L["misc"], L["wps"], L["pp"], L["der"], L["bc"]
    w2h, a2h, g2h, msk, iden, ones_f, mask64, Sst = (L[k] for k in ("w2h", "a2h", "g2h", "msk", "iden", "ones_f", "mask64", "Sst"))
    y_rw, sb, ppc = L["y_rw"], L["sb"], L["ppc"]
    W = rwkv_head.W
    t0 = n * TT
    i64 = iden[0:64, 0:64]
    i64b = L["iden_bf"][0:64, 0:64]
    t1, t2, t3, cl = rw["t1"], rw["t2"], rw["t3"], rw["cl"]
    P.op("act", lambda e: e.activation(out=t1[:], in_=rw["mxw"][:], func=AF.Tanh), reads=["rw_mxw"], writes=["rw_t1"])
    P.op("pe", lambda e: e.matmul(misc[0:64, :], lhsT=w2h[:], rhs=t1[:], start=True, stop=True), reads=["w2h", "rw_t1"],
         writes=["misc"])
    P.op("act", lambda e: e.activation(out=t2[:], in_=misc[0:64, :], func=AF.Sigmoid, bias=ppc("w0")), reads=["misc", "pp"],
         writes=["rw_t2"])
    P.op("dve", lambda e: e.tensor_scalar(out=t2[:], in0=t2[:], scalar1=-0.6065306597126334, scalar2=None, op0=ALU.mult),
         reads=["rw_t2"], writes=["rw_t2"])
    P.op("dve", lambda e: e.tensor_tensor_scan(out=cl[:], data0=mask64[:], data1=t2[:], initial=0.0, op0=ALU.mult,
                                               op1=ALU.add), reads=["mask64", "rw_t2"], writes=["rw_cl"])
    yield
    P.op("act", lambda e: e.activation(out=t3[:], in_=cl[:], func=AF.Exp), reads=["rw_cl"], writes=["rw_t3"])
    P.op("dve", lambda e: e.tensor_tensor(out=rw["Rt"][:], in0=rw["mr"][:], in1=t3[:], op=ALU.mult), reads=["rw_mr", "rw_t3"],
         writes=["rw_Rt"])
    P.op("dve", lambda e: e.tensor_copy(out=W["PCs"][:], in_=t3[:].rearrange("p (c j) -> p c j", j=128)[:, :, 127]),
         reads=["rw_t3"], writes=["w_PCs"])
    P.op("dve", lambda e: e.tensor_tensor(out=t1[:], in0=cl[:], in1=t2[:], op=ALU.subtract), reads=["rw_cl", "rw_t2"],
         writes=["rw_t1"])
    P.op("act", lambda e: e.activation(out=t1[:], in_=t1[:], func=AF.Exp), reads=["rw_t1"], writes=["rw_t1"])
    yield
    P.op("pe", lambda e: e.matmul(misc[0:64, :], lhsT=a2h[:], rhs=rw["mxa"][:], start=True, stop=True),
         reads=["a2h", "rw_mxa"], writes=["misc"])
    P.op("act", lambda e: e.activation(out=rw["asb"][:], in_=misc[0:64, :], func=AF.Sigmoid, bias=ppc("a0")),
         reads=["misc", "pp"], writes=["rw_asb"])
    yield
    kk = rw["kk"]
    P.op("dve", lambda e: e.tensor_scalar(out=kk[:], in0=rw["mk"][:], scalar1=ppc("k_k"), scalar2=None, op0=ALU.mult),
         reads=["rw_mk", "pp"], writes=["rw_kk"])
    P.op("act", lambda e: e.activation(out=t2[:], in_=kk[:], func=AF.Square), reads=["rw_kk"], writes=["rw_t2"])
    P.op("pe", lambda e: e.matmul(misc[0:64, :], lhsT=ones_f[0:64, 0:64], rhs=t2[:], start=True, stop=True),
         reads=["ones_f", "rw_t2"], writes=["misc"])
    P.op("dve", lambda e: e.tensor_scalar(out=t2[:], in0=misc[0:64, :], scalar1=1e-24, scalar2=None, op0=ALU.max),
         reads=["misc"], writes=["rw_t2"])
    P.op("act", lambda e: e.activation(out=t2[:], in_=t2[:], func=AF.Ln), reads=["rw_t2"], writes=["rw_t2"])
    P.op("act", lambda e: e.activation(out=t2[:], in_=t2[:], func=AF.Exp, scale=-0.5), reads=["rw_t2"], writes=["rw_t2"])
    P.op("dve", lambda e: e.tensor_tensor(out=kk[:], in0=kk[:], in1=t2[:], op=ALU.mult), reads=["rw_kk", "rw_t2"],
         writes=["rw_kk"])
    yield
    P.op("dve", lambda e: e.scalar_tensor_tensor(out=rw["At"][:], in0=kk[:], scalar=-1.0, in1=t1[:], op0=ALU.mult,
                                                 op1=ALU.mult), reads=["rw_kk", "rw_t1"], writes=["rw_At"])
    P.op("act", lambda e: e.activation(out=t3[:], in_=cl[:], func=AF.Exp, scale=-1.0), reads=["rw_cl"], writes=["rw_t3"])
    P.op("dve", lambda e: e.tensor_scalar(out=t2[:], in0=rw["asb"][:], scalar1=ppc("k_a"), scalar2=der[0:64, 1:2],
                                          op0=ALU.mult, op1=ALU.add), reads=["rw_asb", "pp", "der1"], writes=["rw_t2"])
    t4 = rw["t4"]
    P.op("dve", lambda e: e.tensor_tensor(out=t4[:], in0=rw["mk"][:], in1=t2[:], op=ALU.mult), reads=["rw_mk", "rw_t2"],
         writes=["rw_t4"])
    P.op("dve", lambda e: e.scalar_tensor_tensor(out=rw["rkb"][:], in0=t4[:], scalar=ppc("r_k"), in1=rw["mr"][:],
                                                 op0=ALU.mult, op1=ALU.mult), reads=["rw_t4", "rw_mr", "pp"], writes=["rw_rkb"])
    yield
    P.op("pool", lambda e: e.tensor_tensor(out=t2[:], in0=kk[:], in1=rw["asb"][:], op=ALU.mult), reads=["rw_kk", "rw_asb"],
         writes=["rw_t2"])
    P.op("dve", lambda e: e.tensor_tensor(out=rw["Kt"][:], in0=t4[:], in1=t3[:], op=ALU.mult), reads=["rw_t4", "rw_t3"],
         writes=["rw_Kt"])
    P.op("dve", lambda e: e.tensor_tensor(out=rw["Bt"][:], in0=t2[:], in1=t3[:], op=ALU.mult), reads=["rw_t2", "rw_t3"],
         writes=["rw_Bt"])
    P.op("pool", lambda e: e.tensor_copy(out=rw["mvb"][:], in_=rw["mv"][:]), reads=["rw_mv"], writes=["rw_mvb"])
    for src, dst in (("Bt", "Bh"), ("Kt", "Kh")):
        P.op("pool", lambda e, src=src, dst=dst: e.tensor_tensor(
            out=rw[dst][:].rearrange("p (c j) -> p c j", j=128), in0=rw[src][:].rearrange("p (c j) -> p c j", j=128),
            in1=W["PCs"][:].unsqueeze(2).to_broadcast([64, 4, 128]), op=ALU.mult),
            reads=["rw_" + src, "w_PCs"], writes=["rw_" + dst])
    P.op("act", lambda e: e.activation(out=sgx[:], in_=rw["mxg"][:], func=AF.Sigmoid), reads=["rw_mxg"], writes=["sgx"])
    yield

    NSL = 4

    def mm(rows, cols, lhsT, rhs, reads, start=True, stop=True, slot=None):
        if slot is None:
            slot = W["slot"] % NSL
            W["slot"] += 1
        P.op("pe", lambda e: e.matmul(wps[0:rows, slot, 0:cols], lhsT=lhsT, rhs=rhs, start=start, stop=stop), reads=reads,
             writes=["wps_%d" % slot])
        return slot

    def ev_copy(dst, slot, rows, cols, eng="act"):
        if eng == "act":
            P.op("act", lambda e: e.activation(out=W[dst][:], in_=wps[0:rows, slot, 0:cols], func=AF.Copy),
                 reads=["wps_%d" % slot], writes=["w_" + dst])
        else:
            P.op("dve", lambda e: e.tensor_copy(out=W[dst][:], in_=wps[0:rows, slot, 0:cols]), reads=["wps_%d" % slot],
                 writes=["w_" + dst])

    def ev_mask(dst, slot, mi):
        P.op("dve", lambda e: e.tensor_tensor(out=W[dst][:], in0=wps[:, slot, :], in1=msk[:, mi, :], op=ALU.mult),
             reads=["wps_%d" % slot, "msk"], writes=["w_" + dst])

    CH = 128
    for c in range(TT // CH):
        cs = slice(c * CH, (c + 1) * CH)
        At, Bt, Kt, Rt, Bh, Kh = (rw[k][:, cs] for k in ("At", "Bt", "Kt", "Rt", "Bh", "Kh"))
        for src, srck, dst in ((rw["mvb"][:, cs], "rw_mvb", "Vtm"), (At, "rw_At", "Atm"), (Bh, "rw_Bh", "Bhtm"), (Kh, "rw_Kh", "Khtm")):
            sl = mm(CH, 64, src, i64b, [srck, "iden_bf"])
            ev_copy(dst, sl, CH, 64)
            yield
        sl = mm(CH, CH, Bt, At, ["rw_Bt", "rw_At"]); ev_mask("Na", sl, 0)
        yield
        sl = mm(CH, CH, At, Bt, ["rw_Bt", "rw_At"]); ev_mask("La", sl, 1)
        P.op("dve", lambda e: e.tensor_tensor(out=W["Ta"][:], in0=W["Na"][:], in1=L["iden_bf"][:], op=ALU.add), reads=["w_Na", "iden_bf"],
             writes=["w_Ta"])
        yield
        Nc, Lc, Tc = "Na", "La", "Ta"
        other = {"Na": "Nb", "Nb": "Na", "La": "Lb", "Lb": "La", "Ta": "Tb", "Tb": "Ta"}
        for k in range(6):
            Nn, Ln, Tn = other[Nc], other[Lc], other[Tc]
            if k < 5:
                sl = mm(CH, CH, W[Lc][:], W[Nc][:], ["w_" + Lc, "w_" + Nc]); ev_copy(Nn, sl, CH, CH, "act")
                yield
            sl = mm(CH, CH, W[Nc][:], W[Lc][:], ["w_" + Lc, "w_" + Nc]); ev_copy(Ln, sl, CH, CH, "act")
            yield
            sl = mm(CH, CH, W[Ln][:], W[Tc][:], ["w_" + Ln, "w_" + Tc])
            P.op("dve", lambda e, sl=sl, Tc=Tc, Tn=Tn: e.tensor_tensor(out=W[Tn][:], in0=wps[:, sl, :], in1=W[Tc][:], op=ALU.add),
                 reads=["wps_%d" % sl, "w_" + Tc], writes=["w_" + Tn])
            yield
            Nc, Lc, Tc = Nn, Ln, Tn
        TT_ = W[Tc]
        tk = "w_" + Tc
        sl = mm(CH, CH, Kt, At, ["rw_Kt", "rw_At"]); ev_mask("AkT", sl, 0)
        yield
        sl = mm(CH, 64, W["AkT"][:], W["Vtm"][:], ["w_AkT", "w_Vtm"]); ev_copy("X2", sl, CH, 64)
        yield
        sl = mm(CH, 64, TT_[:], W["X2"][:], [tk, "w_X2"]); ev_copy("M2", sl, CH, 64)
        yield
        sl = mm(CH, 64, TT_[:], W["Atm"][:], [tk, "w_Atm"]); ev_copy("M1", sl, CH, 64, "dve")
        yield
        sl = mm(64, 64, W["M1"][:], W["Bhtm"][:], ["w_M1", "w_Bhtm"])
        P.op("dve", lambda e, sl=sl, c=c: e.scalar_tensor_tensor(out=W["GT"][:], in0=i64, scalar=W["PCs"][:, c:c + 1],
                                                                 in1=wps[0:64, sl, 0:64], op0=ALU.mult, op1=ALU.add),
             reads=["wps_%d" % sl, "iden", "w_PCs"], writes=["w_GT"])
        yield
        sl = mm(64, 64, W["Bhtm"][:], W["M2"][:], ["w_Bhtm", "w_M2"], start=True, stop=False)
        mm(64, 64, W["Khtm"][:], W["Vtm"][:], ["w_Khtm", "w_Vtm"], start=False, stop=True, slot=sl)
        ev_copy("H", sl, 64, 64)
        yield
        sl = mm(64, 64, W["GT"][:], Sst[:, c, :], ["w_GT", "S%d" % c])
        P.op("dve", lambda e, sl=sl, c=c: e.tensor_tensor(out=Sst[:, c + 1, :], in0=wps[0:64, sl, 0:64], in1=W["H"][:], op=ALU.add),
             reads=["wps_%d" % sl, "w_H"], writes=["S%d" % (c + 1)])
        yield
        sl = mm(CH, CH, Bt, Rt, ["rw_Bt", "rw_Rt"]); ev_mask("ArbT", sl, 2)
        yield
        sl = mm(CH, CH, Kt, Rt, ["rw_Kt", "rw_Rt"]); ev_mask("ArkT", sl, 2)
        yield
        sl = mm(64, CH, W["M1"][:], W["ArbT"][:], ["w_M1", "w_ArbT"])
        P.op("dve", lambda e, sl=sl, Rt=Rt: e.tensor_tensor(out=W["QtT"][:], in0=wps[0:64, sl, :], in1=Rt, op=ALU.add),
             reads=["wps_%d" % sl, "rw_Rt"], writes=["w_QtT"])
        yield
        sl = mm(CH, 64, W["QtT"][:], Sst[:, c, :], ["w_QtT", "S%d" % c], start=True, stop=False)
        mm(CH, 64, W["ArbT"][:], W["M2"][:], ["w_ArbT", "w_M2"], start=False, stop=False, slot=sl)
        mm(CH, 64, W["ArkT"][:], W["Vtm"][:], ["w_ArkT", "w_Vtm"], start=False, stop=True, slot=sl)
        st = W["st"]
        P.op("act", lambda e, sl=sl: e.activation(out=W["y"][:], in_=wps[:, sl, 0:64], func=AF.Copy, accum_out=st[:, 0:1]),
             reads=["wps_%d" % sl], writes=["w_y", "w_st0"])
        P.op("dve", lambda e: e.tensor_scalar(out=st[:, 1:2], in0=st[:, 0:1], scalar1=1.0 / 64, scalar2=None, op0=ALU.mult),
             reads=["w_st0"], writes=["w_st1"])
        P.op("dve", lambda e: e.tensor_scalar(out=W["yc"][:], in0=W["y"][:], scalar1=st[:, 1:2], scalar2=None, op0=ALU.subtract),
             reads=["w_y", "w_st1"], writes=["w_yc"])
        yield
        P.op("act", lambda e: e.activation(out=W["junk"][:], in_=W["yc"][:], func=AF.Square, accum_out=st[:, 2:3]),
             reads=["w_yc"], writes=["w_junk", "w_st2"])
        P.op("act", lambda e: e.activation(out=st[:, 3:4], in_=st[:, 2:3], func=AF.Ln, scale=1.0 / 64, bias=64e-5),
             reads=["w_st2"], writes=["w_st3"])
        P.op("act", lambda e: e.activation(out=st[:, 3:4], in_=st[:, 3:4], func=AF.Exp, scale=-0.5), reads=["w_st3"],
             writes=["w_st3"])
        P.op("dve", lambda e: e.scalar_tensor_tensor(out=W["o1"][:], in0=W["yc"][:], scalar=st[:, 3:4], in1=bc[:, 64:128],
                                                     op0=ALU.mult, op1=ALU.mult), reads=["w_yc", "w_st3", "bc"], writes=["w_o1"])
        yield
        sl = mm(CH, 1, rw["rkb"][:, cs], ones_f[0:64, 0:1], ["rw_rkb", "ones_f"])
        P.op("act", lambda e, sl=sl: e.activation(out=st[:, 4:5], in_=wps[:, sl, 0:1], func=AF.Copy), reads=["wps_%d" % sl],
             writes=["w_st4"])
        P.op("dve", lambda e: e.scalar_tensor_tensor(out=W["o2"][:], in0=W["Vtm"][:], scalar=st[:, 4:5], in1=W["o1"][:],
                                                     op0=ALU.mult, op1=ALU.add), reads=["w_Vtm", "w_st4", "w_o1"], writes=["w_o2"])
        yield
        sl = mm(CH, 64, sgx[:, cs], L["g2h_bf"][:], ["sgx", "g2h_bf"])
        P.op("dve", lambda e, sl=sl: e.tensor_tensor(out=W["yo"][:], in0=wps[:, sl, 0:64], in1=W["o2"][:], op=ALU.mult),
             reads=["wps_%d" % sl, "w_o2"], writes=["w_yo"])
        P.dma("sp", y_rw[t0 + c * CH:t0 + (c + 1) * CH, :], W["yo"][:], reads=["w_yo"])
        yield
    P.op("dve", lambda e: e.tensor_copy(out=Sst[:, 0, :], in_=Sst[:, 4, :]), reads=["S4"], writes=["S0"])


DFF = 2816
NFC = 22


def build_C(TC):
    nc = bass.Bass("TRN2", target_bir_lowering=False)
    din = lambda n, shp, dt=F32: nc.dram_tensor(n, shp, dt, kind="ExternalInput").ap()
    A = dict(xsrc=din("xT", [D, TC]), ytm=din("ytm", [TC, 768]), wg=din("wg", [D, 3072]), pw=din("pw", [768, D]),
             wo=din("wo", [D, D]), wgu=din("wgu", [D, 2 * DFF]), wd=din("wd", [DFF, D]), g1=din("g1", [128, 8]),
             g2=din("g2", [128, 8]), iden=din("iden", [128, 128]))
    A["xdst"] = nc.dram_tensor("outT", [D, TC], F32, kind="ExternalOutput").ap()
    P = Prog(nc)
    with ExitStack() as es:
        emit_C(P, nc, es, TC, A, "")
    P.emit()
    return nc


def emit_C(P, nc, es, TC, A, pfx):
    NS = TC // 512
    xT, ytm, wg, pw, wo, wgu, wd, g1d, g2d, outT = (A[k] for k in ("xsrc", "ytm", "wg", "pw", "wo", "wgu", "wd", "g1", "g2", "xdst"))
    if True:
        def sb(name, shape, dt=F32):
            return es.enter_context(nc.sbuf_tensor("sb_" + pfx + name, shape, dt))

        def ps(name, shape, dt=F32):
            return es.enter_context(nc.psum_tensor("ps_" + pfx + name, shape, dt))
        g1 = sb("g1", [128, 8]); g2 = sb("g2", [128, 8])
        P.dma("sp", g1[:], g1d, writes=["g1"]); P.dma("sp", g2[:], g2d, writes=["g2"])
        ones_bf = sb("ones_bf", [128, 128], BF16)
        P.op("pool", lambda e: e.memset(ones_bf[:], 1.0), writes=["ones_bf"])
        xb = sb("xb", [128, 8, TC], BF16)
        big = sb("big", [128, NFC, TC], BF16)
        mb = big[:, 0:8, :]; ybf = big[:, 8:14, :]; hb = big
        rstd = [sb("rstd%d" % i, [128, TC]) for i in range(2)]
        macc = sb("macc", [128, NS, 512])
        stg = [sb("stg%d" % i, [128, 3072]) for i in range(2)]
        wcb = [sb("wcb%d" % i, [128, 3072], BF16) for i in range(2)]
        iden_f = sb("iden_f", [128, 128]); iden_bf = sb("iden_bf", [128, 128], BF16)
        P.dma("sp", iden_f[:], A["iden"], writes=["iden_f"])
        P.op("dve", lambda e: e.tensor_copy(out=iden_bf[:], in_=iden_f[:]), reads=["iden_f"], writes=["iden_bf"])
        xt = [sb("xt%d" % i, [128, 512]) for i in range(2)]
        sqb = [sb("sqb%d" % i, [128, 512], BF16) for i in range(2)]
        tA = [sb("tA%d" % i, [128, 512]) for i in range(2)]
        tB = [sb("tB%d" % i, [128, 512]) for i in range(2)]
        tC_ = [sb("tC%d" % i, [128, 512]) for i in range(2)]
        pbank = [ps("pbank%d" % i, [128, 512]) for i in range(4)]
        ssb = [ps("ssb%d" % i, [128, 512]) for i in range(NS)]
        cnt = {"pb": 0, "st": 0, "xt": 0, "t": 0}

        def nb():
            i = cnt["pb"] % 4
            cnt["pb"] += 1
            return pbank[i], "pbank%d" % i

        def wload(src_ap, shape3, gain=None, eng="dve"):
            i = cnt["st"] % 2
            cnt["st"] += 1
            a, b = shape3
            sv = stg[i][:, 0:a * b].rearrange("p (a b) -> p a b", b=b)
            wv = wcb[i][:, 0:a * b].rearrange("p (a b) -> p a b", b=b)
            P.dma("sp" if i == 0 else "pool", sv, src_ap, writes=["stg%d" % i])
            if gain is not None:
                P.op("dve", lambda e: e.tensor_tensor(out=wv, in0=sv, in1=gain[:].unsqueeze(2).to_broadcast([128, a, b]),
                                                      op=ALU.mult), reads=["stg%d" % i, "g1", "g2"], writes=["wcb%d" % i])
            elif eng == "act":
                P.op("act", lambda e: e.activation(out=wv, in_=sv, func=AF.Copy), reads=["stg%d" % i], writes=["wcb%d" % i])
            else:
                P.op("pool", lambda e: e.tensor_copy(out=wv, in_=sv), reads=["stg%d" % i], writes=["wcb%d" % i])
            return wv, "wcb%d" % i

        def rstd_from(ssk, s, dst):
            sl = slice(s * 512, (s + 1) * 512)
            P.op("act", lambda e: e.activation(out=dst[:, sl], in_=ssb[s][:], func=AF.Ln, scale=1.0 / D, bias=1e-6),
                 reads=["ssb%d" % s], writes=[ssk + "_%d" % s])
            P.op("act", lambda e: e.activation(out=dst[:, sl], in_=dst[:, sl], func=AF.Exp, scale=-0.5),
                 reads=[ssk + "_%d" % s], writes=[ssk + "_%d" % s])

        for s in range(NS):
            sl = slice(s * 512, (s + 1) * 512)
            for c in range(8):
                i = cnt["xt"] % 2
                cnt["xt"] += 1
                P.dma("sp", xt[i][:], xT[c * 128:(c + 1) * 128, sl], writes=["xt%d" % i])
                P.op("act", lambda e, i=i, c=c, sl=sl: e.activation(out=xb[:, c, sl], in_=xt[i][:], func=AF.Copy),
                     reads=["xt%d" % i], writes=["xb_%d_%d" % (c, s)])
                P.op("pool", lambda e, i=i: e.tensor_tensor(out=sqb[i][:], in0=xt[i][:], in1=xt[i][:], op=ALU.mult),
                     reads=["xt%d" % i], writes=["sqb%d" % i])
                P.op("pe", lambda e, i=i, c=c, s=s: e.matmul(ssb[s][:], lhsT=ones_bf[:], rhs=sqb[i][:], start=(c == 0),
                                                             stop=(c == 7)), reads=["ones_bf", "sqb%d" % i], writes=["ssb%d" % s])
            rstd_from("rstd0", s, rstd[0])
        for s in range(NS):
            i = cnt["st"] % 2
            cnt["st"] += 1
            yv = stg[i][:, 0:3072].rearrange("p (j f) -> p j f", f=768)
            yb = wcb[i][:, 0:3072].rearrange("p (j f) -> p j f", f=768)
            P.dma("sp", yv, ytm[s * 512:(s + 1) * 512, :].rearrange("(j p) f -> p j f", p=128), writes=["stg%d" % i])
            P.op("pool", lambda e, yv=yv, yb=yb: e.tensor_copy(out=yb, in_=yv), reads=["stg%d" % i], writes=["wcb%d" % i])
            for k in range(6):
                bank, bk = nb()
                for j in range(4):
                    P.op("pe", lambda e, bank=bank, j=j, k=k, yb=yb: e.matmul(bank[:, j * 128:(j + 1) * 128],
                                                                              lhsT=yb[:, j, k * 128:(k + 1) * 128], rhs=iden_bf[:],
                                                                              start=True, stop=True),
                         reads=["wcb%d" % i, "iden_bf"], writes=[bk])
                P.op("act", lambda e, bank=bank, k=k, s=s: e.activation(out=ybf[:, k, s * 512:(s + 1) * 512], in_=bank[:], func=AF.Copy),
                     reads=[bk], writes=["ybf%d_%d" % (k, s)])
        for dc in range(8):
            for br in range(3):
                col0 = br * 1024 + dc * 128
                wgb, wgk = wload(wg[:, col0:col0 + 128].rearrange("(c p) n -> p c n", p=128), (8, 128), gain=g1)
                pb, pk = wload(pw[br * 256:(br + 1) * 256, dc * 128:(dc + 1) * 128].rearrange("(c p) n -> p c n", p=128),
                               (2, 128), eng="act")
                for s in range(NS):
                    sl = slice(s * 512, (s + 1) * 512)
                    A, Ak = nb()
                    for c in range(8):
                        P.op("pe", lambda e, A=A, c=c, sl=sl, wgb=wgb: e.matmul(A[:], lhsT=wgb[:, c, :], rhs=xb[:, c, sl],
                                                                                 start=(c == 0), stop=(c == 7)),
                             reads=[wgk, "xb_%d_%d" % (c, s)], writes=[Ak])
                    B, Bk = nb()
                    for k in range(2):
                        P.op("pe", lambda e, B=B, k=k, sl=sl, pb=pb, br=br: e.matmul(B[:], lhsT=pb[:, k, :], rhs=ybf[:, br * 2 + k, sl],
                                                                                      start=(k == 0), stop=(k == 1)),
                             reads=[pk, "ybf%d_%d" % (br * 2 + k, s)], writes=[Bk])
                    i = cnt["t"] % 2
                    cnt["t"] += 1
                    P.op("dve", lambda e, A=A, i=i, sl=sl: e.tensor_tensor(out=tA[i][:], in0=A[:], in1=rstd[0][:, sl], op=ALU.mult),
                         reads=[Ak, "rstd0_%d" % s], writes=["tA%d" % i])
                    P.op("act", lambda e, i=i: e.activation(out=tA[i][:], in_=tA[i][:], func=AF.Sigmoid), reads=["tA%d" % i],
                         writes=["tA%d" % i])
                    if br == 0:
                        P.op("dve", lambda e, B=B, i=i, s=s: e.tensor_tensor(out=macc[:, s, :], in0=B[:], in1=tA[i][:], op=ALU.mult),
                             reads=[Bk, "tA%d" % i], writes=["macc%d" % s])
                    else:
                        P.op("dve", lambda e, B=B, i=i: e.tensor_tensor(out=tB[i][:], in0=B[:], in1=tA[i][:], op=ALU.mult),
                             reads=[Bk, "tA%d" % i], writes=["tB%d" % i])
                        if br == 1:
                            P.op("pool", lambda e, i=i, s=s: e.tensor_tensor(out=macc[:, s, :], in0=macc[:, s, :], in1=tB[i][:], op=ALU.add),
                                 reads=["tB%d" % i, "macc%d" % s], writes=["macc%d" % s])
                        else:
                            P.op("pool", lambda e, i=i, s=s, dc=dc, sl=sl: e.tensor_tensor(out=mb[:, dc, sl], in0=macc[:, s, :], in1=tB[i][:],
                                                                                           op=ALU.add),
                                 reads=["tB%d" % i, "macc%d" % s], writes=["mb_%d_%d" % (dc, s)])
        for dc in range(8):
            wob, wok = wload(wo[:, dc * 128:(dc + 1) * 128].rearrange("(c p) n -> p c n", p=128), (8, 128), eng="act")
            for s in range(NS):
                sl = slice(s * 512, (s + 1) * 512)
                A, Ak = nb()
                for c in range(8):
                    P.op("pe", lambda e, A=A, c=c, sl=sl, wob=wob: e.matmul(A[:], lhsT=wob[:, c, :], rhs=mb[:, c, sl], start=(c == 0),
                                                                             stop=(c == 7)), reads=[wok, "mb_%d_%d" % (c, s)], writes=[Ak])
                i = cnt["xt"] % 2
                cnt["xt"] += 1
                P.dma("sp", xt[i][:], xT[dc * 128:(dc + 1) * 128, sl], writes=["xt%d" % i])
                j = cnt["t"] % 2
                cnt["t"] += 1
                P.op("dve", lambda e, A=A, i=i, j=j: e.tensor_tensor(out=tC_[j][:], in0=A[:], in1=xt[i][:], op=ALU.add),
                     reads=[Ak, "xt%d" % i], writes=["tC%d" % j])
                P.dma("pool", outT[dc * 128:(dc + 1) * 128, sl], tC_[j][:], reads=["tC%d" % j], writes=["o_%d_%d" % (dc, s)])
                P.op("act", lambda e, j=j, dc=dc, sl=sl: e.activation(out=xb[:, dc, sl], in_=tC_[j][:], func=AF.Copy),
                     reads=["tC%d" % j], writes=["xb_%d_%d" % (dc, s)])
                P.op("pool", lambda e, j=j: e.tensor_tensor(out=sqb[j][:], in0=tC_[j][:], in1=tC_[j][:], op=ALU.mult),
                     reads=["tC%d" % j], writes=["sqb%d" % j])
                P.op("pe", lambda e, j=j, s=s, dc=dc: e.matmul(ssb[s][:], lhsT=ones_bf[:], rhs=sqb[j][:], start=(dc == 0),
                                                               stop=(dc == 7)), reads=["ones_bf", "sqb%d" % j], writes=["ssb%d" % s])
        for s in range(NS):
            rstd_from("rstd1", s, rstd[1])
        allmb = ["mb_%d_%d" % (dc, s) for dc in range(8) for s in range(NS)] + ["ybf%d_%d" % (k, s) for k in range(6) for s in range(NS)]
        for fc in range(NFC):
            i0 = cnt["st"] % 2
            cnt["st"] += 1
            sv = stg[i0][:, 0:2048].rearrange("p (a b) -> p a b", b=256)
            wv = wcb[i0][:, 0:2048].rearrange("p (a b) -> p a b", b=256)
            P.dma("sp", sv[:, :, 0:128], wgu[:, fc * 128:(fc + 1) * 128].rearrange("(c p) n -> p c n", p=128), writes=["stg%d" % i0])
            P.dma("pool", sv[:, :, 128:256], wgu[:, DFF + fc * 128:DFF + (fc + 1) * 128].rearrange("(c p) n -> p c n", p=128),
                  reads=["stg%d" % i0], writes=["stgx%d_1" % i0])
            P.op("dve", lambda e, sv=sv, wv=wv: e.tensor_tensor(out=wv, in0=sv, in1=g2[:].unsqueeze(2).to_broadcast([128, 8, 256]),
                                                                op=ALU.mult),
                 reads=["stg%d" % i0, "stgx%d_1" % i0, "g2"], writes=["wcb%d" % i0, "stg%d" % i0])
            wk = "wcb%d" % i0
            for s in range(NS):
                sl = slice(s * 512, (s + 1) * 512)
                A, Ak = nb()
                for c in range(8):
                    P.op("pe", lambda e, A=A, c=c, sl=sl, wv=wv: e.matmul(A[:], lhsT=wv[:, c, 0:128], rhs=xb[:, c, sl], start=(c == 0),
                                                                           stop=(c == 7)), reads=[wk, "xb_%d_%d" % (c, s)], writes=[Ak])
                B, Bk = nb()
                for c in range(8):
                    P.op("pe", lambda e, B=B, c=c, sl=sl, wv=wv: e.matmul(B[:], lhsT=wv[:, c, 128:256], rhs=xb[:, c, sl], start=(c == 0),
                                                                           stop=(c == 7)), reads=[wk, "xb_%d_%d" % (c, s)], writes=[Bk])
                i = cnt["t"] % 2
                cnt["t"] += 1
                P.op("dve", lambda e, A=A, i=i, sl=sl: e.tensor_tensor(out=tA[i][:], in0=A[:], in1=rstd[1][:, sl], op=ALU.mult),
                     reads=[Ak, "rstd1_%d" % s], writes=["tA%d" % i])
                P.op("act", lambda e, i=i: e.activation(out=tA[i][:], in_=tA[i][:], func=AF.Silu), reads=["tA%d" % i], writes=["tA%d" % i])
                P.op("dve", lambda e, B=B, i=i, sl=sl: e.tensor_tensor(out=tB[i][:], in0=B[:], in1=rstd[1][:, sl], op=ALU.mult),
                     reads=[Bk, "rstd1_%d" % s], writes=["tB%d" % i])
                P.op("pool", lambda e, i=i, fc=fc, sl=sl: e.tensor_tensor(out=hb[:, fc, sl], in0=tA[i][:], in1=tB[i][:], op=ALU.mult),
                     reads=["tA%d" % i, "tB%d" % i], writes=["hb_%d_%d" % (fc, s)] + (allmb if fc < 14 else []))
        for dc in range(8):
            wdb, wdk = wload(wd[:, dc * 128:(dc + 1) * 128].rearrange("(c p) n -> p c n", p=128), (NFC, 128), eng="act")
            for s in range(NS):
                sl = slice(s * 512, (s + 1) * 512)
                A, Ak = nb()
                for fc in range(NFC):
                    P.op("pe", lambda e, A=A, fc=fc, sl=sl, wdb=wdb: e.matmul(A[:], lhsT=wdb[:, fc, :], rhs=hb[:, fc, sl], start=(fc == 0),
                                                                               stop=(fc == NFC - 1)),
                         reads=[wdk, "hb_%d_%d" % (fc, s)], writes=[Ak])
                i = cnt["xt"] % 2
                cnt["xt"] += 1
                P.dma("sp", xt[i][:], outT[dc * 128:(dc + 1) * 128, sl], reads=["o_%d_%d" % (dc, s)], writes=["xt%d" % i])
                j = cnt["t"] % 2
                cnt["t"] += 1
                P.op("dve", lambda e, A=A, i=i, j=j: e.tensor_tensor(out=tC_[j][:], in0=A[:], in1=xt[i][:], op=ALU.add),
                     reads=[Ak, "xt%d" % i], writes=["tC%d" % j])
                P.dma("pool", outT[dc * 128:(dc + 1) * 128, sl], tC_[j][:], reads=["tC%d" % j], writes=["o_%d_%d" % (dc, s)])


def build_fused(S):
    TCQ = S // 4
    nc = bass.Bass("TRN2", target_bir_lowering=False)
    din = lambda n, shp, dt=F32: nc.dram_tensor(n, shp, dt, kind="ExternalInput").ap()
    xT = din("xT", [D, S])
    wf = din("wf", [2, 4, D, NF]); wt = din("wt", [2, 4, D, NTM]); gmix = din("gmix", [2, 128, 8])
    pp = din("pp", [2, 4, 128, NPP]); bc = din("bc", [2, 4, 128, NBC]); rope = din("rope", [64, 2, S])
    dmat = din("dmat", [4, 128, 128]); w2h = din("w2h", [2, 4, 64, 64]); a2h = din("a2h", [2, 4, 64, 64])
    g2h = din("g2h", [2, 4, 128, 64]); masks = din("masks", [128, 3, 128]); iden = din("iden", [128, 128])
    wg = din("wg", [2, D, 3072]); pw = din("pw", [2, 768, D]); wo = din("wo", [2, D, D])
    wgu = din("wgu", [2, D, 2 * DFF]); wd = din("wd", [2, DFF, D]); g2n = din("g2n", [2, 128, 8])
    outT = nc.dram_tensor("outT", [D, S], F32, kind="ExternalOutput").ap()
    xs1 = nc.dram_tensor("xs1", [D, S], F32).ap()
    ytm = nc.dram_tensor("ytm", [S, 768], F32).ap()
    P = Prog(nc)
    for l in range(2):
        xsrc = xT if l == 0 else xs1
        xdst = xs1 if l == 0 else outT
        for h in range(4):
            A = dict(xT=xsrc, wf=wf[l, h], wt=wt[l, h], gmix=gmix[l], pp=pp[l, h], bc=bc[l, h], rope=rope, dmat=dmat[h],
                     w2h=w2h[l, h], a2h=a2h[l, h], g2h=g2h[l, h], masks=masks, iden=iden,
                     y_ret=ytm[:, h * 64:(h + 1) * 64], y_fox=ytm[:, 256 + h * 64:256 + (h + 1) * 64],
                     y_rw=ytm[:, 512 + h * 64:512 + (h + 1) * 64])
            with ExitStack() as es:
                emit_B(P, nc, es, S, A, "L%dH%d_" % (l, h))
            P.barrier()
        for q in range(4):
            sl = slice(q * TCQ, (q + 1) * TCQ)
            A = dict(xsrc=xsrc[:, sl], ytm=ytm[sl, :], wg=wg[l], pw=pw[l], wo=wo[l], wgu=wgu[l], wd=wd[l], g1=gmix[l],
                     g2=g2n[l], iden=iden, xdst=xdst[:, sl])
            with ExitStack() as es:
                emit_C(P, nc, es, TCQ, A, "L%dQ%d_" % (l, q))
            P.barrier()
    P.emit()
    return nc


RET0, FOX0, RW0, GATE0 = 0, 1024, 1796, 2820

def consts_B(h, S):
    half = 32
    inv_freq = (np.float32(10000.0) ** (-np.arange(half, dtype=np.float32) / np.float32(half))).astype(np.float32)
    pos = np.arange(S, dtype=np.float32)
    ang = (pos[None, :] * inv_freq[:, None]).astype(np.float32)
    cos = np.cos(ang).astype(np.float32); sin = np.sin(ang).astype(np.float32)
    rope = np.zeros((64, 2, S), np.float32)
    rope[0:32, 0] = cos; rope[32:64, 0] = cos
    rope[0:32, 1] = -sin; rope[32:64, 1] = sin
    lg = np.log1p(-np.exp2(np.float32(-5.0 - h))).astype(np.float32)
    idx = np.arange(128, dtype=np.float32)
    dist = idx[None, :] - idx[:, None]
    dmatT = np.where(dist >= 0, np.exp(lg * np.maximum(dist, 0)), 0).astype(np.float32)
    kdec = np.exp(lg * (127 - idx)).astype(np.float32)
    qdec = np.exp(lg * (idx + 1)).astype(np.float32)
    cdec = np.exp(lg * 128).astype(np.float32)
    ii = np.arange(128)
    masks = np.zeros((128, 3, 128), np.float32)
    masks[:, 0, :] = (ii[None, :] > ii[:, None])
    masks[:, 1, :] = (ii[None, :] < ii[:, None])
    masks[:, 2, :] = (ii[None, :] >= ii[:, None])
    iden = np.eye(128, dtype=np.float32)
    return dict(rope=rope, dmat=dmatT, masks=masks, iden=iden), kdec, qdec, cdec

def prep_B(inp, l, b, h, S, x=None):
    cst, kdec, qdec, cdec = consts_B(h, S)
    w_in = inp["w_in"][l]
    hs = slice(h * 64, (h + 1) * 64)
    def col(base, blk):
        return w_in[:, base + blk * 256 + h * 64: base + blk * 256 + (h + 1) * 64]
    def swp(a):
        return np.concatenate([a[:, 32:64], a[:, 0:32]], axis=1)
    rq, rk, rv, rg = (col(RET0, i) for i in range(4))
    fq, fk, fv = (col(FOX0, i) for i in range(3))
    ff = w_in[:, FOX0 + 768 + h: FOX0 + 768 + h + 1]
    wr, wk, wv = (col(RW0, i) for i in range(3))
    wxw = w_in[:, RW0 + 768: RW0 + 832]; wxa = w_in[:, RW0 + 832: RW0 + 896]; wxg = w_in[:, RW0 + 896: RW0 + 1024]
    wf = np.concatenate([rq, rk, swp(rq), swp(rk), fq, fk, ff, wr, wk, wv, wxw, wxa, wxg], axis=1)
    wt = np.concatenate([rv, fv, rg], axis=1)
    assert wf.shape[1] == NF
    mu = inp["rwkv_mu"][l]
    pp = np.zeros((128, NPP), np.float32)
    def setc(n, v):
        v = np.asarray(v, np.float32).reshape(-1)
        if v.size == 1:
            pp[:, PP[n]] = v[0]
        else:
            pp[:v.size, PP[n]] = v
    setc("fqg", inp["fox_q_norm"][l]); setc("fkg", inp["fox_k_norm"][l]); setc("fb", inp["fox_f_bias"][l][h])
    setc("mu_r", mu[0 * 256 + h * 64: 0 * 256 + (h + 1) * 64]); setc("mu_k", mu[256 + h * 64: 256 + (h + 1) * 64])
    setc("mu_v", mu[512 + h * 64: 512 + (h + 1) * 64]); setc("mu_xw", mu[768:832]); setc("mu_xa", mu[832:896])
    setc("mu_xg", mu[896:1024])
    for n, k in (("w0", "rwkv_w0"), ("a0", "rwkv_a0"), ("k_k", "rwkv_k_k"), ("k_a", "rwkv_k_a"), ("r_k", "rwkv_r_k")):
        setc(n, inp[k][l][hs])
    setc("cdec", cdec); setc("kdec", kdec)
    bc = np.zeros((128, NBC), np.float32)
    bc[:, 0:64] = inp["ret_gn"][l][hs][None, :]
    bc[:, 64:128] = inp["rwkv_gn"][l][hs][None, :]
    bc[:, 128:256] = qdec[None, :]
    xx = inp["x"][b] if x is None else x
    m = dict(xT=np.ascontiguousarray(xx[:S].T), wf=np.ascontiguousarray(wf), wt=np.ascontiguousarray(wt),
             gmix=np.ascontiguousarray(inp["mix_norm"][l].reshape(8, 128).T), pp=pp, bc=bc,
             w2h=np.ascontiguousarray(inp["rwkv_w2"][l][:, hs]), a2h=np.ascontiguousarray(inp["rwkv_a2"][l][:, hs]),
             g2h=np.ascontiguousarray(inp["rwkv_g2"][l][:, hs]))
    m.update(cst)
    return m


def prep_fused(inp, b, S):
    per = [[prep_B(inp, l, b, h, S) for h in range(4)] for l in range(2)]
    st = lambda k: np.ascontiguousarray(np.stack([np.stack([per[l][h][k] for h in range(4)]) for l in range(2)]))
    m = dict(xT=per[0][0]["xT"], wf=st("wf"), wt=st("wt"), pp=st("pp"), bc=st("bc"), w2h=st("w2h"), a2h=st("a2h"), g2h=st("g2h"),
             gmix=np.ascontiguousarray(np.stack([per[l][0]["gmix"] for l in range(2)])),
             rope=per[0][0]["rope"], masks=per[0][0]["masks"], iden=per[0][0]["iden"],
             dmat=np.ascontiguousarray(np.stack([per[0][h]["dmat"] for h in range(4)])),
             wg=np.ascontiguousarray(inp["w_in"][:, :, 2820:]),
             pw=np.ascontiguousarray(np.concatenate([inp["p_ret"], inp["p_fox"], inp["p_rwkv"]], axis=1)),
             wo=np.ascontiguousarray(inp["w_out"]), wgu=np.ascontiguousarray(inp["w_gate_up"]), wd=np.ascontiguousarray(inp["w_down"]),
             g2n=np.ascontiguousarray(np.stack([inp["ffn_norm"][l].reshape(8, 128).T for l in range(2)])))
    return m


_NC = {}


def kernel(**inputs):
    inp = {k: np.asarray(v, dtype=np.float32) for k, v in inputs.items()}
    x = inp["x"]
    Bn, S, Dm = x.shape
    if "F" not in _NC:
        _NC["F"] = build_fused(S)
    maps = [prep_fused(inp, b, S) for b in range(Bn)]
    res = run_bass_kernel_spmd(_NC["F"], maps, core_ids=list(range(Bn))).results
    out = np.stack([np.ascontiguousarray(res[b]["outT"].T) for b in range(Bn)], axis=0)
    return np.ascontiguousarray(out, dtype=np.float32)
```

```python
import heapq
import numpy as np
import concourse.bass as bass
import concourse.mybir as mybir
from concourse.bass_utils import run_bass_kernel_spmd

F32 = mybir.dt.float32
BF16 = mybir.dt.bfloat16
AF = mybir.ActivationFunctionType
ALU = mybir.AluOpType
AX = mybir.AxisListType


class Prog:
    ENGS = ("pe", "act", "dve", "pool", "sp")
    NDMA = 8
    DUR = {"pe": 0.30, "act": 0.35, "dve": 0.30, "pool": 0.40}
    DMA_ISSUE = 0.10
    DMA_LAT = 2.5
    HOP = 0.15
    REORDER = True

    def __init__(self, nc):
        self.nc = nc
        self.allops = []
        self.state = {}
        self.markers = {e: None for e in self.ENGS}
        self.bar_start = 0

    @classmethod
    def bank_of(cls, key):
        if key.startswith("rps"):
            return "B_rps"
        if key.startswith("wps"):
            return "B_wps"
        if key.startswith("misc"):
            return "B_misc"
        if key.startswith("facc"):
            return "B_facc"
        if key.startswith("fsc") or key.startswith("pbank") or key.startswith("ssb"):
            return "B_" + key
        return None

    def _add(self, eng, fn, reads, writes, dma, dur):
        banks = []
        for k in tuple(reads) + tuple(writes):
            bk = self.bank_of(k)
            if bk is not None and bk not in banks:
                banks.append(bk)
        writes = tuple(writes) + tuple(banks)
        deps = []
        for b in reads:
            st = self.state.setdefault(b, [None, []])
            if st[0] is not None:
                deps.append(st[0])
        for b in writes:
            st = self.state.setdefault(b, [None, []])
            if st[0] is not None:
                deps.append(st[0])
            deps.extend(st[1])
        hard, soft, seen = [], [], set()
        for d in deps:
            if id(d) in seen:
                continue
            seen.add(id(d))
            if eng == "pe" and d["eng"] == "pe" and not d["dma"]:
                soft.append(d)
            else:
                hard.append(d)
        m = self.markers[eng]
        if m is not None:
            soft.append(m)
        if dur is None:
            dur = self.DMA_LAT if dma else self.DUR[eng]
        op = {"eng": eng, "fn": fn, "hard": hard, "soft": soft, "dma": dma, "dur": dur, "seq": len(self.allops),
              "marker": False}
        self.allops.append(op)
        for b in reads:
            self.state[b][1].append(op)
        for b in writes:
            self.state[b] = [op, []]
        return op

    def op(self, eng, fn, reads=(), writes=(), dur=None):
        return self._add(eng, fn, tuple(reads), tuple(writes), False, dur)

    def dma(self, eng, out, in_, reads=(), writes=(), dur=None):
        return self._add(eng, lambda e: e.dma_start(out=out, in_=in_), tuple(reads), tuple(writes), True, dur)

    def barrier(self):
        prev = self.allops[self.bar_start:]
        start = len(self.allops)
        for e in self.ENGS:
            mk = {"eng": e, "fn": None, "hard": list(prev), "soft": [], "dma": False, "dur": 0.0,
                  "seq": len(self.allops), "marker": True}
            if self.markers[e] is not None:
                mk["soft"].append(self.markers[e])
            self.allops.append(mk)
            self.markers[e] = mk
        self.bar_start = start
        self.state = {}

    def _schedule(self):
        ops = self.allops
        n = len(ops)
        if not self.REORDER:
            return list(ops)
        succ = [[] for _ in range(n)]
        indeg = [0] * n
        for op in ops:
            for d in op["hard"]:
                succ[d["seq"]].append(op["seq"])
            for d in op["soft"]:
                succ[d["seq"]].append(op["seq"])
            indeg[op["seq"]] = len(op["hard"]) + len(op["soft"])
        prio = [0.0] * n
        for i in range(n - 1, -1, -1):
            best = 0.0
            for s in succ[i]:
                if prio[s] > best:
                    best = prio[s]
            prio[i] = best + ops[i]["dur"] + self.HOP
        ready = [0.0] * n
        finish = [0.0] * n
        free_at = {e: 0.0 for e in self.ENGS}
        future = {e: [] for e in self.ENGS}
        avail = {e: [] for e in self.ENGS}
        for op in ops:
            if indeg[op["seq"]] == 0:
                heapq.heappush(future[op["eng"]], (0.0, op["seq"]))
        order = []
        remaining = n
        while remaining:
            best = None
            for e in self.ENGS:
                fu, av = future[e], avail[e]
                t = free_at[e]
                while fu and fu[0][0] <= t:
                    r, i = heapq.heappop(fu)
                    heapq.heappush(av, (-prio[i], i))
                if av:
                    cand = (t, av[0][0], e, True)
                elif fu:
                    cand = (fu[0][0], -prio[fu[0][1]], e, False)
                else:
                    continue
                if best is None or cand[:2] < best[:2]:
                    best = cand
            start, _, e, from_av = best
            if from_av:
                _, i = heapq.heappop(avail[e])
            else:
                _, i = heapq.heappop(future[e])
            op = ops[i]
            if op["dma"]:
                free_at[e] = start + self.DMA_ISSUE
                finish[i] = start + op["dur"]
            else:
                finish[i] = start + op["dur"]
                free_at[e] = finish[i]
            order.append(op)
            remaining -= 1
            for s in succ[i]:
                so = ops[s]
                hop = self.HOP if (so["eng"] != e or op["dma"]) else 0.05
                r = finish[i] + hop
                if r > ready[s]:
                    ready[s] = r
                indeg[s] -= 1
                if indeg[s] == 0:
                    heapq.heappush(future[so["eng"]], (ready[s], s))
        self.est_time = max(finish) if finish else 0.0
        return order

    def emit(self):
        nc = self.nc
        order = self._schedule()
        per_eng = {e: [] for e in self.ENGS}
        for op in order:
            per_eng[op["eng"]].append(op)
        dma_uses = {}
        for e in self.ENGS:
            ncomp = 0
            ndma = 0
            last = {}
            for op in per_eng[e]:
                if op["marker"]:
                    op["sig"] = None
                elif op["dma"]:
                    slot = ndma % self.NDMA
                    ndma += 1
                    if slot in last:
                        op["hard"].append(last[slot])
                    last[slot] = op
                    dma_uses[(e, slot)] = dma_uses.get((e, slot), 0) + 1
                    op["sig"] = (("dma", e, slot), 16 * dma_uses[(e, slot)])
                else:
                    ncomp += 1
                    op["sig"] = (("eng", e), ncomp)
        semkeys = [("eng", e) for e in self.ENGS] + [("dma", e, s) for (e, s) in sorted(dma_uses)]
        for op in order:
            c = {}
            if op["marker"]:
                for d in op["hard"]:
                    if d["marker"]:
                        for k, v in d["clock"].items():
                            if c.get(k, 0) < v:
                                c[k] = v
                    else:
                        k, v = d["sig"]
                        if c.get(k, 0) < v:
                            c[k] = v
                op["clock"] = c
                continue
            for d in op["hard"]:
                for k, v in d["clock"].items():
                    if c.get(k, 0) < v:
                        c[k] = v
            for d in op["soft"]:
                if d["marker"]:
                    for k, v in d["clock"].items():
                        if c.get(k, 0) < v:
                            c[k] = v
            if op["sig"] is not None:
                k, v = op["sig"]
                c[k] = max(c.get(k, 0), v)
            op["clock"] = c
        from contextlib import ExitStack
        with ExitStack() as es:
            sems = {}
            for k in semkeys:
                sems[k] = es.enter_context(nc.semaphore("s_" + "_".join(str(x) for x in k)))
            block = es.enter_context(nc.Block())

            def run_engine(e, engobj):
                know = {}
                for op in per_eng[e]:
                    need = {}
                    for d in op["hard"]:
                        if d["sig"] is None:
                            continue
                        k, v = d["sig"]
                        if know.get(k, 0) >= v:
                            continue
                        if need.get(k, 0) < v:
                            need[k] = v
                    for k, v in need.items():
                        engobj.wait_ge(sems[k], v)
                    if op["marker"]:
                        for k, v in op["clock"].items():
                            if know.get(k, 0) < v:
                                know[k] = v
                        continue
                    for d in op["hard"]:
                        for k, v in d["clock"].items():
                            if know.get(k, 0) < v:
                                know[k] = v
                    for d in op["soft"]:
                        if d["marker"]:
                            for k, v in d["clock"].items():
                                if know.get(k, 0) < v:
                                    know[k] = v
                    if op["marker"]:
                        continue
                    ins = op["fn"](engobj)
                    k, v = op["sig"]
                    ins.then_inc(sems[k], 16 if op["dma"] else 1)
                for (ee, s), cnt in dma_uses.items():
                    if ee == e:
                        engobj.wait_ge(sems[("dma", e, s)], 16 * cnt)

            if per_eng["sp"]:
                @block.sync
                def _(eng):
                    run_engine("sp", eng)
            if per_eng["pe"]:
                @block.tensor
                def _(eng):
                    run_engine("pe", eng)
            if per_eng["act"]:
                @block.scalar
                def _(eng):
                    run_engine("act", eng)
            if per_eng["dve"]:
                @block.vector
                def _(eng):
                    run_engine("dve", eng)
            if per_eng["pool"]:
                @block.gpsimd
                def _(eng):
                    run_engine("pool", eng)

from contextlib import ExitStack


D = 1024
HD = 64
TT = 512
RW_RATE = 4
FG = [("rq", 64), ("rk", 64), ("rqs", 64), ("rks", 64), ("fq", 64), ("fk", 64), ("ff", 1),
      ("wr", 64), ("wk", 64), ("wv", 64), ("wxw", 64), ("wxa", 64), ("wxg", 128)]
FOFF = {}
_o = 0
for _n, _w in FG:
    FOFF[_n] = (_o, _w)
    _o += _w
NF = _o
NTM = 192
PPN = ["fqg", "fkg", "fb", "mu_r", "mu_k", "mu_v", "mu_xw", "mu_xa", "mu_xg", "w0", "a0", "k_k", "k_a",
       "r_k", "cdec", "kdec"]
PP = {n: i for i, n in enumerate(PPN)}
NPP = len(PPN)
NBC = 256


def build_B(S, dbg=False):
    NT = S // TT
    nc = bass.Bass("TRN2", target_bir_lowering=False)
    din = lambda n, shp, dt=F32: nc.dram_tensor(n, shp, dt, kind="ExternalInput").ap()
    dout = lambda n, shp, dt=F32: nc.dram_tensor(n, shp, dt, kind="ExternalOutput").ap()
    xT = din("xT", [D, S])
    wf = din("wf", [D, NF])
    wt = din("wt", [D, NTM])
    gmix = din("gmix", [128, 8])
    pp_d = din("pp", [128, NPP])
    bc_d = din("bc", [128, NBC])
    rope_d = din("rope", [64, 2, S])
    dmat_d = din("dmat", [128, 128])
    w2_d = din("w2h", [64, 64])
    a2_d = din("a2h", [64, 64])
    g2_d = din("g2h", [128, 64])
    msk_d = din("masks", [128, 3, 128])
    iden_d = din("iden", [128, 128])
    y_ret = dout("y_ret", [S, 64])
    y_fox = dout("y_fox", [S, 64])
    y_rw = dout("y_rw", [S, 64])

    A = dict(xT=xT, wf=wf, wt=wt, gmix=gmix, pp=pp_d, bc=bc_d, rope=rope_d, dmat=dmat_d, w2h=w2_d, a2h=a2_d, g2h=g2_d,
             masks=msk_d, iden=iden_d, y_ret=y_ret, y_fox=y_fox, y_rw=y_rw)
    P = Prog(nc)
    with ExitStack() as es:
        emit_B(P, nc, es, S, A, "")
    P.emit()
    return nc


def emit_B(P, nc, es, S, A, pfx):
    NT = S // TT
    xT, wf, wt, gmix, pp_d, bc_d, rope_d, dmat_d = (A[k] for k in ("xT", "wf", "wt", "gmix", "pp", "bc", "rope", "dmat"))
    w2_d, a2_d, g2_d, msk_d, iden_d, y_ret, y_fox, y_rw = (A[k] for k in ("w2h", "a2h", "g2h", "masks", "iden", "y_ret", "y_fox", "y_rw"))
    if True:
        def sb(name, shape, dt=F32):
            return es.enter_context(nc.sbuf_tensor("sb_" + pfx + name, shape, dt))

        def ps(name, shape, dt=F32):
            return es.enter_context(nc.psum_tensor("ps_" + pfx + name, shape, dt))

        pp = sb("pp", [128, NPP]); bc = sb("bc", [128, NBC]); gm = sb("gm", [128, 8])
        dmat = sb("dmat", [128, 128]); w2h = sb("w2h", [64, 64]); a2h = sb("a2h", [64, 64])
        g2h = sb("g2h", [128, 64]); msk = sb("msk", [128, 3, 128]); iden = sb("iden", [128, 128])
        for t, d, k in ((pp, pp_d, "pp"), (bc, bc_d, "bc"), (gm, gmix, "gm"), (dmat, dmat_d, "dmat"),
                        (w2h, w2_d, "w2h"), (a2h, a2_d, "a2h"), (g2h, g2_d, "g2h"), (msk, msk_d, "msk"),
                        (iden, iden_d, "iden")):
            P.dma("sp", t[:], d, writes=[k])
        ppc = lambda n, np_=64: pp[0:np_, PP[n]:PP[n] + 1]
        iden_bf = sb("iden_bf", [128, 128], BF16)
        P.op("dve", lambda e: e.tensor_copy(out=iden_bf[:], in_=iden[:]), reads=["iden"], writes=["iden_bf"])
        mski_bf = sb("mski_bf", [128, 128], BF16)
        P.op("dve", lambda e: e.tensor_copy(out=mski_bf[:], in_=msk[:, 2, :]), reads=["msk"], writes=["mski_bf"])
        ones_bf = sb("ones_bf", [128, 128], BF16)
        P.op("pool", lambda e: e.memset(ones_bf[:], 1.0), writes=["ones_bf"])
        mask_one = sb("mask_one", [1, TT])
        P.op("pool", lambda e: e.memset(mask_one[:], 1.0), writes=["mask_one"])
        ones_f = sb("ones_f", [128, 64])
        P.op("pool", lambda e: e.memset(ones_f[:], 1.0), writes=["ones_f"])
        mask64 = sb("mask64", [64, TT])
        P.op("pool", lambda e: e.memset(mask64[:], 1.0), writes=["mask64"])
        P.op("pool", lambda e: e.memset(mask64[:].rearrange("p (c j) -> p c j", j=128)[:, :, 0:1], 0.0),
             writes=["mask64"])
        der = sb("der", [128, 4])
        P.op("dve", lambda e: e.tensor_scalar(out=der[0:64, 0:1], in0=ppc("fqg"), scalar1=0.125, scalar2=None,
                                              op0=ALU.mult), reads=["pp"], writes=["der0"])
        P.op("dve", lambda e: e.tensor_scalar(out=der[0:64, 1:2], in0=ppc("k_a"), scalar1=-1.0, scalar2=1.0,
                                              op0=ALU.mult, op1=ALU.add), reads=["pp"], writes=["der1"])
        P.op("dve", lambda e: e.tensor_scalar(out=der[0:1, 2:3], in0=pp[0:1, PP["fb"]:PP["fb"] + 1], scalar1=-1.0,
                                              scalar2=None, op0=ALU.mult), reads=["pp"], writes=["der2"])

        wfb = sb("wfb", [128, 8, NF], BF16)
        wtb = sb("wtb", [128, 8, NTM], BF16)
        xs1 = sb("xs", [128, 8, TT])
        wst = [xs1[:].rearrange("p c t -> p (c t)")[:, i * 2048:i * 2048 + NF + NTM] for i in range(2)]
        for c in range(8):
            st = wst[c % 2]
            k = "wst%d" % (c % 2)
            P.dma("sp", st[:, 0:NF], wf[c * 128:(c + 1) * 128, :], reads=["xsall"], writes=[k + "a"])
            P.dma("sp", st[:, NF:NF + NTM], wt[c * 128:(c + 1) * 128, :], reads=["xsall"], writes=[k + "b"])
            P.op("dve", lambda e, st=st, c=c: e.tensor_scalar(out=wfb[:, c, :], in0=st[:, 0:NF], scalar1=gm[:, c:c + 1],
                                                              scalar2=None, op0=ALU.mult),
                 reads=[k + "a", "gm"], writes=["wfb"])
            P.op("act", lambda e, st=st, c=c: e.activation(out=wtb[:, c, :], in_=st[:, NF:NF + NTM], func=AF.Copy,
                                                           scale=gm[:, c:c + 1]),
                 reads=[k + "b", "gm"], writes=["wtb"])
        for g in ("rk", "rks"):
            o, w = FOFF[g]
            P.op("dve", lambda e, o=o, w=w: e.tensor_scalar(out=wfb[:, :, o:o + w], in0=wfb[:, :, o:o + w], scalar1=0.125,
                                                            scalar2=None, op0=ALU.mult), reads=["wfb"], writes=["wfb"])

        xs = [xs1, xs1]
        xb = sb("xb", [128, 8, TT], BF16)
        sqr = [sb("sq%d" % i, [128, TT], BF16) for i in range(2)]
        rstd = sb("rstd", [128, TT])
        rstd_tm = sb("rstd_tm", [128, 4])
        gbuf = {}
        for n, w in FG:
            gbuf[n] = sb("g_" + n, [w, TT + 1])
        tm = sb("tm", [128, 4, NTM])
        ropet = sb("ropet", [64, 2, TT])
        pbank = [ps("pbank%d" % i, [128, 512]) for i in range(2)]
        misc = ps("misc", [128, 512])
        fsc = [ps("fsc%d" % i, [128, 512]) for i in range(2)]
        facc = ps("facc", [128, 4, 128])
        rps = ps("rps", [128, 512])
        wps = ps("wps", [128, 4, 128])

        Rst = sb("Rst", [64, 64]); Rbf = sb("Rbf", [64, 64], BF16)
        P.op("pool", lambda e: e.memset(Rst[:], 0.0), writes=["Rst"])
        P.op("pool", lambda e: e.memset(Rbf[:], 0.0), writes=["Rbf"])
        KT = sb("KT", [70, S], BF16)
        QT = sb("QT", [70, TT], BF16)
        VA = sb("VA", [128, S // 128, 65], BF16)
        P.op("pool", lambda e: e.memset(KT[64:70, :], 1.0), writes=["KTaug"])
        P.op("pool", lambda e: e.memset(QT[64:70, :], -1.0), writes=["QTaug"])
        P.op("pool", lambda e: e.memset(VA[:, :, 64:65], 1.0), writes=["VAone"])
        ccar = sb("ccar", [1, 1])
        P.op("pool", lambda e: e.memset(ccar[:], 0.0), writes=["ccar"])
        Sst = sb("Sst", [64, 5, 64])
        P.op("pool", lambda e: e.memset(Sst[:, 0, :], 0.0), writes=["S0"])
        for n in ("wr", "wk", "wv", "wxw", "wxa", "wxg"):
            P.op("pool", lambda e, n=n: e.memset(gbuf[n][:, 0:1], 0.0), writes=["g_" + n])

        pcount = [0]

        def proj_bank():
            i = pcount[0] % 2
            pcount[0] += 1
            return pbank[i], "pbank%d" % i

        rw = {}
        for n in ("mr", "mk", "mv", "mxw", "mxa"):
            rw[n] = sb("rw_" + n, [64, TT])
        rw["mxg"] = sb("rw_mxg", [128, TT])
        for n in ("t1", "t2", "t3", "t4", "cl", "asb", "kk", "rkb"):
            rw[n] = sb("rw_" + n, [64, TT])
        for n in ("At", "Bt", "Kt", "Rt", "Bh", "Kh", "mvb"):
            rw[n] = sb("rw_" + n, [64, TT], BF16)
        sgx = sb("rw_sgx", [128, TT], BF16)
        d128 = sb("rw_d128", [128, TT])
        g2h_bf = sb("g2h_bf", [128, 64], BF16)
        P.op("dve", lambda e: e.tensor_copy(out=g2h_bf[:], in_=g2h[:]), reads=["g2h"], writes=["g2h_bf"])

        def proj_gen(n):
            t0 = n * TT
            X = xs[n % 2]
            xk = "xs"
            for c in range(8):
                P.dma("sp" if c % 2 == 0 else "pool", X[:, c, :], xT[c * 128:(c + 1) * 128, t0:t0 + TT],
                      writes=[xk + "_%d" % c] + (["wst0a", "wst0b", "wst1a", "wst1b"] if n == 0 else []))
            P.dma("sp", ropet[:], rope_d[:, :, t0:t0 + TT], writes=["ropet"])
            xr = [xk + "_%d" % c for c in range(8)]
            P.op("act", lambda e, X=X: e.activation(out=xb[:], in_=X[:], func=AF.Copy), reads=xr, writes=["xb"])
            for c in range(8):
                P.op("pool", lambda e, X=X, c=c: e.tensor_tensor(out=sqr[c % 2][:], in0=X[:, c, :], in1=X[:, c, :],
                                                                 op=ALU.mult), reads=[xr[c]], writes=["sq%d" % (c % 2)])
                P.op("pe", lambda e, c=c: e.matmul(misc[:], lhsT=ones_bf[:], rhs=sqr[c % 2][:], start=(c == 0),
                                                   stop=(c == 7)), reads=["ones_bf", "sq%d" % (c % 2)], writes=["misc"])
            P.op("act", lambda e: e.activation(out=rstd[:], in_=misc[:], func=AF.Ln, scale=1.0 / D, bias=1e-6),
                 reads=["misc"], writes=["rstd"])
            P.op("act", lambda e: e.activation(out=rstd[:], in_=rstd[:], func=AF.Exp, scale=-0.5), reads=["rstd"],
                 writes=["rstd"])
            for s in range(4):
                P.op("pe", lambda e, s=s: e.matmul(misc[:, s:s + 1], lhsT=rstd[0:1, s * 128:(s + 1) * 128],
                                                   rhs=ones_f[0:1, 0:1], start=True, stop=True),
                     reads=["rstd", "ones_f"], writes=["misc"])
            P.op("dve", lambda e: e.tensor_copy(out=rstd_tm[:], in_=misc[:, 0:4]), reads=["misc"], writes=["rstd_tm"])
            yield
            for gname, gw in FG:
                o, w = FOFF[gname]
                bank, bk = proj_bank()
                for c in range(8):
                    P.op("pe", lambda e, c=c, o=o, w=w, bank=bank: e.matmul(bank[0:w, :], lhsT=wfb[:, c, o:o + w],
                                                                            rhs=xb[:, c, :], start=(c == 0), stop=(c == 7)),
                         reads=["wfb", "xb"], writes=[bk])
                P.op("dve", lambda e, w=w, bank=bank, gname=gname: e.tensor_tensor(
                    out=gbuf[gname][:, 1:TT + 1], in0=bank[0:w, :], in1=rstd[0:w, :], op=ALU.mult),
                    reads=[bk, "rstd"], writes=["g_" + gname])
                yield
            for s in range(4):
                bank, bk = proj_bank()
                for c in range(8):
                    P.op("pe", lambda e, c=c, s=s, bank=bank: e.matmul(bank[:, 0:NTM], lhsT=xb[:, c, s * 128:(s + 1) * 128],
                                                                       rhs=wtb[:, c, :], start=(c == 0), stop=(c == 7)),
                         reads=["wtb", "xb"], writes=[bk])
                P.op("act", lambda e, s=s, bank=bank: e.activation(out=tm[:, s, :], in_=bank[:, 0:NTM], func=AF.Copy,
                                                                   scale=rstd_tm[:, s:s + 1]),
                     reads=[bk, "rstd_tm"], writes=["tm%d" % s])
                yield


        def run_all(items):
            active = [list(it) for it in items]
            rnd = 0
            while active:
                for it in list(active):
                    g, m, k = it
                    if rnd % k:
                        continue
                    for _ in range(m):
                        try:
                            next(g)
                        except StopIteration:
                            active.remove(it)
                            break
                rnd += 1

        L = dict(locals())
        run_all([(proj_gen(0), 1, 1)])
        for n in range(NT):
            ret_head(P, nc, n, L)
            fox_head(P, nc, n, L)
            rwkv_head(P, nc, n, L)
            items = [(rwkv_body(P, nc, n, L), RW_RATE, 1), (fox_body(P, nc, n, L), 1, 1), (ret_body(P, nc, n, L), 1, 4)]
            if n + 1 < NT:
                items.append((proj_gen(n + 1), 1, 3))
            run_all(items)


def ret_head(P, nc, n, L):
    gbuf, ropet, tm, rps, dmat, bc, pp = L["gbuf"], L["ropet"], L["tm"], L["rps"], L["dmat"], L["bc"], L["pp"]
    Rst, Rbf, iden_bf, y_ret, sb = L["Rst"], L["Rbf"], L["iden_bf"], L["y_ret"], L["sb"]
    ppc = L["ppc"]
    if n == 0:
        R = {}
        R["t1"] = sb("r_t1", [64, TT]); R["t2"] = sb("r_t2", [64, TT])
        R["qT"] = sb("r_qT", [64, TT], BF16); R["kT"] = sb("r_kT", [64, TT], BF16); R["qd"] = sb("r_qd", [64, TT], BF16)
        R["kd"] = sb("r_kd", [128, 64], BF16); R["sc"] = sb("r_sc", [128, 128], BF16)
        R["y"] = sb("r_y", [128, 64]); R["st"] = sb("r_st", [128, 4]); R["yc"] = sb("r_yc", [128, 64])
        R["junk"] = sb("r_junk", [128, 64]); R["yo"] = sb("r_yo", [128, 64])
        R["vall"] = sb("r_vall", [128, 4, 64], BF16); R["sgall"] = sb("r_sgall", [128, 4, 64])
        ret_head.R = R
    R = ret_head.R
    t0 = n * TT
    cosv, sinv = ropet[:, 0, :], ropet[:, 1, :]
    for nm, a, b_ in (("qT", "rq", "rqs"), ("kT", "rk", "rks")):
        P.op("pool", lambda e, a=a: e.tensor_tensor(out=R["t1"][:], in0=gbuf[a][:, 1:TT + 1], in1=cosv, op=ALU.mult),
             reads=["g_" + a, "ropet"], writes=["r_t1"])
        P.op("pool", lambda e, b_=b_: e.tensor_tensor(out=R["t2"][:], in0=gbuf[b_][:, 1:TT + 1], in1=sinv, op=ALU.mult),
             reads=["g_" + b_, "ropet"], writes=["r_t2"])
        P.op("dve", lambda e, nm=nm: e.tensor_tensor(out=R[nm][:], in0=R["t1"][:], in1=R["t2"][:], op=ALU.add),
             reads=["r_t1", "r_t2"], writes=["r_" + nm])
    P.op("pool", lambda e: e.tensor_tensor(out=R["qd"][:].rearrange("p (c j) -> p c j", j=128),
                                           in0=R["qT"][:].rearrange("p (c j) -> p c j", j=128),
                                           in1=bc[0:64, 128:256].unsqueeze(1).to_broadcast([64, 4, 128]), op=ALU.mult),
         reads=["r_qT", "bc"], writes=["r_qd"])
    tmk = ["tm%d" % s for s in range(4)]
    P.op("pool", lambda e: e.tensor_copy(out=R["vall"][:], in_=tm[:, :, 0:64]), reads=tmk, writes=["r_vall"])
    P.op("act", lambda e: e.activation(out=R["sgall"][:], in_=tm[:, :, 128:192], func=AF.Silu), reads=tmk, writes=["r_sgall"])
    P.op("pool", lambda e: e.tensor_tensor(out=R["sgall"][:], in0=R["sgall"][:],
                                           in1=bc[:, 0:64].unsqueeze(1).to_broadcast([128, 4, 64]), op=ALU.mult),
         reads=["r_sgall", "bc"], writes=["r_sgall"])


def ret_body(P, nc, n, L):
    gbuf, ropet, tm, rps, dmat, bc, pp = L["gbuf"], L["ropet"], L["tm"], L["rps"], L["dmat"], L["bc"], L["pp"]
    Rst, Rbf, iden_bf, y_ret, sb = L["Rst"], L["Rbf"], L["iden_bf"], L["y_ret"], L["sb"]
    ppc = L["ppc"]
    R = ret_head.R
    t0 = n * TT
    for s in range(4):
        cs = slice(s * 128, (s + 1) * 128)
        P.op("pe", lambda e, cs=cs: e.matmul(rps[:, 320:384], lhsT=R["kT"][:, cs], rhs=iden_bf[0:64, 0:64], start=True,
                                             stop=True), reads=["r_kT", "iden_bf"], writes=["rps_k"])
        P.op("dve", lambda e: e.tensor_scalar(out=R["kd"][:], in0=rps[:, 320:384], scalar1=ppc("kdec", 128), scalar2=None,
                                              op0=ALU.mult), reads=["rps_k", "pp"], writes=["r_kd"])
        P.op("pe", lambda e, cs=cs: e.matmul(rps[:, 0:128], lhsT=R["kT"][:, cs], rhs=R["qT"][:, cs], start=True, stop=True),
             reads=["r_kT", "r_qT"], writes=["rps_s"])
        P.op("dve", lambda e: e.tensor_tensor(out=R["sc"][:], in0=rps[:, 0:128], in1=dmat[:], op=ALU.mult),
             reads=["rps_s", "dmat"], writes=["r_sc"])
        P.op("pe", lambda e, s=s: e.matmul(rps[:, 128:192], lhsT=R["sc"][:], rhs=R["vall"][:, s, :], start=True, stop=False),
             reads=["r_sc", "r_vall"], writes=["rps_y"])
        P.op("pe", lambda e, cs=cs: e.matmul(rps[:, 128:192], lhsT=R["qd"][:, cs], rhs=Rbf[:], start=False, stop=True),
             reads=["r_qd", "Rbf"], writes=["rps_y"])
        P.op("pe", lambda e, s=s: e.matmul(rps[0:64, 192:256], lhsT=R["kd"][:], rhs=R["vall"][:, s, :], start=True, stop=True),
             reads=["r_kd", "r_vall"], writes=["rps_u"])
        yield
        P.op("dve", lambda e: e.scalar_tensor_tensor(out=Rst[:], in0=Rst[:], scalar=ppc("cdec"), in1=rps[0:64, 192:256],
                                                     op0=ALU.mult, op1=ALU.add), reads=["Rst", "rps_u", "pp"], writes=["Rst"])
        P.op("act", lambda e: e.activation(out=Rbf[:], in_=Rst[:], func=AF.Copy), reads=["Rst"], writes=["Rbf"])
        yield
        st = R["st"]
        P.op("act", lambda e: e.activation(out=R["y"][:], in_=rps[:, 128:192], func=AF.Copy, accum_out=st[:, 0:1]),
             reads=["rps_y"], writes=["r_y", "r_st0"])
        P.op("dve", lambda e: e.tensor_scalar(out=st[:, 1:2], in0=st[:, 0:1], scalar1=1.0 / 64, scalar2=None, op0=ALU.mult),
             reads=["r_st0"], writes=["r_st1"])
        P.op("dve", lambda e: e.tensor_scalar(out=R["yc"][:], in0=R["y"][:], scalar1=st[:, 1:2], scalar2=None,
                                              op0=ALU.subtract), reads=["r_y", "r_st1"], writes=["r_yc"])
        P.op("act", lambda e: e.activation(out=R["junk"][:], in_=R["yc"][:], func=AF.Square, accum_out=st[:, 2:3]),
             reads=["r_yc"], writes=["r_junk", "r_st2"])
        P.op("act", lambda e: e.activation(out=st[:, 3:4], in_=st[:, 2:3], func=AF.Ln, scale=1.0 / 64, bias=1e-5),
             reads=["r_st2"], writes=["r_st3"])
        P.op("act", lambda e: e.activation(out=st[:, 3:4], in_=st[:, 3:4], func=AF.Exp, scale=-0.5), reads=["r_st3"],
             writes=["r_st3"])
        P.op("dve", lambda e, s=s: e.scalar_tensor_tensor(out=R["yo"][:], in0=R["yc"][:], scalar=st[:, 3:4], in1=R["sgall"][:, s, :],
                                                          op0=ALU.mult, op1=ALU.mult), reads=["r_yc", "r_st3", "r_sgall"],
             writes=["r_yo"])
        P.dma("sp", y_ret[t0 + s * 128:t0 + (s + 1) * 128, :], R["yo"][:], reads=["r_yo"])
        yield


def fox_head(P, nc, n, L):
    gbuf, tm, misc, pp, der = L["gbuf"], L["tm"], L["misc"], L["pp"], L["der"]
    KT, QT, VA, fsc, facc, ccar = L["KT"], L["QT"], L["VA"], L["fsc"], L["facc"], L["ccar"]
    ones_bf, ones_f, mski_bf, y_fox, sb, ppc = L["ones_bf"], L["ones_f"], L["mski_bf"], L["y_fox"], L["sb"], L["ppc"]
    if n == 0:
        F = {}
        F["sq"] = sb("f_sq", [64, TT], BF16); F["rs"] = sb("f_rs", [64, TT])
        F["e"] = sb("f_e", [1, TT]); F["c"] = sb("f_c", [1, TT]); F["r1"] = sb("f_r1", [1, TT])
        F["pc"] = sb("f_pc", [1, 3, TT], BF16)
        F["pT"] = [sb("f_pT%d" % i, [128, TT], BF16) for i in range(2)]
        F["acc"] = sb("f_acc", [128, 4, 65]); F["rec"] = sb("f_rec", [128, 4, 1]); F["yo"] = sb("f_yo", [128, 4, 64])
        F["cnt"] = 0
        fox_head.F = F
    F = fox_head.F
    t0 = n * TT
    for src, gcol, dst, dk in (("fq", der[0:64, 0:1], QT[0:64, :], "QTm"), ("fk", ppc("fkg"), KT[0:64, t0:t0 + TT], "KTm")):
        P.op("act", lambda e, src=src: e.activation(out=F["sq"][:], in_=gbuf[src][:, 1:TT + 1], func=AF.Square),
             reads=["g_" + src], writes=["f_sq"])
        P.op("pe", lambda e: e.matmul(misc[0:64, :], lhsT=ones_bf[0:64, 0:64], rhs=F["sq"][:], start=True, stop=True),
             reads=["ones_bf", "f_sq"], writes=["misc"])
        P.op("act", lambda e: e.activation(out=F["rs"][:], in_=misc[0:64, :], func=AF.Ln, scale=1.0 / 64, bias=1e-6),
             reads=["misc"], writes=["f_rs"])
        P.op("act", lambda e: e.activation(out=F["rs"][:], in_=F["rs"][:], func=AF.Exp, scale=-0.5), reads=["f_rs"],
             writes=["f_rs"])
        P.op("dve", lambda e, src=src, gcol=gcol, dst=dst: e.scalar_tensor_tensor(
            out=dst, in0=gbuf[src][:, 1:TT + 1], scalar=gcol, in1=F["rs"][:], op0=ALU.mult, op1=ALU.mult),
            reads=["g_" + src, "f_rs", "pp", "der0"], writes=[dk])
    P.op("act", lambda e: e.activation(out=F["e"][:], in_=gbuf["ff"][:, 1:TT + 1], func=AF.Exp, scale=-1.0,
                                       bias=der[0:1, 2:3]), reads=["g_ff", "der2"], writes=["f_e"])
    P.op("act", lambda e: e.activation(out=F["e"][:], in_=F["e"][:], func=AF.Ln, bias=1.0), reads=["f_e"], writes=["f_e"])
    P.op("dve", lambda e: e.tensor_tensor_scan(out=F["c"][:], data0=L["mask_one"][:],
                                               data1=F["e"][:], initial=ccar[:], op0=ALU.mult, op1=ALU.subtract),
         reads=["f_e", "ccar", "mask_one"], writes=["f_c"])
    P.op("dve", lambda e: e.tensor_copy(out=ccar[:], in_=F["c"][:, TT - 1:TT]), reads=["f_c"], writes=["ccar"])
    pc = F["pc"]
    P.op("dve", lambda e: e.tensor_copy(out=pc[:, 0, :], in_=F["c"][:]), reads=["f_c"], writes=["f_pc0"])
    P.op("dve", lambda e: e.tensor_tensor(out=F["r1"][:], in0=F["c"][:], in1=pc[:, 0, :], op=ALU.subtract),
         reads=["f_c", "f_pc0"], writes=["f_r1"])
    P.op("dve", lambda e: e.tensor_copy(out=pc[:, 1, :], in_=F["r1"][:]), reads=["f_r1"], writes=["f_pc1"])
    P.op("dve", lambda e: e.tensor_tensor(out=F["r1"][:], in0=F["r1"][:], in1=pc[:, 1, :], op=ALU.subtract),
         reads=["f_r1", "f_pc1"], writes=["f_r1"])
    P.op("dve", lambda e: e.tensor_copy(out=pc[:, 2, :], in_=F["r1"][:]), reads=["f_r1"], writes=["f_pc2"])
    pcr = ["f_pc0", "f_pc1", "f_pc2"]
    P.dma("sp", QT[64:67, :], pc[0:1, :, :], reads=pcr + ["QTaug"], writes=["QTc"])
    P.dma("sp", KT[67:70, t0:t0 + TT], pc[0:1, :, :], reads=pcr + ["KTaug"], writes=["KTc"])
    for s in range(4):
        P.op("pool", lambda e, s=s: e.tensor_copy(out=VA[:, 4 * n + s, 0:64], in_=tm[:, s, 64:128]), reads=["tm%d" % s],
             writes=["VAv"])


def fox_body(P, nc, n, L):
    gbuf, tm, misc, pp, der = L["gbuf"], L["tm"], L["misc"], L["pp"], L["der"]
    KT, QT, VA, fsc, facc, ccar = L["KT"], L["QT"], L["VA"], L["fsc"], L["facc"], L["ccar"]
    ones_bf, ones_f, mski_bf, y_fox, sb, ppc = L["ones_bf"], L["ones_f"], L["mski_bf"], L["y_fox"], L["sb"], L["ppc"]
    F = fox_head.F
    t0 = n * TT
    nk = 4 * n + 4
    P.op("dve", lambda e: e.memset(facc[:, :, 0:65], 0.0), writes=["facc"])
    for kt in range(nk):
        jmin = max(0, kt - 4 * n)
        c0 = jmin * 128
        i = F["cnt"] % 2
        F["cnt"] += 1
        sc, sk, pT, pk = fsc[i], "fsc%d" % i, F["pT"][i], "f_pT%d" % i
        P.op("pe", lambda e, sc=sc, kt=kt, c0=c0: e.matmul(sc[:, c0:TT], lhsT=KT[0:70, kt * 128:(kt + 1) * 128],
                                                           rhs=QT[0:70, c0:TT], start=True, stop=True),
             reads=["KTm", "KTc", "KTaug", "QTm", "QTc", "QTaug"], writes=[sk])
        P.op("act", lambda e, sc=sc, pT=pT, c0=c0: e.activation(out=pT[:, c0:TT], in_=sc[:, c0:TT], func=AF.Exp),
             reads=[sk], writes=[pk])
        if kt >= 4 * n:
            P.op("pool", lambda e, pT=pT, c0=c0: e.tensor_tensor(out=pT[:, c0:c0 + 128], in0=pT[:, c0:c0 + 128],
                                                                 in1=mski_bf[:], op=ALU.mult),
                 reads=[pk, "mski_bf"], writes=[pk])
        for qs in range(jmin, 4):
            P.op("pe", lambda e, pT=pT, qs=qs, kt=kt: e.matmul(facc[:, qs, 0:65], lhsT=pT[:, qs * 128:(qs + 1) * 128],
                                                               rhs=VA[:, kt, :], start=False, stop=(kt == 4 * n + qs), skip_group_check=True),
                 reads=[pk, "VAv", "VAone"], writes=["facc"])
        yield
    P.op("act", lambda e: e.activation(out=F["acc"][:], in_=facc[:, :, 0:65], func=AF.Copy), reads=["facc"], writes=["f_acc"])
    P.op("dve", lambda e: e.reciprocal(out=F["rec"][:], in_=F["acc"][:, :, 64:65]), reads=["f_acc"], writes=["f_rec"])
    P.op("dve", lambda e: e.tensor_tensor(out=F["yo"][:], in0=F["acc"][:, :, 0:64],
                                          in1=F["rec"][:].to_broadcast([128, 4, 64]), op=ALU.mult),
         reads=["f_acc", "f_rec"], writes=["f_yo"])
    P.dma("sp", y_fox[t0:t0 + TT, :].rearrange("(s p) d -> p s d", p=128), F["yo"][:], reads=["f_yo"])


def rwkv_head(P, nc, n, L):
    gbuf, rw, sgx, misc, wps, pp, der, bc = L["gbuf"], L["rw"], L["sgx"], L["misc"], L["wps"], L["pp"], L["der"], L["bc"]
    w2h, a2h, g2h, msk, iden, ones_f, mask64, Sst = (L[k] for k in ("w2h", "a2h", "g2h", "msk", "iden", "ones_f", "mask64", "Sst"))
    y_rw, sb, ppc = L["y_rw"], L["sb"], L["ppc"]
    if n == 0:
        W = {}
        for nm in ("Na", "Nb", "La", "Lb", "Ta", "Tb", "AkT", "ArbT", "ArkT"):
            W[nm] = sb("w_" + nm, [128, 128], BF16)
        for nm in ("X2", "M1", "M2", "Atm", "Bhtm", "Khtm", "Vtm"):
            W[nm] = sb("w_" + nm, [128, 64], BF16)
        for nm in ("GT", "H"):
            W[nm] = sb("w_" + nm, [64, 64])
        W["QtT"] = sb("w_QtT", [64, 128])
        for nm in ("y", "yc", "junk", "o1", "o2", "yo"):
            W[nm] = sb("w_" + nm, [128, 64])
        W["PCs"] = sb("w_PCs", [64, 4]); W["st"] = sb("w_st", [128, 6])
        W["slot"] = 0
        rwkv_head.W = W
    W = rwkv_head.W
    t0 = n * TT
    i64 = iden[0:64, 0:64]
    for g, m, mu in (("wr", "mr", "mu_r"), ("wk", "mk", "mu_k"), ("wv", "mv", "mu_v"), ("wxw", "mxw", "mu_xw"),
                     ("wxa", "mxa", "mu_xa"), ("wxg", "mxg", "mu_xg")):
        w = 128 if g == "wxg" else 64
        tmp = L["d128"] if g == "wxg" else rw["t1"]
        tk = "d128" if g == "wxg" else "rw_t1"
        gb = gbuf[g]
        P.op("pool", lambda e, gb=gb, tmp=tmp: e.tensor_tensor(out=tmp[:], in0=gb[:, 0:TT], in1=gb[:, 1:TT + 1], op=ALU.subtract),
             reads=["g_" + g], writes=[tk])
        P.op("dve", lambda e, gb=gb, tmp=tmp, m=m, mu=mu, w=w: e.scalar_tensor_tensor(
            out=rw[m][:], in0=tmp[:], scalar=ppc(mu, w), in1=gb[:, 1:TT + 1], op0=ALU.mult, op1=ALU.add),
            reads=[tk, "g_" + g, "pp"], writes=["rw_" + m])
        P.op("pool", lambda e, gb=gb: e.tensor_copy(out=gb[:, 0:1], in_=gb[:, TT:TT + 1]), reads=[], writes=["g_" + g])


def rwkv_body(P, nc, n, L):
    gbuf, rw, sgx, misc, wps, pp, der, bc = L["gbuf"], L["rw"], L["sgx"], L["misc"], L["wps"], L["pp"], L["der"], L["bc"]
    w2h, a2h, g2h, msk, iden, ones_f, mask64, Sst = (L[k] for k in ("w2h", "a2h", "g2h", "msk", "iden", "ones_f", "mask64", "Sst"))
    y_rw, sb, ppc = L["y_rw"], L["sb"], L["ppc"]
    W = rwkv_head.W
    t0 = n * TT
    i64 = iden[0:64, 0:64]
    i64b = L["iden_bf"][0:64, 0:64]
    t1, t2, t3, cl = rw["t1"], rw["t2"], rw["t3"], rw["cl"]
    P.op("act", lambda e: e.activation(out=t1[:], in_=rw["mxw"][:], func=AF.Tanh), reads=["rw_mxw"], writes=["rw_t1"])
    P.op("pe", lambda e: e.matmul(misc[0:64, :], lhsT=w2h[:], rhs=t1[:], start=True, stop=True), reads=["w2h", "rw_t1"],
         writes=["misc"])
    P.op("act", lambda e: e.activation(out=t2[:], in_=misc[0:64, :], func=AF.Sigmoid, bias=ppc("w0")), reads=["misc", "pp"],
         writes=["rw_t2"])
    P.op("dve", lambda e: e.tensor_scalar(out=t2[:], in0=t2[:], scalar1=-0.6065306597126334, scalar2=None, op0=ALU.mult),
         reads=["rw_t2"], writes=["rw_t2"])
    P.op("dve", lambda e: e.tensor_tensor_scan(out=cl[:], data0=mask64[:], data1=t2[:], initial=0.0, op0=ALU.mult,
                                               op1=ALU.add), reads=["mask64", "rw_t2"], writes=["rw_cl"])
    yield
    P.op("act", lambda e: e.activation(out=t3[:], in_=cl[:], func=AF.Exp), reads=["rw_cl"], writes=["rw_t3"])
    P.op("dve", lambda e: e.tensor_tensor(out=rw["Rt"][:], in0=rw["mr"][:], in1=t3[:], op=ALU.mult), reads=["rw_mr", "rw_t3"],
         writes=["rw_Rt"])
    P.op("dve", lambda e: e.tensor_copy(out=W["PCs"][:], in_=t3[:].rearrange("p (c j) -> p c j", j=128)[:, :, 127]),
         reads=["rw_t3"], writes=["w_PCs"])
    P.op("dve", lambda e: e.tensor_tensor(out=t1[:], in0=cl[:], in1=t2[:], op=ALU.subtract), reads=["rw_cl", "rw_t2"],
         writes=["rw_t1"])
    P.op("act", lambda e: e.activation(out=t1[:], in_=t1[:], func=AF.Exp), reads=["rw_t1"], writes=["rw_t1"])
    yield
    P.op("pe", lambda e: e.matmul(misc[0:64, :], lhsT=a2h[:], rhs=rw["mxa"][:], start=True, stop=True),
         reads=["a2h", "rw_mxa"], writes=["misc"])
    P.op("act", lambda e: e.activation(out=rw["asb"][:], in_=misc[0:64, :], func=AF.Sigmoid, bias=ppc("a0")),
         reads=["misc", "pp"], writes=["rw_asb"])
    yield
    kk = rw["kk"]
    P.op("dve", lambda e: e.tensor_scalar(out=kk[:], in0=rw["mk"][:], scalar1=ppc("k_k"), scalar2=None, op0=ALU.mult),
         reads=["rw_mk", "pp"], writes=["rw_kk"])
    P.op("act", lambda e: e.activation(out=t2[:], in_=kk[:], func=AF.Square), reads=["rw_kk"], writes=["rw_t2"])
    P.op("pe", lambda e: e.matmul(misc[0:64, :], lhsT=ones_f[0:64, 0:64], rhs=t2[:], start=True, stop=True),
         reads=["ones_f", "rw_t2"], writes=["misc"])
    P.op("dve", lambda e: e.tensor_scalar(out=t2[:], in0=misc[0:64, :], scalar1=1e-24, scalar2=None, op0=ALU.max),
         reads=["misc"], writes=["rw_t2"])
    P.op("act", lambda e: e.activation(out=t2[:], in_=t2[:], func=AF.Ln), reads=["rw_t2"], writes=["rw_t2"])
    P.op("act", lambda e: e.activation(out=t2[:], in_=t2[:], func=AF.Exp, scale=-0.5), reads=["rw_t2"], writes=["rw_t2"])
    P.op("dve", lambda e: e.tensor_tensor(out=kk[:], in0=kk[:], in1=t2[:], op=ALU.mult), reads=["rw_kk", "rw_t2"],
         writes=["rw_kk"])
    yield
    P.op("dve", lambda e: e.scalar_tensor_tensor(out=rw["At"][:], in0=kk[:], scalar=-1.0, in1=t1[:], op0=ALU.mult,
                                                 op1=ALU.mult), reads=["rw_kk", "rw_t1"], writes=["rw_At"])
    P.op("act", lambda e: e.activation(out=t3[:], in_=cl[:], func=AF.Exp, scale=-1.0), reads=["rw_cl"], writes=["rw_t3"])
    P.op("dve", lambda e: e.tensor_scalar(out=t2[:], in0=rw["asb"][:], scalar1=ppc("k_a"), scalar2=der[0:64, 1:2],
                                          op0=ALU.mult, op1=ALU.add), reads=["rw_asb", "pp", "der1"], writes=["rw_t2"])
    t4 = rw["t4"]
    P.op("dve", lambda e: e.tensor_tensor(out=t4[:], in0=rw["mk"][:], in1=t2[:], op=ALU.mult), reads=["rw_mk", "rw_t2"],
         writes=["rw_t4"])
    P.op("dve", lambda e: e.scalar_tensor_tensor(out=rw["rkb"][:], in0=t4[:], scalar=ppc("r_k"), in1=rw["mr"][:],
                                                 op0=ALU.mult, op1=ALU.mult), reads=["rw_t4", "rw_mr", "pp"], writes=["rw_rkb"])
    yield
    P.op("pool", lambda e: e.tensor_tensor(out=t2[:], in0=kk[:], in1=rw["asb"][:], op=ALU.mult), reads=["rw_kk", "rw_asb"],
         writes=["rw_t2"])
    P.op("dve", lambda e: e.tensor_tensor(out=rw["Kt"][:], in0=t4[:], in1=t3[:], op=ALU.mult), reads=["rw_t4", "rw_t3"],
         writes=["rw_Kt"])
    P.op("dve", lambda e: e.tensor_tensor(out=rw["Bt"][:], in0=t2[:], in1=t3[:], op=ALU.mult), reads=["rw_t2", "rw_t3"],
         writes=["rw_Bt"])
    P.op("pool", lambda e: e.tensor_copy(out=rw["mvb"][:], in_=rw["mv"][:]), reads=["rw_mv"], writes=["rw_mvb"])
    for src, dst in (("Bt", "Bh"), ("Kt", "Kh")):
        P.op("pool", lambda e, src=src, dst=dst: e.tensor_tensor(
            out=rw[dst][:].rearrange("p (c j) -> p c j", j=128), in0=rw[src][:].rearrange("p (c j) -> p c j", j=128),
            in1=W["PCs"][:].unsqueeze(2).to_broadcast([64, 4, 128]), op=ALU.mult),
            reads=["rw_" + src, "w_PCs"], writes=["rw_" + dst])
    P.op("act", lambda e: e.activation(out=sgx[:], in_=rw["mxg"][:], func=AF.Sigmoid), reads=["rw_mxg"], writes=["sgx"])
    yield

    NSL = 4

    def mm(rows, cols, lhsT, rhs, reads, start=True, stop=True, slot=None):
        if slot is None:
            slot = W["slot"] % NSL
            W["slot"] += 1
        P.op("pe", lambda e: e.matmul(wps[0:rows, slot, 0:cols], lhsT=lhsT, rhs=rhs, start=start, stop=stop), reads=reads,
             writes=["wps_%d" % slot])
        return slot

    def ev_copy(dst, slot, rows, cols, eng="act"):
        if eng == "act":
            P.op("act", lambda e: e.activation(out=W[dst][:], in_=wps[0:rows, slot, 0:cols], func=AF.Copy),
                 reads=["wps_%d" % slot], writes=["w_" + dst])
        else:
            P.op("dve", lambda e: e.tensor_copy(out=W[dst][:], in_=wps[0:rows, slot, 0:cols]), reads=["wps_%d" % slot],
                 writes=["w_" + dst])

    def ev_mask(dst, slot, mi):
        P.op("dve", lambda e: e.tensor_tensor(out=W[dst][:], in0=wps[:, slot, :], in1=msk[:, mi, :], op=ALU.mult),
             reads=["wps_%d" % slot, "msk"], writes=["w_" + dst])

    CH = 128
    for c in range(TT // CH):
        cs = slice(c * CH, (c + 1) * CH)
        At, Bt, Kt, Rt, Bh, Kh = (rw[k][:, cs] for k in ("At", "Bt", "Kt", "Rt", "Bh", "Kh"))
        for src, srck, dst in ((rw["mvb"][:, cs], "rw_mvb", "Vtm"), (At, "rw_At", "Atm"), (Bh, "rw_Bh", "Bhtm"), (Kh, "rw_Kh", "Khtm")):
            sl = mm(CH, 64, src, i64b, [srck, "iden_bf"])
            ev_copy(dst, sl, CH, 64)
            yield
        sl = mm(CH, CH, Bt, At, ["rw_Bt", "rw_At"]); ev_mask("Na", sl, 0)
        yield
        sl = mm(CH, CH, At, Bt, ["rw_Bt", "rw_At"]); ev_mask("La", sl, 1)
        P.op("dve", lambda e: e.tensor_tensor(out=W["Ta"][:], in0=W["Na"][:], in1=L["iden_bf"][:], op=ALU.add), reads=["w_Na", "iden_bf"],
             writes=["w_Ta"])
        yield
        Nc, Lc, Tc = "Na", "La", "Ta"
        other = {"Na": "Nb", "Nb": "Na", "La": "Lb", "Lb": "La", "Ta": "Tb", "Tb": "Ta"}
        for k in range(6):
            Nn, Ln, Tn = other[Nc], other[Lc], other[Tc]
            if k < 5:
                sl = mm(CH, CH, W[Lc][:], W[Nc][:], ["w_" + Lc, "w_" + Nc]); ev_copy(Nn, sl, CH, CH, "act")
                yield
            sl = mm(CH, CH, W[Nc][:], W[Lc][:], ["w_" + Lc, "w_" + Nc]); ev_copy(Ln, sl, CH, CH, "act")
            yield
            sl = mm(CH, CH, W[Ln][:], W[Tc][:], ["w_" + Ln, "w_" + Tc])
            P.op("dve", lambda e, sl=sl, Tc=Tc, Tn=Tn: e.tensor_tensor(out=W[Tn][:], in0=wps[:, sl, :], in1=W[Tc][:], op=ALU.add),
                 reads=["wps_%d" % sl, "w_" + Tc], writes=["w_" + Tn])
            yield
            Nc, Lc, Tc = Nn, Ln, Tn
        TT_ = W[Tc]
        tk = "w_" + Tc
        sl = mm(CH, CH, Kt, At, ["rw_Kt", "rw_At"]); ev_mask("AkT", sl, 0)
        yield
        sl = mm(CH, 64, W["AkT"][:], W["Vtm"][:], ["w_AkT", "w_Vtm"]); ev_copy("X2", sl, CH, 64)
        yield
        sl = mm(CH, 64, TT_[:], W["X2"][:], [tk, "w_X2"]); ev_copy("M2", sl, CH, 64)
        yield
        sl = mm(CH, 64, TT_[:], W["Atm"][:], [tk, "w_Atm"]); ev_copy("M1", sl, CH, 64, "dve")
        yield
        sl = mm(64, 64, W["M1"][:], W["Bhtm"][:], ["w_M1", "w_Bhtm"])
        P.op("dve", lambda e, sl=sl, c=c: e.scalar_tensor_tensor(out=W["GT"][:], in0=i64, scalar=W["PCs"][:, c:c + 1],
                                                                 in1=wps[0:64, sl, 0:64], op0=ALU.mult, op1=ALU.add),
             reads=["wps_%d" % sl, "iden", "w_PCs"], writes=["w_GT"])
        yield
        sl = mm(64, 64, W["Bhtm"][:], W["M2"][:], ["w_Bhtm", "w_M2"], start=True, stop=False)
        mm(64, 64, W["Khtm"][:], W["Vtm"][:], ["w_Khtm", "w_Vtm"], start=False, stop=True, slot=sl)
        ev_copy("H", sl, 64, 64)
        yield
        sl = mm(64, 64, W["GT"][:], Sst[:, c, :], ["w_GT", "S%d" % c])
        P.op("dve", lambda e, sl=sl, c=c: e.tensor_tensor(out=Sst[:, c + 1, :], in0=wps[0:64, sl, 0:64], in1=W["H"][:], op=ALU.add),
             reads=["wps_%d" % sl, "w_H"], writes=["S%d" % (c + 1)])
        yield
        sl = mm(CH, CH, Bt, Rt, ["rw_Bt", "rw_Rt"]); ev_mask("ArbT", sl, 2)
        yield
        sl = mm(CH, CH, Kt, Rt, ["rw_Kt", "rw_Rt"]); ev_mask("ArkT", sl, 2)
        yield
        sl = mm(64, CH, W["M1"][:], W["ArbT"][:], ["w_M1", "w_ArbT"])
        P.op("dve", lambda e, sl=sl, Rt=Rt: e.tensor_tensor(out=W["QtT"][:], in0=wps[0:64, sl, :], in1=Rt, op=ALU.add),
             reads=["wps_%d" % sl, "rw_Rt"], writes=["w_QtT"])
        yield
        sl = mm(CH, 64, W["QtT"][:], Sst[:, c, :], ["w_QtT", "S%d" % c], start=True, stop=False)
        mm(CH, 64, W["ArbT"][:], W["M2"][:], ["w_ArbT", "w_M2"], start=False, stop=False, slot=sl)
        mm(CH, 64, W["ArkT"][:], W["Vtm"][:], ["w_ArkT", "w_Vtm"], start=False, stop=True, slot=sl)
        st = W["st"]
        P.op("act", lambda e, sl=sl: e.activation(out=W["y"][:], in_=wps[:, sl, 0:64], func=AF.Copy, accum_out=st[:, 0:1]),
             reads=["wps_%d" % sl], writes=["w_y", "w_st0"])
        P.op("dve", lambda e: e.tensor_scalar(out=st[:, 1:2], in0=st[:, 0:1], scalar1=1.0 / 64, scalar2=None, op0=ALU.mult),
             reads=["w_st0"], writes=["w_st1"])
        P.op("dve", lambda e: e.tensor_scalar(out=W["yc"][:], in0=W["y"][:], scalar1=st[:, 1:2], scalar2=None, op0=ALU.subtract),
             reads=["w_y", "w_st1"], writes=["w_yc"])
        yield
        P.op("act", lambda e: e.activation(out=W["junk"][:], in_=W["yc"][:], func=AF.Square, accum_out=st[:, 2:3]),
             reads=["w_yc"], writes=["w_junk", "w_st2"])
        P.op("act", lambda e: e.activation(out=st[:, 3:4], in_=st[:, 2:3], func=AF.Ln, scale=1.0 / 64, bias=64e-5),
             reads=["w_st2"], writes=["w_st3"])
        P.op("act", lambda e: e.activation(out=st[:, 3:4], in_=st[:, 3:4], func=AF.Exp, scale=-0.5), reads=["w_st3"],
             writes=["w_st3"])
        P.op("dve", lambda e: e.scalar_tensor_tensor(out=W["o1"][:], in0=W["yc"][:], scalar=st[:, 3:4], in1=bc[:, 64:128],
                                                     op0=ALU.mult, op1=ALU.mult), reads=["w_yc", "w_st3", "bc"], writes=["w_o1"])
        yield
        sl = mm(CH, 1, rw["rkb"][:, cs], ones_f[0:64, 0:1], ["rw_rkb", "ones_f"])
        P.op("act", lambda e, sl=sl: e.activation(out=st[:, 4:5], in_=wps[:, sl, 0:1], func=AF.Copy), reads=["wps_%d" % sl],
             writes=["w_st4"])
        P.op("dve", lambda e: e.scalar_tensor_tensor(out=W["o2"][:], in0=W["Vtm"][:], scalar=st[:, 4:5], in1=W["o1"][:],
                                                     op0=ALU.mult, op1=ALU.add), reads=["w_Vtm", "w_st4", "w_o1"], writes=["w_o2"])
        yield
        sl = mm(CH, 64, sgx[:, cs], L["g2h_bf"][:], ["sgx", "g2h_bf"])
        P.op("dve", lambda e, sl=sl: e.tensor_tensor(out=W["yo"][:], in0=wps[:, sl, 0:64], in1=W["o2"][:], op=ALU.mult),
             reads=["wps_%d" % sl, "w_o2"], writes=["w_yo"])
        P.dma("sp", y_rw[t0 + c * CH:t0 + (c + 1) * CH, :], W["yo"][:], reads=["w_yo"])
        yield
    P.op("dve", lambda e: e.tensor_copy(out=Sst[:, 0, :], in_=Sst[:, 4, :]), reads=["S4"], writes=["S0"])


DFF = 2816
NFC = 22


def build_C(TC):
    nc = bass.Bass("TRN2", target_bir_lowering=False)
    din = lambda n, shp, dt=F32: nc.dram_tensor(n, shp, dt, kind="ExternalInput").ap()
    A = dict(xsrc=din("xT", [D, TC]), ytm=din("ytm", [TC, 768]), wg=din("wg", [D, 3072]), pw=din("pw", [768, D]),
             wo=din("wo", [D, D]), wgu=din("wgu", [D, 2 * DFF]), wd=din("wd", [DFF, D]), g1=din("g1", [128, 8]),
             g2=din("g2", [128, 8]), iden=din("iden", [128, 128]))
    A["xdst"] = nc.dram_tensor("outT", [D, TC], F32, kind="ExternalOutput").ap()
    P = Prog(nc)
    with ExitStack() as es:
        emit_C(P, nc, es, TC, A, "")
    P.emit()
    return nc


def emit_C(P, nc, es, TC, A, pfx):
    NS = TC // 512
    xT, ytm, wg, pw, wo, wgu, wd, g1d, g2d, outT = (A[k] for k in ("xsrc", "ytm", "wg", "pw", "wo", "wgu", "wd", "g1", "g2", "xdst"))
    if True:
        def sb(name, shape, dt=F32):
            return es.enter_context(nc.sbuf_tensor("sb_" + pfx + name, shape, dt))

        def ps(name, shape, dt=F32):
            return es.enter_context(nc.psum_tensor("ps_" + pfx + name, shape, dt))
        g1 = sb("g1", [128, 8]); g2 = sb("g2", [128, 8])
        P.dma("sp", g1[:], g1d, writes=["g1"]); P.dma("sp", g2[:], g2d, writes=["g2"])
        ones_bf = sb("ones_bf", [128, 128], BF16)
        P.op("pool", lambda e: e.memset(ones_bf[:], 1.0), writes=["ones_bf"])
        xb = sb("xb", [128, 8, TC], BF16)
        big = sb("big", [128, NFC, TC], BF16)
        mb = big[:, 0:8, :]; ybf = big[:, 8:14, :]; hb = big
        rstd = [sb("rstd%d" % i, [128, TC]) for i in range(2)]
        macc = sb("macc", [128, NS, 512])
        stg = [sb("stg%d" % i, [128, 3072]) for i in range(2)]
        wcb = [sb("wcb%d" % i, [128, 3072], BF16) for i in range(2)]
        iden_f = sb("iden_f", [128, 128]); iden_bf = sb("iden_bf", [128, 128], BF16)
        P.dma("sp", iden_f[:], A["iden"], writes=["iden_f"])
        P.op("dve", lambda e: e.tensor_copy(out=iden_bf[:], in_=iden_f[:]), reads=["iden_f"], writes=["iden_bf"])
        xt = [sb("xt%d" % i, [128, 512]) for i in range(2)]
        sqb = [sb("sqb%d" % i, [128, 512], BF16) for i in range(2)]
        tA = [sb("tA%d" % i, [128, 512]) for i in range(2)]
        tB = [sb("tB%d" % i, [128, 512]) for i in range(2)]
        tC_ = [sb("tC%d" % i, [128, 512]) for i in range(2)]
        pbank = [ps("pbank%d" % i, [128, 512]) for i in range(4)]
        ssb = [ps("ssb%d" % i, [128, 512]) for i in range(NS)]
        cnt = {"pb": 0, "st": 0, "xt": 0, "t": 0}

        def nb():
            i = cnt["pb"] % 4
            cnt["pb"] += 1
            return pbank[i], "pbank%d" % i

        def wload(src_ap, shape3, gain=None, eng="dve"):
            i = cnt["st"] % 2
            cnt["st"] += 1
            a, b = shape3
            sv = stg[i][:, 0:a * b].rearrange("p (a b) -> p a b", b=b)
            wv = wcb[i][:, 0:a * b].rearrange("p (a b) -> p a b", b=b)
            P.dma("sp" if i == 0 else "pool", sv, src_ap, writes=["stg%d" % i])
            if gain is not None:
                P.op("dve", lambda e: e.tensor_tensor(out=wv, in0=sv, in1=gain[:].unsqueeze(2).to_broadcast([128, a, b]),
                                                      op=ALU.mult), reads=["stg%d" % i, "g1", "g2"], writes=["wcb%d" % i])
            elif eng == "act":
                P.op("act", lambda e: e.activation(out=wv, in_=sv, func=AF.Copy), reads=["stg%d" % i], writes=["wcb%d" % i])
            else:
                P.op("pool", lambda e: e.tensor_copy(out=wv, in_=sv), reads=["stg%d" % i], writes=["wcb%d" % i])
            return wv, "wcb%d" % i

        def rstd_from(ssk, s, dst):
            sl = slice(s * 512, (s + 1) * 512)
            P.op("act", lambda e: e.activation(out=dst[:, sl], in_=ssb[s][:], func=AF.Ln, scale=1.0 / D, bias=1e-6),
                 reads=["ssb%d" % s], writes=[ssk + "_%d" % s])
            P.op("act", lambda e: e.activation(out=dst[:, sl], in_=dst[:, sl], func=AF.Exp, scale=-0.5),
                 reads=[ssk + "_%d" % s], writes=[ssk + "_%d" % s])

        for s in range(NS):
            sl = slice(s * 512, (s + 1) * 512)
            for c in range(8):
                i = cnt["xt"] % 2
                cnt["xt"] += 1
                P.dma("sp", xt[i][:], xT[c * 128:(c + 1) * 128, sl], writes=["xt%d" % i])
                P.op("act", lambda e, i=i, c=c, sl=sl: e.activation(out=xb[:, c, sl], in_=xt[i][:], func=AF.Copy),
                     reads=["xt%d" % i], writes=["xb_%d_%d" % (c, s)])
                P.op("pool", lambda e, i=i: e.tensor_tensor(out=sqb[i][:], in0=xt[i][:], in1=xt[i][:], op=ALU.mult),
                     reads=["xt%d" % i], writes=["sqb%d" % i])
                P.op("pe", lambda e, i=i, c=c, s=s: e.matmul(ssb[s][:], lhsT=ones_bf[:], rhs=sqb[i][:], start=(c == 0),
                                                             stop=(c == 7)), reads=["ones_bf", "sqb%d" % i], writes=["ssb%d" % s])
            rstd_from("rstd0", s, rstd[0])
        for s in range(NS):
            i = cnt["st"] % 2
            cnt["st"] += 1
            yv = stg[i][:, 0:3072].rearrange("p (j f) -> p j f", f=768)
            yb = wcb[i][:, 0:3072].rearrange("p (j f) -> p j f", f=768)
            P.dma("sp", yv, ytm[s * 512:(s + 1) * 512, :].rearrange("(j p) f -> p j f", p=128), writes=["stg%d" % i])
            P.op("pool", lambda e, yv=yv, yb=yb: e.tensor_copy(out=yb, in_=yv), reads=["stg%d" % i], writes=["wcb%d" % i])
            for k in range(6):
                bank, bk = nb()
                for j in range(4):
                    P.op("pe", lambda e, bank=bank, j=j, k=k, yb=yb: e.matmul(bank[:, j * 128:(j + 1) * 128],
                                                                              lhsT=yb[:, j, k * 128:(k + 1) * 128], rhs=iden_bf[:],
                                                                              start=True, stop=True),
                         reads=["wcb%d" % i, "iden_bf"], writes=[bk])
                P.op("act", lambda e, bank=bank, k=k, s=s: e.activation(out=ybf[:, k, s * 512:(s + 1) * 512], in_=bank[:], func=AF.Copy),
                     reads=[bk], writes=["ybf%d_%d" % (k, s)])
        for dc in range(8):
            for br in range(3):
                col0 = br * 1024 + dc * 128
                wgb, wgk = wload(wg[:, col0:col0 + 128].rearrange("(c p) n -> p c n", p=128), (8, 128), gain=g1)
                pb, pk = wload(pw[br * 256:(br + 1) * 256, dc * 128:(dc + 1) * 128].rearrange("(c p) n -> p c n", p=128),
                               (2, 128), eng="act")
                for s in range(NS):
                    sl = slice(s * 512, (s + 1) * 512)
                    A, Ak = nb()
                    for c in range(8):
                        P.op("pe", lambda e, A=A, c=c, sl=sl, wgb=wgb: e.matmul(A[:], lhsT=wgb[:, c, :], rhs=xb[:, c, sl],
                                                                                 start=(c == 0), stop=(c == 7)),
                             reads=[wgk, "xb_%d_%d" % (c, s)], writes=[Ak])
                    B, Bk = nb()
                    for k in range(2):
                        P.op("pe", lambda e, B=B, k=k, sl=sl, pb=pb, br=br: e.matmul(B[:], lhsT=pb[:, k, :], rhs=ybf[:, br * 2 + k, sl],
                                                                                      start=(k == 0), stop=(k == 1)),
                             reads=[pk, "ybf%d_%d" % (br * 2 + k, s)], writes=[Bk])
                    i = cnt["t"] % 2
                    cnt["t"] += 1
                    P.op("dve", lambda e, A=A, i=i, sl=sl: e.tensor_tensor(out=tA[i][:], in0=A[:], in1=rstd[0][:, sl], op=ALU.mult),
                         reads=[Ak, "rstd0_%d" % s], writes=["tA%d" % i])
                    P.op("act", lambda e, i=i: e.activation(out=tA[i][:], in_=tA[i][:], func=AF.Sigmoid), reads=["tA%d" % i],
                         writes=["tA%d" % i])
                    if br == 0:
                        P.op("dve", lambda e, B=B, i=i, s=s: e.tensor_tensor(out=macc[:, s, :], in0=B[:], in1=tA[i][:], op=ALU.mult),
                             reads=[Bk, "tA%d" % i], writes=["macc%d" % s])
                    else:
                        P.op("dve", lambda e, B=B, i=i: e.tensor_tensor(out=tB[i][:], in0=B[:], in1=tA[i][:], op=ALU.mult),
                             reads=[Bk, "tA%d" % i], writes=["tB%d" % i])
                        if br == 1:
                            P.op("pool", lambda e, i=i, s=s: e.tensor_tensor(out=macc[:, s, :], in0=macc[:, s, :], in1=tB[i][:], op=ALU.add),
                                 reads=["tB%d" % i, "macc%d" % s], writes=["macc%d" % s])
                        else:
                            P.op("pool", lambda e, i=i, s=s, dc=dc, sl=sl: e.tensor_tensor(out=mb[:, dc, sl], in0=macc[:, s, :], in1=tB[i][:],
                                                                                           op=ALU.add),
                                 reads=["tB%d" % i, "macc%d" % s], writes=["mb_%d_%d" % (dc, s)])
        for dc in range(8):
            wob, wok = wload(wo[:, dc * 128:(dc + 1) * 128].rearrange("(c p) n -> p c n", p=128), (8, 128), eng="act")
            for s in range(NS):
                sl = slice(s * 512, (s + 1) * 512)
                A, Ak = nb()
                for c in range(8):
                    P.op("pe", lambda e, A=A, c=c, sl=sl, wob=wob: e.matmul(A[:], lhsT=wob[:, c, :], rhs=mb[:, c, sl], start=(c == 0),
                                                                             stop=(c == 7)), reads=[wok, "mb_%d_%d" % (c, s)], writes=[Ak])
                i = cnt["xt"] % 2
                cnt["xt"] += 1
                P.dma("sp", xt[i][:], xT[dc * 128:(dc + 1) * 128, sl], writes=["xt%d" % i])
                j = cnt["t"] % 2
                cnt["t"] += 1
                P.op("dve", lambda e, A=A, i=i, j=j: e.tensor_tensor(out=tC_[j][:], in0=A[:], in1=xt[i][:], op=ALU.add),
                     reads=[Ak, "xt%d" % i], writes=["tC%d" % j])
                P.dma("pool", outT[dc * 128:(dc + 1) * 128, sl], tC_[j][:], reads=["tC%d" % j], writes=["o_%d_%d" % (dc, s)])
                P.op("act", lambda e, j=j, dc=dc, sl=sl: e.activation(out=xb[:, dc, sl], in_=tC_[j][:], func=AF.Copy),
                     reads=["tC%d" % j], writes=["xb_%d_%d" % (dc, s)])
                P.op("pool", lambda e, j=j: e.tensor_tensor(out=sqb[j][:], in0=tC_[j][:], in1=tC_[j][:], op=ALU.mult),
                     reads=["tC%d" % j], writes=["sqb%d" % j])
                P.op("pe", lambda e, j=j, s=s, dc=dc: e.matmul(ssb[s][:], lhsT=ones_bf[:], rhs=sqb[j][:], start=(dc == 0),
                                                               stop=(dc == 7)), reads=["ones_bf", "sqb%d" % j], writes=["ssb%d" % s])
        for s in range(NS):
            rstd_from("rstd1", s, rstd[1])
        allmb = ["mb_%d_%d" % (dc, s) for dc in range(8) for s in range(NS)] + ["ybf%d_%d" % (k, s) for k in range(6) for s in range(NS)]
        for fc in range(NFC):
            i0 = cnt["st"] % 2
            cnt["st"] += 1
            sv = stg[i0][:, 0:2048].rearrange("p (a b) -> p a b", b=256)
            wv = wcb[i0][:, 0:2048].rearrange("p (a b) -> p a b", b=256)
            P.dma("sp", sv[:, :, 0:128], wgu[:, fc * 128:(fc + 1) * 128].rearrange("(c p) n -> p c n", p=128), writes=["stg%d" % i0])
            P.dma("pool", sv[:, :, 128:256], wgu[:, DFF + fc * 128:DFF + (fc + 1) * 128].rearrange("(c p) n -> p c n", p=128),
                  reads=["stg%d" % i0], writes=["stgx%d_1" % i0])
            P.op("dve", lambda e, sv=sv, wv=wv: e.tensor_tensor(out=wv, in0=sv, in1=g2[:].unsqueeze(2).to_broadcast([128, 8, 256]),
                                                                op=ALU.mult),
                 reads=["stg%d" % i0, "stgx%d_1" % i0, "g2"], writes=["wcb%d" % i0, "stg%d" % i0])
            wk = "wcb%d" % i0
            for s in range(NS):
                sl = slice(s * 512, (s + 1) * 512)
                A, Ak = nb()
                for c in range(8):
                    P.op("pe", lambda e, A=A, c=c, sl=sl, wv=wv: e.matmul(A[:], lhsT=wv[:, c, 0:128], rhs=xb[:, c, sl], start=(c == 0),
                                                                           stop=(c == 7)), reads=[wk, "xb_%d_%d" % (c, s)], writes=[Ak])
                B, Bk = nb()
                for c in range(8):
                    P.op("pe", lambda e, B=B, c=c, sl=sl, wv=wv: e.matmul(B[:], lhsT=wv[:, c, 128:256], rhs=xb[:, c, sl], start=(c == 0),
                                                                           stop=(c == 7)), reads=[wk, "xb_%d_%d" % (c, s)], writes=[Bk])
                i = cnt["t"] % 2
                cnt["t"] += 1
                P.op("dve", lambda e, A=A, i=i, sl=sl: e.tensor_tensor(out=tA[i][:], in0=A[:], in1=rstd[1][:, sl], op=ALU.mult),
                     reads=[Ak, "rstd1_%d" % s], writes=["tA%d" % i])
                P.op("act", lambda e, i=i: e.activation(out=tA[i][:], in_=tA[i][:], func=AF.Silu), reads=["tA%d" % i], writes=["tA%d" % i])
                P.op("dve", lambda e, B=B, i=i, sl=sl: e.tensor_tensor(out=tB[i][:], in0=B[:], in1=rstd[1][:, sl], op=ALU.mult),
                     reads=[Bk, "rstd1_%d" % s], writes=["tB%d" % i])
                P.op("pool", lambda e, i=i, fc=fc, sl=sl: e.tensor_tensor(out=hb[:, fc, sl], in0=tA[i][:], in1=tB[i][:], op=ALU.mult),
                     reads=["tA%d" % i, "tB%d" % i], writes=["hb_%d_%d" % (fc, s)] + (allmb if fc < 14 else []))
        for dc in range(8):
            wdb, wdk = wload(wd[:, dc * 128:(dc + 1) * 128].rearrange("(c p) n -> p c n", p=128), (NFC, 128), eng="act")
            for s in range(NS):
                sl = slice(s * 512, (s + 1) * 512)
                A, Ak = nb()
                for fc in range(NFC):
                    P.op("pe", lambda e, A=A, fc=fc, sl=sl, wdb=wdb: e.matmul(A[:], lhsT=wdb[:, fc, :], rhs=hb[:, fc, sl], start=(fc == 0),
                                                                               stop=(fc == NFC - 1)),
                         reads=[wdk, "hb_%d_%d" % (fc, s)], writes=[Ak])
                i = cnt["xt"] % 2
                cnt["xt"] += 1
                P.dma("sp", xt[i][:], outT[dc * 128:(dc + 1) * 128, sl], reads=["o_%d_%d" % (dc, s)], writes=["xt%d" % i])
                j = cnt["t"] % 2
                cnt["t"] += 1
                P.op("dve", lambda e, A=A, i=i, j=j: e.tensor_tensor(out=tC_[j][:], in0=A[:], in1=xt[i][:], op=ALU.add),
                     reads=[Ak, "xt%d" % i], writes=["tC%d" % j])
                P.dma("pool", outT[dc * 128:(dc + 1) * 128, sl], tC_[j][:], reads=["tC%d" % j], writes=["o_%d_%d" % (dc, s)])


RET0, FOX0, RW0, GATE0 = 0, 1024, 1796, 2820

def consts_B(h, S):
    half = 32
    inv_freq = (np.float32(10000.0) ** (-np.arange(half, dtype=np.float32) / np.float32(half))).astype(np.float32)
    pos = np.arange(S, dtype=np.float32)
    ang = (pos[None, :] * inv_freq[:, None]).astype(np.float32)
    cos = np.cos(ang).astype(np.float32); sin = np.sin(ang).astype(np.float32)
    rope = np.zeros((64, 2, S), np.float32)
    rope[0:32, 0] = cos; rope[32:64, 0] = cos
    rope[0:32, 1] = -sin; rope[32:64, 1] = sin
    lg = np.log1p(-np.exp2(np.float32(-5.0 - h))).astype(np.float32)
    idx = np.arange(128, dtype=np.float32)
    dist = idx[None, :] - idx[:, None]
    dmatT = np.where(dist >= 0, np.exp(lg * np.maximum(dist, 0)), 0).astype(np.float32)
    kdec = np.exp(lg * (127 - idx)).astype(np.float32)
    qdec = np.exp(lg * (idx + 1)).astype(np.float32)
    cdec = np.exp(lg * 128).astype(np.float32)
    ii = np.arange(128)
    masks = np.zeros((128, 3, 128), np.float32)
    masks[:, 0, :] = (ii[None, :] > ii[:, None])
    masks[:, 1, :] = (ii[None, :] < ii[:, None])
    masks[:, 2, :] = (ii[None, :] >= ii[:, None])
    iden = np.eye(128, dtype=np.float32)
    return dict(rope=rope, dmat=dmatT, masks=masks, iden=iden), kdec, qdec, cdec

def prep_B(inp, l, b, h, S, x=None):
    cst, kdec, qdec, cdec = consts_B(h, S)
    w_in = inp["w_in"][l]
    hs = slice(h * 64, (h + 1) * 64)
    def col(base, blk):
        return w_in[:, base + blk * 256 + h * 64: base + blk * 256 + (h + 1) * 64]
    def swp(a):
        return np.concatenate([a[:, 32:64], a[:, 0:32]], axis=1)
    rq, rk, rv, rg = (col(RET0, i) for i in range(4))
    fq, fk, fv = (col(FOX0, i) for i in range(3))
    ff = w_in[:, FOX0 + 768 + h: FOX0 + 768 + h + 1]
    wr, wk, wv = (col(RW0, i) for i in range(3))
    wxw = w_in[:, RW0 + 768: RW0 + 832]; wxa = w_in[:, RW0 + 832: RW0 + 896]; wxg = w_in[:, RW0 + 896: RW0 + 1024]
    wf = np.concatenate([rq, rk, swp(rq), swp(rk), fq, fk, ff, wr, wk, wv, wxw, wxa, wxg], axis=1)
    wt = np.concatenate([rv, fv, rg], axis=1)
    assert wf.shape[1] == NF
    mu = inp["rwkv_mu"][l]
    pp = np.zeros((128, NPP), np.float32)
    def setc(n, v):
        v = np.asarray(v, np.float32).reshape(-1)
        if v.size == 1:
            pp[:, PP[n]] = v[0]
        else:
            pp[:v.size, PP[n]] = v
    setc("fqg", inp["fox_q_norm"][l]); setc("fkg", inp["fox_k_norm"][l]); setc("fb", inp["fox_f_bias"][l][h])
    setc("mu_r", mu[0 * 256 + h * 64: 0 * 256 + (h + 1) * 64]); setc("mu_k", mu[256 + h * 64: 256 + (h + 1) * 64])
    setc("mu_v", mu[512 + h * 64: 512 + (h + 1) * 64]); setc("mu_xw", mu[768:832]); setc("mu_xa", mu[832:896])
    setc("mu_xg", mu[896:1024])
    for n, k in (("w0", "rwkv_w0"), ("a0", "rwkv_a0"), ("k_k", "rwkv_k_k"), ("k_a", "rwkv_k_a"), ("r_k", "rwkv_r_k")):
        setc(n, inp[k][l][hs])
    setc("cdec", cdec); setc("kdec", kdec)
    bc = np.zeros((128, NBC), np.float32)
    bc[:, 0:64] = inp["ret_gn"][l][hs][None, :]
    bc[:, 64:128] = inp["rwkv_gn"][l][hs][None, :]
    bc[:, 128:256] = qdec[None, :]
    xx = inp["x"][b] if x is None else x
    m = dict(xT=np.ascontiguousarray(xx[:S].T), wf=np.ascontiguousarray(wf), wt=np.ascontiguousarray(wt),
             gmix=np.ascontiguousarray(inp["mix_norm"][l].reshape(8, 128).T), pp=pp, bc=bc,
             w2h=np.ascontiguousarray(inp["rwkv_w2"][l][:, hs]), a2h=np.ascontiguousarray(inp["rwkv_a2"][l][:, hs]),
             g2h=np.ascontiguousarray(inp["rwkv_g2"][l][:, hs]))
    m.update(cst)
    return m

def prep_C(inp, l, xs, ys):
    return dict(xT=np.ascontiguousarray(xs.T), ytm=np.ascontiguousarray(ys), iden=np.eye(128, dtype=np.float32), wg=np.ascontiguousarray(inp["w_in"][l][:, 2820:]),
                pw=np.ascontiguousarray(np.concatenate([inp["p_ret"][l], inp["p_fox"][l], inp["p_rwkv"][l]], axis=0)),
                wo=np.ascontiguousarray(inp["w_out"][l]), wgu=np.ascontiguousarray(inp["w_gate_up"][l]),
                wd=np.ascontiguousarray(inp["w_down"][l]),
                g1=np.ascontiguousarray(inp["mix_norm"][l].reshape(8, 128).T),
                g2=np.ascontiguousarray(inp["ffn_norm"][l].reshape(8, 128).T))


_NC = {}


def kernel(**inputs):
    inp = {k: np.asarray(v, dtype=np.float32) for k, v in inputs.items()}
    x = inp["x"]
    Bn, S, Dm = x.shape
    if "B" not in _NC:
        _NC["B"] = build_B(S)
        _NC["C"] = build_C(Bn * S // 8)
    TC = Bn * S // 8
    for l in range(2):
        maps = [prep_B(inp, l, c // 4, c % 4, S, x=x[c // 4]) for c in range(8)]
        res = run_bass_kernel_spmd(_NC["B"], maps, core_ids=list(range(8))).results
        y = np.zeros((Bn, S, 768), np.float32)
        for c in range(8):
            b, h = c // 4, c % 4
            y[b, :, h * 64:(h + 1) * 64] = res[c]["y_ret"]
            y[b, :, 256 + h * 64:256 + (h + 1) * 64] = res[c]["y_fox"]
            y[b, :, 512 + h * 64:512 + (h + 1) * 64] = res[c]["y_rw"]
        xf = x.reshape(Bn * S, Dm)
        yf = y.reshape(Bn * S, 768)
        maps = [prep_C(inp, l, xf[c * TC:(c + 1) * TC], yf[c * TC:(c + 1) * TC]) for c in range(8)]
        res = run_bass_kernel_spmd(_NC["C"], maps, core_ids=list(range(8))).results
        x = np.concatenate([np.ascontiguousarray(res[c]["outT"].T) for c in range(8)], axis=0).reshape(Bn, S, Dm)
    return np.ascontiguousarray(x, dtype=np.float32)
```
